# Optimizing a Trainium2 kernel written in Bass

```python
import math
import jax, jax.numpy as jnp
from jax import lax
import numpy as np

D_MODEL = 1024
BATCH = 2
SEQ = 8192
DEPTH = 4

CHUNK = 64
Q_BLOCK = 128
MIX_WIDTH = D_MODEL
SSM_WIDTH = MIX_WIDTH // 2
SSM_GROUP_CH = 16
SSM_GROUPS = SSM_WIDTH // SSM_GROUP_CH
SSM_STATE = 64
ATTN_WIDTH = MIX_WIDTH - SSM_WIDTH
DIFF_HEADS = 4
DIFF_V_DIM = ATTN_WIDTH // DIFF_HEADS
DIFF_QK_DIM = DIFF_V_DIM // 2
QK_WIDTH = DIFF_HEADS * 2 * DIFF_QK_DIM
ROT_DIM = DIFF_QK_DIM // 4
ROT_HALF = ROT_DIM // 2
ROPE_THETA = 500000.0
IN_WIDTH = SSM_WIDTH + 2 * QK_WIDTH + ATTN_WIDTH
FFN_DIM = 2816
CONV_WIDTH = 3
EPS = 1e-6

kernel_name = "hybrid_s5_diffattn_convffn_trunk"


def _rmsnorm(x, g):
    xf = x.astype(jnp.float32)
    y = xf * lax.rsqrt(jnp.mean(xf * xf, axis=-1, keepdims=True) + EPS)
    return y.astype(x.dtype) * g


def _rotary(t, cos, sin):
    cos = cos.astype(t.dtype)
    sin = sin.astype(t.dtype)
    t1 = t[..., :ROT_HALF]
    t2 = t[..., ROT_HALF:ROT_DIM]
    return jnp.concatenate([t1 * cos - t2 * sin, t2 * cos + t1 * sin, t[..., ROT_DIM:]], axis=-1)


def _complex_affine_combine(e1, e2):
    a1r, a1i, b1r, b1i = e1
    a2r, a2i, b2r, b2i = e2
    ar = a2r * a1r - a2i * a1i
    ai = a2r * a1i + a2i * a1r
    br = a2r * b1r - a2i * b1i + b2r
    bi = a2r * b1i + a2i * b1r + b2i
    return (ar, ai, br, bi)


def _s5_group(u, lam_re, lam_im, log_step, b_re, b_im, c_re, c_im, d, w_glu, g_norm):
    bsz, seqlen, _ = u.shape
    f32 = jnp.float32
    uf = u.astype(f32).reshape(bsz, seqlen, SSM_GROUPS, SSM_GROUP_CH)
    lr = lam_re.astype(f32)
    li = lam_im.astype(f32)
    step = jnp.exp(log_step.astype(f32))[:, None]
    mag = jnp.exp(step * lr)
    ar = mag * jnp.cos(step * li)
    ai = mag * jnp.sin(step * li)
    den = lr * lr + li * li
    fr = ((ar - 1.0) * lr + ai * li) / den
    fi = (ai * lr - (ar - 1.0) * li) / den
    bur = jnp.einsum('blgc,gpc->blgp', uf, b_re.astype(f32))
    bui = jnp.einsum('blgc,gpc->blgp', uf, b_im.astype(f32))
    xr0 = fr * bur - fi * bui
    xi0 = fr * bui + fi * bur
    a_r = jnp.broadcast_to(ar, xr0.shape)
    a_i = jnp.broadcast_to(ai, xr0.shape)
    _, _, sr, si = lax.associative_scan(_complex_affine_combine, (a_r, a_i, xr0, xi0), axis=1)
    y = (jnp.einsum('blgp,gcp->blgc', sr, c_re.astype(f32))
         - jnp.einsum('blgp,gcp->blgc', si, c_im.astype(f32))
         + d.astype(f32) * uf)
    y = jax.nn.gelu(y.reshape(bsz, seqlen, SSM_WIDTH)).astype(u.dtype)
    ab = y @ w_glu
    out = ab[..., :SSM_WIDTH] * jax.nn.sigmoid(ab[..., SSM_WIDTH:])
    return _rmsnorm(out, g_norm)


def _diff_attention_group(q, k, v, cos, sin, lam, lam_init, g_subln):
    bsz, seqlen, _ = q.shape
    f32 = jnp.float32
    q = _rotary(q.reshape(bsz, seqlen, DIFF_HEADS, 2, DIFF_QK_DIM), cos, sin)
    k = _rotary(k.reshape(bsz, seqlen, DIFF_HEADS, 2, DIFF_QK_DIM), cos, sin)
    q = q.transpose(0, 2, 3, 1, 4).astype(f32)
    k = k.transpose(0, 2, 3, 1, 4).astype(f32)
    vf = v.reshape(bsz, seqlen, DIFF_HEADS, DIFF_V_DIM).transpose(0, 2, 1, 3).astype(f32)
    n_blocks = seqlen // Q_BLOCK
    q_blocks = jnp.moveaxis(q.reshape(bsz, DIFF_HEADS, 2, n_blocks, Q_BLOCK, DIFF_QK_DIM), 3, 0)
    key_chunk = jnp.arange(seqlen) // CHUNK
    scale = 1.0 / math.sqrt(DIFF_QK_DIM)

    def block(args):
        qb, bi = args
        s = jnp.einsum('bhcqd,bhckd->bhcqk', qb, k) * scale
        q_chunk = (bi * Q_BLOCK + jnp.arange(Q_BLOCK)) // CHUNK
        mask = key_chunk[None, :] <= q_chunk[:, None]
        s = jnp.where(mask, s, -jnp.inf)
        p = jax.nn.softmax(s, axis=-1)
        w = p[:, :, 0] - lam * p[:, :, 1]
        return jnp.einsum('bhqk,bhkd->bhqd', w, vf)

    out = lax.map(block, (q_blocks, jnp.arange(n_blocks)))
    out = out.transpose(1, 0, 3, 2, 4).reshape(bsz, seqlen, DIFF_HEADS, DIFF_V_DIM).astype(v.dtype)
    out = _rmsnorm(out, g_subln) * (1.0 - lam_init)
    return out.reshape(bsz, seqlen, ATTN_WIDTH)


def _conv_gated_ffn(h, w_up, w_conv, b_conv, w_down):
    seqlen = h.shape[1]
    up = h @ w_up
    padded = jnp.pad(up, ((0, 0), (CONV_WIDTH - 1, 0), (0, 0)))
    conv = b_conv + sum(w_conv[j] * padded[:, j:j + seqlen] for j in range(CONV_WIDTH))
    gate = conv[..., :FFN_DIM]
    val = conv[..., FFN_DIM:]
    return (jax.nn.silu(gate) * val) @ w_down


def setup_inputs(seed: int = 0) -> dict:
    key = jax.random.key(seed)
    ks = jax.random.split(key, 26)
    f32 = jnp.float32
    nrm = lambda k, shape, s: jax.random.normal(k, shape, f32) * s
    x = jax.random.normal(ks[0], (BATCH, SEQ, D_MODEL), f32)
    offset = jax.random.randint(ks[1], (BATCH, 1), 0, 4096, dtype=jnp.int32)
    positions = offset + jnp.arange(SEQ, dtype=jnp.int32)[None, :]
    n_idx = jnp.arange(SSM_STATE, dtype=f32)
    ssm_lambda_re = -0.5 + nrm(ks[2], (DEPTH, SSM_GROUPS, SSM_STATE), 0.01)
    ssm_lambda_im = math.pi * n_idx + nrm(ks[3], (DEPTH, SSM_GROUPS, SSM_STATE), 0.01)
    ssm_log_step = jax.random.uniform(ks[4], (DEPTH, SSM_GROUPS), f32, math.log(1e-3), math.log(1e-1))
    return {
        "x": x,
        "positions": positions,
        "norm_mix": 1.0 + nrm(ks[5], (DEPTH, D_MODEL), 0.02),
        "w_in": nrm(ks[6], (DEPTH, D_MODEL, IN_WIDTH), D_MODEL ** -0.5),
        "ssm_lambda_re": ssm_lambda_re,
        "ssm_lambda_im": ssm_lambda_im,
        "ssm_log_step": ssm_log_step,
        "ssm_b_re": nrm(ks[7], (DEPTH, SSM_GROUPS, SSM_STATE, SSM_GROUP_CH), (2 * SSM_GROUP_CH) ** -0.5),
        "ssm_b_im": nrm(ks[8], (DEPTH, SSM_GROUPS, SSM_STATE, SSM_GROUP_CH), (2 * SSM_GROUP_CH) ** -0.5),
        "ssm_c_re": nrm(ks[9], (DEPTH, SSM_GROUPS, SSM_GROUP_CH, SSM_STATE), (2 * SSM_STATE) ** -0.5),
        "ssm_c_im": nrm(ks[10], (DEPTH, SSM_GROUPS, SSM_GROUP_CH, SSM_STATE), (2 * SSM_STATE) ** -0.5),
        "ssm_d": nrm(ks[11], (DEPTH, SSM_GROUPS, SSM_GROUP_CH), 1.0),
        "ssm_w_glu": nrm(ks[12], (DEPTH, SSM_WIDTH, 2 * SSM_WIDTH), SSM_WIDTH ** -0.5),
        "ssm_norm": 1.0 + nrm(ks[13], (DEPTH, SSM_WIDTH), 0.02),
        "lambda_q1": nrm(ks[14], (DEPTH, DIFF_QK_DIM), 0.1),
        "lambda_k1": nrm(ks[15], (DEPTH, DIFF_QK_DIM), 0.1),
        "lambda_q2": nrm(ks[16], (DEPTH, DIFF_QK_DIM), 0.1),
        "lambda_k2": nrm(ks[17], (DEPTH, DIFF_QK_DIM), 0.1),
        "attn_subln": 1.0 + nrm(ks[18], (DEPTH, DIFF_V_DIM), 0.02),
        "w_out": nrm(ks[19], (DEPTH, MIX_WIDTH, D_MODEL), MIX_WIDTH ** -0.5),
        "norm_ffn": 1.0 + nrm(ks[20], (DEPTH, D_MODEL), 0.02),
        "w_up": nrm(ks[21], (DEPTH, D_MODEL, 2 * FFN_DIM), D_MODEL ** -0.5),
        "w_conv": nrm(ks[22], (DEPTH, CONV_WIDTH, 2 * FFN_DIM), CONV_WIDTH ** -0.5),
        "b_conv": nrm(ks[23], (DEPTH, 2 * FFN_DIM), 0.02),
        "w_down": nrm(ks[24], (DEPTH, FFN_DIM, D_MODEL), FFN_DIM ** -0.5),
        "norm_final": 1.0 + nrm(ks[25], (D_MODEL,), 0.02),
    }


def reference(x, positions, norm_mix, w_in, ssm_lambda_re, ssm_lambda_im, ssm_log_step,
              ssm_b_re, ssm_b_im, ssm_c_re, ssm_c_im, ssm_d, ssm_w_glu, ssm_norm,
              lambda_q1, lambda_k1, lambda_q2, lambda_k2, attn_subln, w_out,
              norm_ffn, w_up, w_conv, b_conv, w_down, norm_final):
    f32 = jnp.float32
    inv_freq = ROPE_THETA ** (-(jnp.arange(0, ROT_DIM, 2, dtype=f32) / ROT_DIM))
    ang = positions.astype(f32)[..., None] * inv_freq
    cos = jnp.cos(ang)[:, :, None, None, :]
    sin = jnp.sin(ang)[:, :, None, None, :]
    for i in range(DEPTH):
        lam_init = 0.8 - 0.6 * math.exp(-0.3 * i)
        lam = (jnp.exp(jnp.sum(lambda_q1[i].astype(f32) * lambda_k1[i].astype(f32)))
               - jnp.exp(jnp.sum(lambda_q2[i].astype(f32) * lambda_k2[i].astype(f32)))
               + lam_init)
        h = _rmsnorm(x, norm_mix[i])
        proj = h @ w_in[i]
        u = proj[..., :SSM_WIDTH]
        q = proj[..., SSM_WIDTH:SSM_WIDTH + QK_WIDTH]
        k = proj[..., SSM_WIDTH + QK_WIDTH:SSM_WIDTH + 2 * QK_WIDTH]
        v = proj[..., SSM_WIDTH + 2 * QK_WIDTH:]
        y_ssm = _s5_group(u, ssm_lambda_re[i], ssm_lambda_im[i], ssm_log_step[i],
                          ssm_b_re[i], ssm_b_im[i], ssm_c_re[i], ssm_c_im[i], ssm_d[i],
                          ssm_w_glu[i], ssm_norm[i])
        y_att = _diff_attention_group(q, k, v, cos, sin, lam, lam_init, attn_subln[i])
        x = x + jnp.concatenate([y_ssm, y_att], axis=-1) @ w_out[i]
        h = _rmsnorm(x, norm_ffn[i])
        x = x + _conv_gated_ffn(h, w_up[i], w_conv[i], b_conv[i], w_down[i])
    return _rmsnorm(x, norm_final)
```

```python
import math
from contextlib import ExitStack

import numpy as np
import ml_dtypes
import concourse.bass as bass
import concourse.mybir as mybir
from concourse.bass_utils import run_bass_kernel_spmd

F32 = mybir.dt.float32
BF16 = mybir.dt.bfloat16
I32 = mybir.dt.int32
ALU = mybir.AluOpType
AF = mybir.ActivationFunctionType
_ESZ = {F32: 4, BF16: 2, I32: 4}

D = 1024
T = 2048
NT = 4
DEPTH = 4
FFN = 2816
NFC = 22
R = 16
NB = T // R
EPS = 1e-6
TWO_PI = 2.0 * math.pi
C1 = 6.28125
C2 = TWO_PI - C1
SB_LO = 16512
SB_HI = 229344
GROUPS = [[0, 1, 2, 3], [4, 5, 6, 7]]


class Sched:
    ENGS = ("pe", "act", "dve", "pool", "sp")
    BK = 2048

    def __init__(self, nc, n_dma_sems=32):
        self.nc = nc
        self.ops = {e: [] for e in self.ENGS}
        self.tinfo = {}
        self.recs = {}
        self.buckets = {}
        self.known = {e: {} for e in self.ENGS}
        self.known_dma = {e: {} for e in self.ENGS}
        self.snap = {}
        self.targets = {e: set() for e in self.ENGS}
        self.n_dma = 0
        self.n_dma_sems = n_dma_sems
        self.n_cc = 0
        self.uid = 0

    def sb(self, shape, dtype, offset):
        self.uid += 1
        h = self.nc.alloc_sbuf_tensor_at("t%d" % self.uid, list(shape), dtype, offset=offset)
        self.tinfo[h.name] = ("sb", offset, int(np.prod(shape[1:])) * _ESZ[dtype])
        return h

    def ps(self, name, shape, dtype=F32):
        h = self.nc.alloc_psum_tensor(name, list(shape), dtype)
        self.tinfo[h.name] = ("ps", 0, int(np.prod(shape[1:])) * _ESZ[dtype])
        return h

    def dram(self, name, shape, dtype, kind="Internal"):
        h = self.nc.dram_tensor(name, list(shape), dtype, kind=kind)
        self.tinfo[h.name] = ("dr:" + name, 0, None)
        return h

    def region(self, ap):
        space, base, psb = self.tinfo[ap.tensor.name]
        esz = _ESZ[ap.dtype]
        aps = ap.ap
        off = int(ap.offset) * esz
        if psb is None:
            span = sum((c - 1) * abs(s) for s, c in aps) * esz
            return (space, off, off + span + esz, 0, 1)
        p0 = off // psb
        fo = off % psb
        span = sum((c - 1) * abs(s) for s, c in aps[1:]) * esz
        pstep, pcnt = aps[0]
        nstep = max(1, (pstep * esz) // psb) if pstep else 1
        return (space, base + fo, base + fo + span + esz, p0, p0 + (pcnt - 1) * nstep + 1)

    def _bk(self, rg):
        if rg[0][0] == "d":
            return range(0, 1)
        return range(rg[1] // self.BK, (rg[2] - 1) // self.BK + 1)

    def add(self, eng, emit, reads=(), writes=(), kind="cmp"):
        rr = [self.region(a) for a in reads if a is not None and hasattr(a, "tensor")]
        ww = [self.region(a) for a in writes if a is not None]
        pa = [(g[0], g[1] // 2048 * 2048, ((g[2] - 1) // 2048 + 1) * 2048, 0, 128) for g in rr + ww if g[0] == "ps"]
        rr = [g for g in rr if g[0] != "ps"]
        ww = [g for g in ww if g[0] != "ps"] + pa
        seq = len(self.ops[eng])
        op = {"eng": eng, "emit": emit, "kind": kind, "seq": seq, "waits": [], "dmawaits": [], "sem": None}
        if kind == "dma":
            i = self.n_dma
            self.n_dma += 1
            P = self.n_dma_sems
            op["sem"] = ("d", i % P, 16 * (i // P + 1))
            if i >= P:
                self._need_dma(op, ("d", i % P, 16 * (i // P)))
        elif kind == "cc":
            i = self.n_cc
            self.n_cc += 1
            op["sem"] = ("cc", i, 1)
        wid = ("E", eng, seq) if kind == "cmp" else ("D",) + op["sem"]
        for rg in rr:
            for key in self._overlaps(rg):
                w = self.recs[key][0]
                if w is not None:
                    self._need(op, w)
        for rg in ww:
            for key in list(self._overlaps(rg)):
                rec = self.recs[key]
                if rec[0] is not None:
                    self._need(op, rec[0])
                for e2, s2 in rec[1].items():
                    self._need(op, ("E", e2, s2))
                for d in rec[2]:
                    self._need(op, d)
                if key[1] >= rg[1] and key[2] <= rg[2] and key[3] >= rg[3] and key[4] <= rg[4]:
                    self._del(key)
        for rg in rr:
            rec = self._get(rg)
            if kind == "cmp":
                rec[1][eng] = seq
            else:
                rec[2].append(wid)
        for rg in ww:
            rec = self._get(rg)
            rec[0] = wid
            rec[1] = {}
            rec[2] = []
        if kind == "cmp":
            self.snap[(eng, seq)] = dict(self.known[eng])
        self.ops[eng].append(op)
        return op

    def _get(self, rg):
        rec = self.recs.get(rg)
        if rec is None:
            rec = [None, {}, []]
            self.recs[rg] = rec
            for b in self._bk(rg):
                self.buckets.setdefault((rg[0], b), set()).add(rg)
        return rec

    def _del(self, key):
        del self.recs[key]
        for b in self._bk(key):
            self.buckets[(key[0], b)].discard(key)

    def _overlaps(self, rg):
        out = set()
        for b in self._bk(rg):
            for key in self.buckets.get((rg[0], b), ()):
                if key[1] < rg[2] and rg[1] < key[2] and key[3] < rg[4] and rg[3] < key[4]:
                    out.add(key)
        return out

    def _need(self, op, w):
        if w[0] == "E":
            self._need_eng(op, w[1], w[2])
        else:
            self._need_dma(op, w[1:])

    def _need_eng(self, op, f, s):
        e = op["eng"]
        if e == f and e == "pe":
            return
        if e == f and s >= op["seq"]:
            return
        kn = self.known[e]
        if kn.get(f, -1) >= s:
            return
        kn[f] = s
        for f2, s2 in self.snap.get((f, s), {}).items():
            if f2 != e and kn.get(f2, -1) < s2:
                kn[f2] = s2
        op["waits"].append((f, s))
        self.targets[f].add(s)

    def _need_dma(self, op, d):
        kd = self.known_dma[op["eng"]]
        key = (d[0], d[1])
        if kd.get(key, 0) >= d[2]:
            return
        kd[key] = d[2]
        op["dmawaits"].append(d)

    def finish(self):
        op = {"eng": "sp", "emit": lambda e: e.nop(), "kind": "cmp", "seq": len(self.ops["sp"]),
              "waits": [], "dmawaits": [], "sem": None}
        for e in self.ENGS:
            if e != "sp" and self.ops[e]:
                self._need_eng(op, e, len(self.ops[e]) - 1)
        P = self.n_dma_sems
        for i in range(max(0, self.n_dma - P), self.n_dma):
            self._need_dma(op, ("d", i % P, 16 * (i // P + 1)))
        for i in range(self.n_cc):
            self._need_dma(op, ("cc", i, 1))
        self.ops["sp"].append(op)

    def build(self):
        nc = self.nc
        with ExitStack() as es:
            sem_e = {e: es.enter_context(nc.semaphore("s_" + e)) for e in self.ENGS}
            sem_d = {}
            for i in range(self.n_dma_sems):
                sem_d[("d", i)] = es.enter_context(nc.semaphore("d%d" % i))
            for i in range(self.n_cc):
                sem_d[("cc", i)] = es.enter_context(nc.semaphore("cc%d" % i))
            rank = {}
            for e in self.ENGS:
                for r, s in enumerate(sorted(self.targets[e])):
                    rank[(e, s)] = r + 1
            block = es.enter_context(nc.Block())
            handles = {"pe": block.tensor, "act": block.scalar, "dve": block.vector,
                       "pool": block.gpsimd, "sp": block.sync}
            for e in self.ENGS:
                ops = self.ops[e]
                if not ops:
                    continue

                def run(eng, ops=ops, e=e):
                    for op in ops:
                        for f, s in op["waits"]:
                            eng.wait_ge(sem_e[f], rank[(f, s)])
                        for d in op["dmawaits"]:
                            eng.wait_ge(sem_d[(d[0], d[1])], d[2])
                        ins = op["emit"](eng)
                        if op["kind"] == "dma":
                            ins.then_inc(sem_d[(op["sem"][0], op["sem"][1])], 16)
                        elif op["kind"] == "cc":
                            ins.then_inc(sem_d[(op["sem"][0], op["sem"][1])])
                        elif (e, op["seq"]) in rank:
                            ins.then_inc(sem_e[e], 1)
                handles[e](run)

    def mm(self, out, lhsT, rhs, start=True, stop=True, **kw):
        return self.add("pe", lambda e: e.matmul(out, lhsT, rhs, start=start, stop=stop, **kw), [lhsT, rhs], [out])

    def act(self, out, in_, func, bias=0.0, scale=1.0):
        return self.add("act", lambda e: e.activation(out, in_, func, bias=bias, scale=scale), [in_, bias, scale], [out])

    def tt(self, out, in0, in1, op, eng="dve"):
        return self.add(eng, lambda e: e.tensor_tensor(out, in0, in1, op), [in0, in1], [out])

    def ts(self, out, in0, s1, s2=None, op0=ALU.mult, op1=None, eng="dve"):
        if op1 is None:
            return self.add(eng, lambda e: e.tensor_scalar(out, in0, s1, None, op0), [in0, s1], [out])
        return self.add(eng, lambda e: e.tensor_scalar(out, in0, s1, s2, op0, op1), [in0, s1, s2], [out])

    def stt(self, out, in0, scalar, in1, op0, op1, eng="dve"):
        return self.add(eng, lambda e: e.scalar_tensor_tensor(out, in0, scalar, in1, op0, op1), [in0, scalar, in1], [out])

    def copy(self, out, in_, eng="dve"):
        return self.add(eng, lambda e: e.tensor_copy(out, in_), [in_], [out])

    def memset(self, out, val, eng="dve"):
        return self.add(eng, lambda e: e.memset(out, val), [], [out])

    def recip(self, out, in_):
        return self.add("dve", lambda e: e.reciprocal(out, in_), [in_], [out])

    def rsum(self, out, in_):
        return self.add("dve", lambda e: e.reduce_sum(out, in_, mybir.AxisListType.X), [in_], [out])

    def scan(self, out, d0, d1, init):
        return self.add("dve", lambda e: e.tensor_tensor_scan(out, d0, d1, init, ALU.mult, ALU.add), [d0, d1, init], [out])

    def dma(self, out, in_):
        return self.add("sp", lambda e: e.dma_start(out=out, in_=in_), [in_], [out], kind="dma")

    def allgather(self, out, in_):
        return self.add("pool", lambda e: e.collective_compute(
            "AllGather", ALU.bypass, replica_groups=GROUPS, ins=[in_], outs=[out]), [in_], [out], kind="cc")


class Arena:
    def __init__(self, S, lo, hi):
        self.S, self.lo, self.hi, self.top = S, lo, hi, lo

    def alloc(self, shape, dt):
        nb = int(np.prod(shape[1:])) * _ESZ[dt]
        off = (self.top + 63) // 64 * 64
        assert off + nb <= self.hi, ("SBUF overflow", shape, off + nb - self.hi)
        self.top = off + nb
        return self.S.sb(shape, dt, off)

    def mark(self):
        return self.top

    def release(self, m):
        self.top = m


LP = {}
_o = 0
for _n, _w in (("gmix", 8), ("gffn", 8), ("gssm", 4), ("gsub", 1), ("dvec", 4), ("lamre", 16), ("lamim", 16),
               ("logstep", 16), ("wconv", 132), ("bconv", 44), ("lamqk", 256), ("gfin", 8), ("nlaminit", 1), ("oml", 1)):
    LP[_n] = (_o, _w)
    _o += _w
NLP = _o
CS = {}
_o = 0
for _n, _w in (("invf", 1), ("sgn", 1), ("tau", 17), ("bv", 129), ("rankbias", 3), ("ohq", 4), ("ohprev", 4)):
    CS[_n] = (_o, _w)
    _o += _w
NCS = _o
AGW = 16384


def build_program(L, final, debug=(), stop=None):
    nc = bass.Bass("TRN2", target_bir_lowering=False)
    S = Sched(nc)
    A = Arena(S, SB_LO, SB_HI)
    dbg_out = {}

    x_in = S.dram("x_in", [128, 8, T], F32, kind="ExternalInput")
    pos = S.dram("pos", [1, T], I32, kind="ExternalInput")
    cst_d = S.dram("cst", [128, NCS], F32, kind="ExternalInput")
    cbf_d = S.dram("cbf", [128, 256 + 2048], BF16, kind="ExternalInput")
    lp_d = S.dram("lp", [L, 128, NLP], F32, kind="ExternalInput")
    lpb_d = S.dram("lpb", [L, 128, 2048], F32, kind="ExternalInput")
    w_in_d = S.dram("w_in", [L, 8, 128, 8, 256], F32, kind="ExternalInput")
    w_glu_d = S.dram("w_glu", [L, 2, 128, 4, 512], F32, kind="ExternalInput")
    w_out_d = S.dram("w_out", [L, 4, 128, 8, 256], F32, kind="ExternalInput")
    w_up_d = S.dram("w_up", [L, NFC, 128, 8, 256], F32, kind="ExternalInput")
    w_dn_d = S.dram("w_dn", [L, 8, 2, 128, 11, 128], F32, kind="ExternalInput")
    out_d = S.dram("out", [128, 8, T], F32, kind="ExternalOutput")
    agin = [S.dram("agin%d" % h, [128, 4096], BF16) for h in range(4)]
    agout = [S.dram("agout%d" % h, [512, 4096], BF16) for h in range(4)]
    agEin = S.dram("agEin", [128, 32], F32)
    agEout = S.dram("agEout", [512, 32], F32)
    ag2in = S.dram("ag2in", [128, 16], F32)
    ag2out = S.dram("ag2out", [512, 16], F32)

    def dbg(name, ap):
        if name not in debug:
            return
        t = S.dram("dbg_" + name, list(ap.shape), ap.dtype, kind="ExternalOutput")
        S.dma(t.ap(), ap)
        dbg_out[name] = True

    X = A.alloc([128, 8, T], F32)
    uqkv_lo = (A.top + 63) // 64 * 64
    U = A.alloc([128, 4, T], BF16)
    Q = A.alloc([128, 4, T], BF16)
    K = A.alloc([128, 4, T], BF16)
    V = A.alloc([128, 4, 16, 128], BF16)
    A2 = Arena(S, uqkv_lo, A.top)
    CST = A.alloc([128, NCS], F32)
    CBF = A.alloc([128, 256 + 2048], BF16)
    ONES = A.alloc([128, 128], BF16)
    EPSC = A.alloc([128, 1], F32)
    LPT = A.alloc([128, NLP], F32)
    PS = S.ps("psum", [128, 4096], F32)

    def cs(name, a=0, b=None):
        o, w = CS[name]
        return CST[:, o + a:o + (w if b is None else b)]

    def lpc(name, a=0, b=None):
        o, w = LP[name]
        return LPT[:, o + a:o + (w if b is None else b)]

    IDENT = CBF[:, 0:128]
    ROTM = CBF[:, 128:256]

    def MASK(i):
        return CBF[:, 256 + 512 * i:256 + 512 * (i + 1)]

    bank_ctr = [0]

    def bank(n=1):
        b = bank_ctr[0] % 8
        bank_ctr[0] += 1
        return PS[:, 512 * b:512 * (b + 1)]

    def bankn(b):
        return PS[:, 512 * b:512 * (b + 1)]

    S.dma(CST[:, :], cst_d.ap())
    S.dma(CBF[:, :], cbf_d.ap())
    for q in range(4):
        S.dma(X[:, 2 * q:2 * q + 2, :], x_in.ap()[:, 2 * q:2 * q + 2, :])
    S.memset(ONES[:, :], 1.0)
    S.memset(EPSC[:, :], EPS)

    def reduce_angle(rout, ang, tmpf, tmpi):
        S.ts(tmpf, ang, 1.0 / TWO_PI, None, ALU.mult)
        S.copy(tmpi, tmpf)
        S.copy(tmpf, tmpi)
        S.stt(rout, tmpf, -C1, ang, ALU.mult, ALU.add)
        S.stt(rout, tmpf, -C2, rout, ALU.mult, ALU.add)

    def rmsnorm_tile(dst, src_of_dc, ndc, width, gcol, nfeat, sq_tmp, rs, dst_is_list=False):
        P = bank()
        for dc in range(ndc):
            sq = sq_tmp[dc % 2]
            S.act(sq, src_of_dc(dc), AF.Square)
            S.mm(P[:, 0:width], ONES[:, :], sq, start=(dc == 0), stop=(dc == ndc - 1))
        S.act(rs, P[:, 0:width], AF.Sqrt, bias=EPSC[:, 0:1], scale=1.0 / nfeat)
        S.recip(rs, rs)
        for dc in range(ndc):
            S.stt(dst(dc), src_of_dc(dc), gcol(dc), rs, ALU.mult, ALU.mult)

    def layer(li):
        S.dma(LPT[:, :], lp_d.ap()[li])
        m_layer = A.mark()

        WIN = A.alloc([128, 8, 2048], BF16)
        STG = A.alloc([128, 8, 256], F32)
        for g in range(8):
            S.dma(STG[:, :, :], w_in_d.ap()[li, g])
            S.copy(WIN[:, :, 256 * g:256 * (g + 1)], STG[:, :, :], eng="pool")
        HT = A.alloc([128, 8, 512], BF16)
        SQ = [A.alloc([128, 512], BF16) for _ in range(2)]
        RS = A.alloc([128, 512], F32)
        PI = A.alloc([128, 512], I32)
        ANG = A.alloc([128, 512], F32)
        TF = A.alloc([128, 512], F32)
        COSF = A.alloc([128, 512], F32)
        SINF = A.alloc([128, 512], F32)
        QB = A.alloc([128, 512], BF16)
        T1 = A.alloc([128, 512], F32)
        T2 = A.alloc([128, 512], F32)
        PARTS = set(stop.split(':')[1].split('+')) if (stop and ':' in stop) else {'norm', 'rope', 'proj', 'v'}
        for tt in range(NT):
            tok = slice(512 * tt, 512 * (tt + 1))
            rmsnorm_tile(lambda dc: HT[:, dc, :], lambda dc: X[:, dc, tok], 8, 512,
                         lambda dc: lpc("gmix", dc, dc + 1), D, [SQ[0][:, :], SQ[1][:, :]], RS[:, :])
            if 'rope' not in PARTS:
                continue
            S.dma(PI[:, :], pos.ap()[0:1, tok].partition_broadcast(128))
            S.copy(TF[:, :], PI[:, :])
            S.ts(ANG[:, :], TF[:, :], cs("invf"), None, ALU.mult)
            reduce_angle(ANG[:, :], ANG[:, :], TF[:, :], PI[:, :])
            S.act(SINF[:, :], ANG[:, :], AF.Sin, scale=cs("sgn"))
            S.act(COSF[:, :], ANG[:, :], AF.Sin, scale=0.5)
            S.tt(COSF[:, :], COSF[:, :], COSF[:, :], ALU.mult)
            S.ts(COSF[:, :], COSF[:, :], -2.0, 1.0, ALU.mult, ALU.add)
            for fc in range(12 if 'proj' in PARTS else 0):
                P = bank()
                for dc in range(8):
                    S.mm(P, WIN[:, dc, 128 * fc:128 * (fc + 1)], HT[:, dc, :], start=(dc == 0), stop=(dc == 7))
                if fc < 4:
                    S.act(U[:, fc, tok], P, AF.Copy)
                else:
                    dst = Q[:, fc - 4, tok] if fc < 8 else K[:, fc - 8, tok]
                    S.act(QB[:, :], P, AF.Copy)
                    PR = bank()
                    S.mm(PR, ROTM, QB[:, :])
                    S.tt(T1[:, :], P, COSF[:, :], ALU.mult)
                    S.tt(T2[:, :], PR, SINF[:, :], ALU.mult)
                    S.tt(dst, T1[:, :], T2[:, :], ALU.add, eng="pool")
            for s in range(4 if 'v' in PARTS else 0):
                P = bank()
                for dc in range(8):
                    S.mm(P, HT[:, dc, 128 * s:128 * (s + 1)], WIN[:, dc, 1536:2048], start=(dc == 0), stop=(dc == 7))
                S.act(V[:, :, 4 * tt + s, :], P.rearrange("p (h d) -> p h d", h=4), AF.Copy)
        dbg("U", U[:, :, :]); dbg("Q", Q[:, :, :]); dbg("K", K[:, :, :]); dbg("V", V[:, :, :, :])
        A.release(m_layer)
        if stop and (stop == "A" or stop.startswith("A:")):
            return

        for h in range(4):
            S.dma(agin[h].ap()[:, 0:2048], K[:, h, :])
            S.dma(agin[h].ap()[:, 2048:4096], V[:, h, :, :].rearrange("p a b -> p (a b)"))
            S.allgather(agout[h].ap().opt(), agin[h].ap().opt())

        AR = A.alloc([128, 17, 16], F32)
        AI = A.alloc([128, 17, 16], F32)
        NAI = A.alloc([128, 17, 16], F32)
        SLI = A.alloc([128, 16], F32)
        RHO = A.alloc([128, 16], F32)
        MT = A.alloc([128, 16], F32)
        CT128 = A.alloc([128, 16], F32)
        ST128 = A.alloc([128, 16], F32)
        BR = A.alloc([128, 512], F32)
        BI = A.alloc([128, 512], F32)
        CB = A.alloc([128, 2, 512], BF16)
        XRE = A.alloc([128, 16, NB], F32)
        XIM = A.alloc([128, 16, NB], F32)
        EL = A.alloc([128, 2, 16], F32)
        m_ssm = A.mark()
        if True:
            CRE = A.alloc([128, 512], F32)
            CIM = A.alloc([128, 512], F32)
            S.dma(CRE[:, :], lpb_d.ap()[li, :, 1024:1536])
            S.dma(CIM[:, :], lpb_d.ap()[li, :, 1536:2048])
            BRE = A.alloc([128, 512], F32)
            BIM = A.alloc([128, 512], F32)
            S.dma(BRE[:, :], lpb_d.ap()[li, :, 0:512])
            S.dma(BIM[:, :], lpb_d.ap()[li, :, 512:1024])
            STEP = A.alloc([128, 16], F32)
            SLR = A.alloc([128, 16], F32)
            W17 = [A.alloc([128, 17, 16], F32) for _ in range(4)]
            W17I = A.alloc([128, 17, 16], I32)
            S.act(STEP[:, :], lpc("logstep"), AF.Exp)
            S.tt(SLR[:, :], STEP[:, :], lpc("lamre"), ALU.mult)
            S.tt(SLI[:, :], STEP[:, :], lpc("lamim"), ALU.mult)
            S.act(RHO[:, :], SLR[:, :], AF.Exp, scale=float(R))
            S.act(MT[:, :], SLR[:, :], AF.Exp, scale=float(T))
            taub = cs("tau").unsqueeze(2).broadcast_to([128, 17, 16])
            S.tt(W17[0][:, :, :], SLR[:, :].unsqueeze(1).broadcast_to([128, 17, 16]), taub, ALU.mult)
            S.act(W17[0][:, :, :], W17[0][:, :, :], AF.Exp)
            S.tt(W17[1][:, :, :], SLI[:, :].unsqueeze(1).broadcast_to([128, 17, 16]), taub, ALU.mult)
            reduce_angle(W17[1][:, :, :], W17[1][:, :, :], W17[2][:, :, :], W17I[:, :, :])
            S.act(W17[2][:, :, :], W17[1][:, :, :], AF.Sin)
            S.act(W17[3][:, :, :], W17[1][:, :, :], AF.Sin, scale=0.5)
            S.tt(W17[3][:, :, :], W17[3][:, :, :], W17[3][:, :, :], ALU.mult)
            S.ts(W17[3][:, :, :], W17[3][:, :, :], -2.0, 1.0, ALU.mult, ALU.add)
            S.tt(AR[:, :, :], W17[0][:, :, :], W17[3][:, :, :], ALU.mult)
            S.tt(AI[:, :, :], W17[0][:, :, :], W17[2][:, :, :], ALU.mult)
            S.ts(NAI[:, :, :], AI[:, :, :], -1.0, None, ALU.mult)
            den, am1, fr, fi, t0 = [A.alloc([128, 16], F32) for _ in range(5)]
            S.tt(den[:, :], lpc("lamre"), lpc("lamre"), ALU.mult)
            S.tt(t0[:, :], lpc("lamim"), lpc("lamim"), ALU.mult)
            S.tt(den[:, :], den[:, :], t0[:, :], ALU.add)
            S.recip(den[:, :], den[:, :])
            S.ts(am1[:, :], AR[:, 1, :], -1.0, None, ALU.add)
            S.tt(fr[:, :], am1[:, :], lpc("lamre"), ALU.mult)
            S.tt(t0[:, :], AI[:, 1, :], lpc("lamim"), ALU.mult)
            S.tt(fr[:, :], fr[:, :], t0[:, :], ALU.add)
            S.tt(fr[:, :], fr[:, :], den[:, :], ALU.mult)
            S.tt(fi[:, :], AI[:, 1, :], lpc("lamre"), ALU.mult)
            S.tt(t0[:, :], am1[:, :], lpc("lamim"), ALU.mult)
            S.tt(fi[:, :], fi[:, :], t0[:, :], ALU.subtract)
            S.tt(fi[:, :], fi[:, :], den[:, :], ALU.mult)
            frb = fr[:, :].unsqueeze(2).broadcast_to([128, 16, 32])
            fib = fi[:, :].unsqueeze(2).broadcast_to([128, 16, 32])
            v3 = lambda t: t[:, :].rearrange("p (k c) -> p k c", k=16)
            TB1 = A.alloc([128, 512], F32)
            S.tt(v3(BR), frb, v3(BRE), ALU.mult)
            S.tt(v3(TB1), fib, v3(BIM), ALU.mult)
            S.tt(BR[:, :], BR[:, :], TB1[:, :], ALU.subtract)
            S.tt(v3(BI), frb, v3(BIM), ALU.mult)
            S.tt(v3(TB1), fib, v3(BRE), ALU.mult)
            S.tt(BI[:, :], BI[:, :], TB1[:, :], ALU.add)
            S.copy(CB[:, 0, :], CRE[:, :])
            S.ts(CB[:, 1, :], CIM[:, :], -1.0, None, ALU.mult)
        A.release(m_ssm)

        def pair_tables(ct, CT, ST, tf, ti):
            S.tt(CT[:, :, :], SLI[:, 4 * ct:4 * ct + 4].unsqueeze(2).broadcast_to([128, 4, 129]),
                 cs("bv").unsqueeze(1).broadcast_to([128, 4, 129]), ALU.mult)
            reduce_angle(CT[:, :, :], CT[:, :, :], tf[:, :, :], ti[:, :, :])
            S.act(ST[:, :, :], CT[:, :, :], AF.Sin)
            S.act(CT[:, :, :], CT[:, :, :], AF.Sin, scale=0.5)
            S.tt(CT[:, :, :], CT[:, :, :], CT[:, :, :], ALU.mult)
            S.ts(CT[:, :, :], CT[:, :, :], -2.0, 1.0, ALU.mult, ALU.add)

        def make_WE(ct, WE, ta, tb):
            for half in range(2):
                ts_ = slice(8 * half, 8 * half + 8)
                arb = AR[:, ts_, 4 * ct:4 * ct + 4].unsqueeze(3).broadcast_to([128, 8, 4, 32])
                aib = AI[:, ts_, 4 * ct:4 * ct + 4].unsqueeze(3).broadcast_to([128, 8, 4, 32])
                brb = BR[:, 128 * ct:128 * ct + 128].rearrange("p (i c) -> p i c", i=4).unsqueeze(1).broadcast_to([128, 8, 4, 32])
                bib = BI[:, 128 * ct:128 * ct + 128].rearrange("p (i c) -> p i c", i=4).unsqueeze(1).broadcast_to([128, 8, 4, 32])
                v4 = lambda t: t[:, :].rearrange("p (a i c) -> p a i c", a=8, i=4)
                S.tt(v4(ta), arb, brb, ALU.mult)
                S.tt(v4(tb), aib, bib, ALU.mult)
                S.tt(WE[:, 0, ts_, :].rearrange("p a n -> p (a n)"), ta[:, :], tb[:, :], ALU.subtract, eng="pool")
                S.tt(v4(ta), arb, bib, ALU.mult)
                S.tt(v4(tb), aib, brb, ALU.mult)
                S.tt(WE[:, 1, ts_, :].rearrange("p a n -> p (a n)"), ta[:, :], tb[:, :], ALU.add, eng="pool")

        for ct in range(4):
            m = A.mark()
            WE = A.alloc([128, 2, 16, 128], BF16)
            WDT = A.alloc([128, 2, 16, 128], BF16)
            ta = A.alloc([128, 1024], F32)
            tb = A.alloc([128, 1024], F32)
            CT = A.alloc([128, 4, 129], F32)
            ST = A.alloc([128, 4, 129], F32)
            TI = S.sb([128, 4, 129], I32, S.tinfo[tb.name][1])
            WS = A.alloc([128, 2, 4, NB], F32)
            make_WE(ct, WE, ta, tb)
            pair_tables(ct, CT, ST, ta[:, 0:516].rearrange("p (i b) -> p i b", i=4), TI)
            S.copy(CT128[:, 4 * ct:4 * ct + 4], CT[:, :, 128])
            S.copy(ST128[:, 4 * ct:4 * ct + 4], ST[:, :, 128])
            for x in range(2):
                for tg in range(4):
                    P = bank()
                    for t4 in range(4):
                        S.mm(P[:, 128 * t4:128 * (t4 + 1)], WE[:, x, 4 * tg + t4, :], IDENT)
                    S.act(WDT[:, x, 4 * tg:4 * tg + 4, :].rearrange("p a n -> p (a n)"), P, AF.Copy)
            for x in range(2):
                for j in range(R):
                    for i in range(4):
                        S.mm(PS[:, 512 * (4 * x + i):512 * (4 * x + i) + 128], WDT[32 * i:32 * i + 32, x, R - 1 - j, :],
                             U[32 * i:32 * i + 32, ct, :].rearrange("p (b j) -> p j b", j=R)[:, j, :],
                             start=(j == 0), stop=(j == R - 1), tile_position=(32 * i, 0))
            cc = CT[:, :, 1:129]
            ss = ST[:, :, 1:129]
            pre = PS[:, 0:2048].rearrange("p (i c) -> p i c", i=4)[:, :, 0:128]
            pim = PS[:, 2048:4096].rearrange("p (i c) -> p i c", i=4)[:, :, 0:128]
            v3b = lambda t: t[:, 0:512].rearrange("p (i b) -> p i b", i=4)
            S.tt(v3b(ta), pre, cc, ALU.mult)
            S.tt(v3b(tb), pim, ss, ALU.mult)
            S.tt(XRE[:, 4 * ct:4 * ct + 4, :], v3b(ta), v3b(tb), ALU.add, eng="pool")
            S.tt(v3b(ta), pim, cc, ALU.mult)
            S.tt(v3b(tb), pre, ss, ALU.mult)
            S.tt(XIM[:, 4 * ct:4 * ct + 4, :], v3b(ta), v3b(tb), ALU.subtract, eng="pool")
            for i in range(4):
                k = 4 * ct + i
                S.scan(WS[:, 0, i, :], RHO[:, k:k + 1].broadcast_to([128, NB]), XRE[:, k, :], 0.0)
                S.scan(WS[:, 1, i, :], RHO[:, k:k + 1].broadcast_to([128, NB]), XIM[:, k, :], 0.0)
            e1 = ta[:, 0:4]
            e2 = ta[:, 4:8]
            S.tt(e1, WS[:, 0, :, NB - 1], CT[:, :, 128], ALU.mult)
            S.tt(e2, WS[:, 1, :, NB - 1], ST[:, :, 128], ALU.mult)
            S.tt(EL[:, 0, 4 * ct:4 * ct + 4], e1, e2, ALU.subtract)
            S.tt(e1, WS[:, 1, :, NB - 1], CT[:, :, 128], ALU.mult)
            S.tt(e2, WS[:, 0, :, NB - 1], ST[:, :, 128], ALU.mult)
            S.tt(EL[:, 1, 4 * ct:4 * ct + 4], e1, e2, ALU.add)
            A.release(m)
        if stop == "S1":
            dbg("XRE", XRE[:, :, :]); dbg("EL", EL[:, :, :])
            A.release(m_layer)
            return
        S.dma(agEin.ap(), EL[:, :, :].rearrange("p a b -> p (a b)"))
        S.allgather(agEout.ap().opt(), agEin.ap().opt())
        if stop == "AG":
            A.release(m_layer)
            return

        m_att = A.mark()
        KVP = A.alloc([128, 3, 4096], BF16)
        PTT = A.alloc([128, 4, 512], BF16)
        PT = [PTT[:, i, :] for i in range(4)]
        F1, F2, F3, F4 = [A.alloc([128, 512], F32) for _ in range(4)]
        SQB = A.alloc([128, 512], BF16)
        LAMT = A.alloc([128, 64], F32)
        LS = A.alloc([128, 4], F32)
        NLAM = A.alloc([128, 1], F32)
        GS = A.alloc([128, 1], F32)
        ZB = A.alloc([128, 1], F32)
        S.memset(ZB[:, :], 0.0)
        for c in range(2):
            o = LP["lamqk"][0] + 128 * c
            S.tt(LAMT[:, :], LPT[:, o:o + 64], LPT[:, o + 64:o + 128], ALU.mult)
            S.rsum(LS[:, c:c + 1], LAMT[:, :])
        S.act(LS[:, 2:4], LS[:, 0:2], AF.Exp)
        S.tt(NLAM[:, :], LS[:, 3:4], LS[:, 2:3], ALU.subtract)
        S.tt(NLAM[:, :], NLAM[:, :], lpc("nlaminit"), ALU.add)
        S.tt(GS[:, :], lpc("gsub"), lpc("oml"), ALU.mult)
        pctr = 0
        for h in range(4):
            for r in range(3):
                S.dma(KVP[:, r, :], agout[h].ap().rearrange("(r p) n -> p r n", p=128)[:, r, :])
            for qt in range(4):
                qs = slice(512 * qt, 512 * (qt + 1))
                steps = []
                for kt in range(4 * qt + 4):
                    steps.append((K[:, h, 128 * kt:128 * (kt + 1)], V[:, h, kt, :], ZB[:, 0:1],
                                  (kt - 4 * qt) if kt >= 4 * qt else None))
                for r in range(3):
                    for kt in range(16):
                        steps.append((KVP[:, r, 128 * kt:128 * (kt + 1)], KVP[:, r, 2048 + 128 * kt:2048 + 128 * (kt + 1)],
                                      cs("rankbias", r, r + 1), None))
                O1, O2, D1, D2 = bankn(4), bankn(5), bankn(6), bankn(7)
                def emit_scores(idx):
                    kt_ap = steps[idx][0]
                    S.mm(bankn((2 * idx) % 4), kt_ap[0:64, :], Q[0:64, h, qs])
                    S.mm(bankn((2 * idx + 1) % 4), kt_ap[64:128, :], Q[64:128, h, qs])

                emit_scores(0)
                for idx, (kt_ap, v_ap, b_ap, mi) in enumerate(steps):
                    first, last = idx == 0, idx == len(steps) - 1
                    S1 = bankn((2 * idx) % 4)
                    S2 = bankn((2 * idx + 1) % 4)
                    if not last:
                        emit_scores(idx + 1)
                    pb = pctr % 4
                    P1 = PT[pb]
                    P2 = PT[pb + 1]
                    pctr += 2
                    sb0 = (2 * idx) % 4
                    S.act(PTT[:, pb:pb + 2, :], PS[:, 512 * sb0:512 * (sb0 + 2)].rearrange("p (c n) -> p c n", c=2),
                          AF.Exp, bias=b_ap, scale=0.125)
                    if mi is not None:
                        S.tt(PTT[:, pb:pb + 2, :], PTT[:, pb:pb + 2, :],
                             MASK(mi).unsqueeze(1).broadcast_to([128, 2, 512]), ALU.mult, eng="pool")
                    S.mm(O1, v_ap, P1, start=first, stop=last)
                    S.mm(D1, ONES[:, :], P1, start=first, stop=last)
                    S.mm(O2, v_ap, P2, start=first, stop=last)
                    S.mm(D2, ONES[:, :], P2, start=first, stop=last)
                S.recip(F1[:, :], D1)
                S.recip(F2[:, :], D2)
                S.tt(F1[:, :], O1, F1[:, :], ALU.mult)
                S.tt(F2[:, :], O2, F2[:, :], ALU.mult)
                S.stt(F3[:, :], F2[:, :], NLAM[:, 0:1], F1[:, :], ALU.mult, ALU.add)
                S.act(SQB[:, :], F3[:, :], AF.Square)
                PN = bankn(0)
                S.mm(PN, ONES[:, :], SQB[:, :])
                S.act(F4[:, :], PN, AF.Sqrt, bias=EPSC[:, 0:1], scale=1.0 / 128)
                S.recip(F4[:, :], F4[:, :])
                S.stt(Q[:, h, qs], F3[:, :], GS[:, 0:1], F4[:, :], ALU.mult, ALU.mult)
        dbg("ATT", Q[:, :, :])
        A.release(m_att)
        if stop == "ATT":
            A.release(m_layer)
            return

        m2 = A.mark()
        EA = A.alloc([128, 4, 2, 16], F32)
        S.dma(EA[:, :, :, :].rearrange("p r a b -> p r (a b)"), agEout.ap().rearrange("(r p) n -> p r n", p=128))
        ATR, ATI, SCR, SCI, SNR, SNI, SINR, SINI, c1, c2 = [A.alloc([128, 16], F32) for _ in range(10)]
        CRE = A.alloc([128, 512], F32)
        CIM = A.alloc([128, 512], F32)
        KD = A.alloc([128, 16, 128], BF16)
        S.dma(CRE[:, :], lpb_d.ap()[li, :, 1024:1536])
        S.dma(CIM[:, :], lpb_d.ap()[li, :, 1536:2048])
        S.memset(KD[:, :, :], 0.0, eng="pool")
        S.tt(ATR[:, :], MT[:, :], CT128[:, :], ALU.mult)
        S.tt(ATI[:, :], MT[:, :], ST128[:, :], ALU.mult)
        S.copy(SCR[:, :], EA[:, 0, 0, :])
        S.copy(SCI[:, :], EA[:, 0, 1, :])
        S.ts(SINR[:, :], SCR[:, :], cs("ohq", 1, 2), None, ALU.mult)
        S.ts(SINI[:, :], SCI[:, :], cs("ohq", 1, 2), None, ALU.mult)
        for q in (1, 2):
            S.tt(c1[:, :], ATR[:, :], SCR[:, :], ALU.mult)
            S.tt(c2[:, :], ATI[:, :], SCI[:, :], ALU.mult)
            S.tt(SNR[:, :], c1[:, :], c2[:, :], ALU.subtract)
            S.tt(SNR[:, :], SNR[:, :], EA[:, q, 0, :], ALU.add)
            S.tt(c1[:, :], ATR[:, :], SCI[:, :], ALU.mult)
            S.tt(c2[:, :], ATI[:, :], SCR[:, :], ALU.mult)
            S.tt(SNI[:, :], c1[:, :], c2[:, :], ALU.add)
            S.tt(SNI[:, :], SNI[:, :], EA[:, q, 1, :], ALU.add)
            S.copy(SCR[:, :], SNR[:, :])
            S.copy(SCI[:, :], SNI[:, :])
            S.stt(SINR[:, :], SCR[:, :], cs("ohq", q + 1, q + 2), SINR[:, :], ALU.mult, ALU.add)
            S.stt(SINI[:, :], SCI[:, :], cs("ohq", q + 1, q + 2), SINI[:, :], ALU.mult, ALU.add)
        for ct in range(4):
            m = A.mark()
            WE = A.alloc([128, 2, 16, 128], BF16)
            CAE = A.alloc([128, 2, 16, 128], BF16)
            ta = A.alloc([128, 1024], F32)
            tb = A.alloc([128, 1024], F32)
            CT = A.alloc([128, 4, 129], F32)
            ST = A.alloc([128, 4, 129], F32)
            TI = S.sb([128, 4, 129], I32, S.tinfo[tb.name][1])
            WB = A.alloc([128, 2, 4, 129], F32)
            SS = A.alloc([128, 2, 4, NB], BF16)
            YF = tb
            make_WE(ct, WE, ta, tb)
            pair_tables(ct, CT, ST, ta[:, 0:516].rearrange("p (i b) -> p i b", i=4), TI)
            for half in range(2):
                js = slice(8 * half + 1, 8 * half + 9)
                jo = slice(8 * half, 8 * half + 8)
                arb = AR[:, js, 4 * ct:4 * ct + 4].unsqueeze(3).broadcast_to([128, 8, 4, 32])
                aib = AI[:, js, 4 * ct:4 * ct + 4].unsqueeze(3).broadcast_to([128, 8, 4, 32])
                naib = NAI[:, js, 4 * ct:4 * ct + 4].unsqueeze(3).broadcast_to([128, 8, 4, 32])
                crb = CRE[:, 128 * ct:128 * ct + 128].rearrange("p (i c) -> p i c", i=4).unsqueeze(1).broadcast_to([128, 8, 4, 32])
                cib = CIM[:, 128 * ct:128 * ct + 128].rearrange("p (i c) -> p i c", i=4).unsqueeze(1).broadcast_to([128, 8, 4, 32])
                v4 = lambda t: t[:, :].rearrange("p (a i c) -> p a i c", a=8, i=4)
                S.tt(v4(ta), arb, crb, ALU.mult)
                S.tt(v4(tb), aib, cib, ALU.mult)
                S.tt(CAE[:, 0, jo, :].rearrange("p a n -> p (a n)"), ta[:, :], tb[:, :], ALU.subtract, eng="pool")
                S.tt(v4(ta), naib, crb, ALU.mult)
                S.tt(v4(tb), arb, cib, ALU.mult)
                S.tt(CAE[:, 1, jo, :].rearrange("p a n -> p (a n)"), ta[:, :], tb[:, :], ALU.subtract, eng="pool")
            PK = bank()
            for tau in range(R):
                for i in range(4):
                    k = 4 * ct + i
                    for x in range(2):
                        S.mm(PK[32 * i:32 * i + 32, 32 * tau:32 * tau + 32], WE[:, x, tau, 32 * i:32 * i + 32],
                             CB[:, x, 32 * k:32 * k + 32], start=(x == 0), stop=(x == 1), tile_position=(0, 32 * i))
            for i in range(4):
                S.act(KD[32 * i:32 * i + 32, :, 32 * i:32 * i + 32],
                      PK[32 * i:32 * i + 32, :].rearrange("p (t c) -> p t c", t=R), AF.Copy)
            for i in range(4):
                k = 4 * ct + i
                S.copy(WB[:, 0, i, 0:1], SINR[:, k:k + 1])
                S.copy(WB[:, 1, i, 0:1], SINI[:, k:k + 1])
                S.scan(WB[:, 0, i, 1:129], RHO[:, k:k + 1].broadcast_to([128, NB]), XRE[:, k, :], SINR[:, k:k + 1])
                S.scan(WB[:, 1, i, 1:129], RHO[:, k:k + 1].broadcast_to([128, NB]), XIM[:, k, :], SINI[:, k:k + 1])
            cc = CT[:, :, 0:128]
            ss = ST[:, :, 0:128]
            v3b = lambda t: t[:, 0:512].rearrange("p (i b) -> p i b", i=4)
            S.tt(v3b(ta), WB[:, 0, :, 0:128], cc, ALU.mult)
            S.tt(v3b(tb), WB[:, 1, :, 0:128], ss, ALU.mult)
            S.tt(SS[:, 0, :, :], v3b(ta), v3b(tb), ALU.subtract, eng="pool")
            S.tt(v3b(ta), WB[:, 1, :, 0:128], cc, ALU.mult)
            S.tt(v3b(tb), WB[:, 0, :, 0:128], ss, ALU.mult)
            S.tt(SS[:, 1, :, :], v3b(ta), v3b(tb), ALU.add, eng="pool")
            yb = 4 * (ct % 2)
            Y = PS[:, 512 * yb:512 * (yb + 4)].rearrange("p (j b) -> p j b", j=R)
            Uv = U[:, ct, :].rearrange("p (b j) -> p j b", j=R)
            for j in range(R):
                for j2 in range(j + 1):
                    S.mm(Y[:, j, :], KD[:, j - j2, :], Uv[:, j2, :], start=(j2 == 0), stop=False)
                for i in range(4):
                    for x in range(2):
                        S.mm(Y[32 * i:32 * i + 32, j, :], CAE[:, x, j, 32 * i:32 * i + 32], SS[:, x, i, :],
                             start=False, stop=(x == 1), tile_position=(0, 32 * i))
            for hb in range(2):
                bs = slice(64 * hb, 64 * hb + 64)
                ts_ = slice(1024 * hb, 1024 * hb + 1024)
                S.stt(YF[:, :].rearrange("p (b j) -> p j b", j=R), Uv[:, :, bs], lpc("dvec", ct, ct + 1), Y[:, :, bs],
                      ALU.mult, ALU.add)
                if "YSSM" in debug:
                    if "_y" not in dbg_out:
                        dbg_out["_y"] = S.dram("dbg_YSSM", [128, 4, T], F32, kind="ExternalOutput")
                    S.dma(dbg_out["_y"].ap()[:, ct, ts_], YF[:, :])
                S.act(ta[:, :], YF[:, :], AF.Square)
                S.ts(ta[:, :], ta[:, :], 0.044715, 1.0, ALU.mult, ALU.add)
                S.tt(ta[:, :], ta[:, :], YF[:, :], ALU.mult)
                S.act(ta[:, :], ta[:, :], AF.Sigmoid, scale=2.0 * math.sqrt(2.0 / math.pi))
                S.tt(U[:, ct, ts_], YF[:, :], ta[:, :], ALU.mult, eng="pool")
            A.release(m)
        A.release(m2)
        A.release(m_layer)
        if stop == "S2":
            return

        WG = A.alloc([128, 4, 1024], BF16)
        WO = A.alloc([128, 8, 1024], BF16)
        STG = A.alloc([128, 2048], F32)
        for hf in range(2):
            sv = STG[:, :].rearrange("p (k n) -> p k n", k=4)
            S.dma(sv, w_glu_d.ap()[li, hf])
            S.copy(WG[:, :, 512 * hf:512 * (hf + 1)], sv, eng="pool")
        for g in range(4):
            sv = STG[:, :].rearrange("p (k n) -> p k n", k=8)
            S.dma(sv, w_out_d.ap()[li, g])
            S.copy(WO[:, :, 256 * g:256 * (g + 1)], sv, eng="pool")
        GL = A.alloc([128, 4, 512], F32)
        SG = A.alloc([128, 512], F32)
        SQ = [A.alloc([128, 512], BF16) for _ in range(2)]
        RS = A.alloc([128, 512], F32)
        for tt in range(NT):
            tok = slice(512 * tt, 512 * (tt + 1))
            for oc in range(4):
                PA, PB = bank(), bank()
                for kc in range(4):
                    S.mm(PB, WG[:, kc, 512 + 128 * oc:512 + 128 * (oc + 1)], U[:, kc, tok], start=(kc == 0), stop=(kc == 3))
                for kc in range(4):
                    S.mm(PA, WG[:, kc, 128 * oc:128 * (oc + 1)], U[:, kc, tok], start=(kc == 0), stop=(kc == 3))
                S.act(SG[:, :], PB, AF.Sigmoid)
                S.tt(GL[:, oc, :], PA, SG[:, :], ALU.mult)
            rmsnorm_tile(lambda oc: U[:, oc, tok], lambda oc: GL[:, oc, :], 4, 512,
                         lambda oc: lpc("gssm", oc, oc + 1), 512, [SQ[0][:, :], SQ[1][:, :]], RS[:, :])
        dbg("SSM", U[:, :, :])
        for tt in range(NT):
            tok = slice(512 * tt, 512 * (tt + 1))
            for dc in range(8):
                P = bank()
                for kc in range(8):
                    src = U[:, kc, tok] if kc < 4 else Q[:, kc - 4, tok]
                    S.mm(P, WO[:, kc, 128 * dc:128 * (dc + 1)], src, start=(kc == 0), stop=(kc == 7))
                S.tt(X[:, dc, tok], X[:, dc, tok], P, ALU.add)
        dbg("XMID", X[:, :, :])
        A.release(m_layer)
        if stop == "OUT":
            return

        S.dma(ag2in.ap().rearrange("p (c t) -> p c t", c=8), X[:, :, T - 2:T])
        S.allgather(ag2out.ap().opt(), ag2in.ap().opt())
        H4 = A.alloc([128, 4, 16], F32)
        XH = A.alloc([128, 16], F32)
        HH = A.alloc([128, 8, 2], BF16)
        HSQ = A.alloc([128, 16], BF16)
        HRS = A.alloc([128, 2], F32)
        HTMP = A.alloc([128, 16], F32)
        CARRY = [A.alloc([128, 44, 2], F32) for _ in range(5)]
        S.dma(H4[:, :, :], ag2out.ap().rearrange("(r p) n -> p r n", p=128))
        S.ts(XH[:, :], H4[:, 0, :], cs("ohprev", 0, 1), None, ALU.mult)
        for r in range(1, 4):
            S.stt(XH[:, :], H4[:, r, :], cs("ohprev", r, r + 1), XH[:, :], ALU.mult, ALU.add)
        XHv = XH[:, :].rearrange("p (c t) -> p c t", c=8)
        S.act(HSQ[:, :], XH[:, :], AF.Square)
        PHn = bank()
        for dc in range(8):
            S.mm(PHn[:, 0:2], ONES[:, :], HSQ[:, 2 * dc:2 * dc + 2], start=(dc == 0), stop=(dc == 7))
        S.act(HRS[:, :], PHn[:, 0:2], AF.Sqrt, bias=EPSC[:, 0:1], scale=1.0 / D)
        S.recip(HRS[:, :], HRS[:, :])
        S.tt(HTMP[:, :].rearrange("p (c t) -> p c t", c=8), XHv, HRS[:, :].unsqueeze(1).broadcast_to([128, 8, 2]), ALU.mult)
        S.tt(HH[:, :, :], HTMP[:, :].rearrange("p (c t) -> p c t", c=8),
             lpc("gffn").unsqueeze(2).broadcast_to([128, 8, 2]), ALU.mult)
        A2.release(A2.lo)
        HTF = A2.alloc([128, 8, 1024], BF16)
        ACTT = A2.alloc([128, NFC, 1024], BF16)
        SQ = [A.alloc([128, 512], BF16) for _ in range(2)]
        RS = A.alloc([128, 512], F32)
        STU = [A.alloc([128, 8, 256], F32) for _ in range(2)]
        WUB = [A.alloc([128, 8, 256], BF16) for _ in range(2)]
        STD = [A.alloc([128, 11, 128], F32) for _ in range(2)]
        WDB = [A.alloc([128, NFC, 128], BF16) for _ in range(2)]
        ACC = [A.alloc([128, 512], F32) for _ in range(2)]
        SGT = A.alloc([128, 512], F32)
        wctr = 0
        for hf in range(2):
            for t2 in range(2):
                tok = slice(1024 * hf + 512 * t2, 1024 * hf + 512 * (t2 + 1))
                rmsnorm_tile(lambda dc: HTF[:, dc, 512 * t2:512 * (t2 + 1)], lambda dc: X[:, dc, tok], 8, 512,
                             lambda dc: lpc("gffn", dc, dc + 1), D, [SQ[0][:, :], SQ[1][:, :]], RS[:, :])
            for fc in range(NFC):
                b = wctr % 2
                wctr += 1
                S.dma(STU[b][:, :, :], w_up_d.ap()[li, fc])
                S.copy(WUB[b][:, :, :], STU[b][:, :, :], eng="pool")
                for gv in range(2):
                    c = gv * NFC + fc
                    if hf == 0:
                        PH = bank()
                        for dc in range(8):
                            S.mm(PH[:, 0:2], WUB[b][:, dc, 128 * gv:128 * (gv + 1)], HH[:, dc, :], start=(dc == 0), stop=(dc == 7))
                        S.act(CARRY[0][:, c, :], PH[:, 0:2], AF.Copy)
                    for t2 in range(2):
                        gt = 2 * hf + t2
                        P = bank()
                        for dc in range(8):
                            S.mm(P, WUB[b][:, dc, 128 * gv:128 * (gv + 1)], HTF[:, dc, 512 * t2:512 * (t2 + 1)],
                                 start=(dc == 0), stop=(dc == 7))
                        acc = ACC[gv] if t2 == 0 else ACC[gv]
                        w0 = lpc("wconv", c, c + 1)
                        w1 = lpc("wconv", 44 + c, 44 + c + 1)
                        w2 = lpc("wconv", 88 + c, 88 + c + 1)
                        if t2 == 1:
                            pass
                        S.act(acc[:, :], P, AF.Identity, bias=lpc("bconv", c, c + 1), scale=w2)
                        S.stt(acc[:, 1:512], P[:, 0:511], w1, acc[:, 1:512], ALU.mult, ALU.add)
                        S.stt(acc[:, 2:512], P[:, 0:510], w0, acc[:, 2:512], ALU.mult, ALU.add)
                        S.stt(acc[:, 0:2], CARRY[gt][:, c, 0:2], w0, acc[:, 0:2], ALU.mult, ALU.add)
                        S.stt(acc[:, 0:1], CARRY[gt][:, c, 1:2], w1, acc[:, 0:1], ALU.mult, ALU.add)
                        S.act(CARRY[gt + 1][:, c, :], P[:, 510:512], AF.Copy)
                        if gv == 0:
                            S.act(ACTT[:, fc, 512 * t2:512 * (t2 + 1)], acc[:, :], AF.Silu)
                        else:
                            S.tt(ACTT[:, fc, 512 * t2:512 * (t2 + 1)], ACTT[:, fc, 512 * t2:512 * (t2 + 1)], acc[:, :], ALU.mult)
            for dc in range(8):
                b = dc % 2
                for h2 in range(2):
                    S.dma(STD[h2][:, :, :], w_dn_d.ap()[li, dc, h2])
                    S.copy(WDB[b][:, 11 * h2:11 * h2 + 11, :], STD[h2][:, :, :], eng="pool")
                for t2 in range(2):
                    tok = slice(1024 * hf + 512 * t2, 1024 * hf + 512 * (t2 + 1))
                    P = bank()
                    for fc in range(NFC):
                        S.mm(P, WDB[b][:, fc, :], ACTT[:, fc, 512 * t2:512 * (t2 + 1)], start=(fc == 0), stop=(fc == NFC - 1))
                    S.tt(X[:, dc, tok], X[:, dc, tok], P, ALU.add)
        dbg("XOUT", X[:, :, :])
        A.release(m_layer)

    for li in range(L):
        layer(li)

    if final:
        OT = A.alloc([128, 8, 512], F32)
        SQ = [A.alloc([128, 512], BF16) for _ in range(2)]
        RS = A.alloc([128, 512], F32)
        for tt in range(NT):
            tok = slice(512 * tt, 512 * (tt + 1))
            rmsnorm_tile(lambda dc: OT[:, dc, :], lambda dc: X[:, dc, tok], 8, 512,
                         lambda dc: lpc("gfin", dc, dc + 1), D, [SQ[0][:, :], SQ[1][:, :]], RS[:, :])
            S.dma(out_d.ap()[:, :, tok], OT[:, :, :])
    else:
        for q in range(4):
            S.dma(out_d.ap()[:, 2 * q:2 * q + 2, :], X[:, 2 * q:2 * q + 2, :])
    S.finish()
    S.build()
    return nc


def _consts(core):
    qi = core % 4
    c = np.zeros((128, NCS), np.float32)
    inv_freq = (500000.0 ** (-(np.arange(0, 16, 2, dtype=np.float32) / 16.0))).astype(np.float32)
    for p in range(128):
        d = p % 64
        if d < 16:
            c[p, CS["invf"][0]] = inv_freq[d % 8]
            c[p, CS["sgn"][0]] = -1.0 if d < 8 else 1.0
    c[:, CS["tau"][0]:CS["tau"][0] + 17] = np.arange(17, dtype=np.float32)[None]
    c[:, CS["bv"][0]:CS["bv"][0] + 129] = (R * np.arange(129, dtype=np.float32))[None]
    for r in range(3):
        c[:, CS["rankbias"][0] + r] = 0.0 if r < qi else -30000.0
    c[:, CS["ohq"][0] + qi] = 1.0
    if qi > 0:
        c[:, CS["ohprev"][0] + qi - 1] = 1.0
    return c


def _cbf():
    c = np.zeros((128, 256 + 2048), np.float32)
    c[:, 0:128] = np.eye(128, dtype=np.float32)
    for m in range(128):
        d = m % 64
        if d < 8:
            c[m + 8, 128 + m] = 1.0
        elif d < 16:
            c[m - 8, 128 + m] = 1.0
    kk = np.arange(128)[:, None]
    qq = np.arange(512)[None, :]
    for i in range(4):
        c[:, 256 + 512 * i:256 + 512 * (i + 1)] = ((128 * i + kk) // 64 <= qq // 64).astype(np.float32)
    return c.astype(ml_dtypes.bfloat16)


def _layer_params(inp, l):
    lp = np.zeros((128, NLP), np.float32)

    def put(name, arr):
        o, w = LP[name]
        lp[:, o:o + w] = arr

    put("gmix", inp["norm_mix"][l].reshape(8, 128).T)
    put("gffn", inp["norm_ffn"][l].reshape(8, 128).T)
    put("gssm", inp["ssm_norm"][l].reshape(4, 128).T)
    put("gsub", inp["attn_subln"][l].reshape(128, 1))
    put("dvec", inp["ssm_d"][l].reshape(4, 128).T)
    pl = lambda a: a.reshape(16, 2, 64).transpose(1, 2, 0).reshape(128, 16)
    put("lamre", pl(inp["ssm_lambda_re"][l]))
    put("lamim", pl(inp["ssm_lambda_im"][l]))
    put("logstep", pl(np.repeat(inp["ssm_log_step"][l][:, None], 64, axis=1)))
    put("wconv", inp["w_conv"][l].reshape(3, 44, 128).transpose(2, 0, 1).reshape(128, 132))
    put("bconv", inp["b_conv"][l].reshape(44, 128).T)
    lam = np.concatenate([inp["lambda_q1"][l], inp["lambda_k1"][l], inp["lambda_q2"][l], inp["lambda_k2"][l]])
    put("lamqk", np.repeat(lam[None, :], 128, axis=0))
    put("gfin", inp["norm_final"].reshape(8, 128).T)
    lam_init = 0.8 - 0.6 * math.exp(-0.3 * l)
    put("nlaminit", np.full((128, 1), -lam_init, np.float32))
    put("oml", np.full((128, 1), 1.0 - lam_init, np.float32))
    lpb = np.zeros((128, 4, 16, 2, 16), np.float32)
    b_re = inp["ssm_b_re"][l].reshape(16, 2, 64, 16)
    b_im = inp["ssm_b_im"][l].reshape(16, 2, 64, 16)
    c_re = inp["ssm_c_re"][l].reshape(16, 2, 16, 64)
    c_im = inp["ssm_c_im"][l].reshape(16, 2, 16, 64)
    for g2 in range(2):
        rows = slice(64 * g2, 64 * g2 + 64)
        lpb[rows, 0, :, g2, :] = b_re[:, g2].transpose(1, 0, 2)
        lpb[rows, 1, :, g2, :] = b_im[:, g2].transpose(1, 0, 2)
        lpb[rows, 2, :, g2, :] = c_re[:, g2].transpose(2, 0, 1)
        lpb[rows, 3, :, g2, :] = c_im[:, g2].transpose(2, 0, 1)
    return lp, lpb.reshape(128, 2048)


def _layer_weights(inp, l):
    w_in = inp["w_in"][l].reshape(8, 128, 8, 256).transpose(2, 1, 0, 3)
    w_glu = inp["ssm_w_glu"][l].reshape(4, 128, 2, 512).transpose(2, 1, 0, 3)
    w_out = inp["w_out"][l].reshape(8, 128, 4, 256).transpose(2, 1, 0, 3)
    w_up = inp["w_up"][l].reshape(8, 128, 2, NFC, 128).transpose(3, 1, 0, 2, 4).reshape(NFC, 128, 8, 256)
    w_dn = inp["w_down"][l].reshape(2, 11, 128, 8, 128).transpose(3, 0, 2, 1, 4)
    return {k: np.ascontiguousarray(v, dtype=np.float32) for k, v in
            (("w_in", w_in), ("w_glu", w_glu), ("w_out", w_out), ("w_up", w_up), ("w_dn", w_dn))}


_PROG = {}


def _get_prog(L, final):
    key = (L, final)
    if key not in _PROG:
        _PROG[key] = build_program(L, final)
    return _PROG[key]


FUSED = True


def kernel(**inp):
    inp = {k: np.asarray(v) for k, v in inp.items()}
    x = inp["x"].astype(np.float32)
    xs = []
    for c in range(8):
        b, qi = c // 4, c % 4
        xs.append(np.ascontiguousarray(x[b, qi * T:(qi + 1) * T, :].T.reshape(8, 128, T).transpose(1, 0, 2)))
    poss = [np.ascontiguousarray(inp["positions"][c // 4, (c % 4) * T:(c % 4 + 1) * T].astype(np.int32)[None]) for c in range(8)]
    csts = [_consts(c) for c in range(8)]
    cbf = _cbf()
    lps = [_layer_params(inp, l) for l in range(DEPTH)]
    wts = [_layer_weights(inp, l) for l in range(DEPTH)]

    def stack(ls):
        maps = {"lp": np.stack([lps[l][0] for l in ls]), "lpb": np.stack([lps[l][1] for l in ls])}
        for k in ("w_in", "w_glu", "w_out", "w_up", "w_dn"):
            maps[k] = np.stack([wts[l][k] for l in ls])
        return maps

    if FUSED:
        launches = [(list(range(DEPTH)), True)]
    else:
        launches = [([l], l == DEPTH - 1) for l in range(DEPTH)]
    for ls, final in launches:
        nc = _get_prog(len(ls), final)
        shared = stack(ls)
        in_maps = []
        for c in range(8):
            m = {"x_in": xs[c], "pos": poss[c], "cst": csts[c], "cbf": cbf}
            m.update(shared)
            in_maps.append(m)
        res = run_bass_kernel_spmd(nc, in_maps, core_ids=list(range(8)))
        xs = [np.asarray(r["out"], dtype=np.float32) for r in res.results]
    out = np.zeros((2, 8192, D), np.float32)
    for c in range(8):
        b, qi = c // 4, c % 4
        out[b, qi * T:(qi + 1) * T, :] = xs[c].transpose(1, 0, 2).reshape(D, T).T
    return out
```

```python
import math
from contextlib import ExitStack

import numpy as np
import ml_dtypes
import concourse.bass as bass
import concourse.mybir as mybir
from concourse.bass_utils import run_bass_kernel_spmd

F32 = mybir.dt.float32
BF16 = mybir.dt.bfloat16
I32 = mybir.dt.int32
ALU = mybir.AluOpType
AF = mybir.ActivationFunctionType
_ESZ = {F32: 4, BF16: 2, I32: 4}

D = 1024
T = 2048
NT = 4
DEPTH = 4
FFN = 2816
NFC = 22
R = 16
NB = T // R
EPS = 1e-6
TWO_PI = 2.0 * math.pi
C1 = 6.28125
C2 = TWO_PI - C1
SB_LO = 16512
SB_HI = 229344
GROUPS = [[0, 1, 2, 3], [4, 5, 6, 7]]


class Sched:
    ENGS = ("pe", "act", "dve", "pool", "sp")
    BK = 2048

    def __init__(self, nc, n_dma_sems=32):
        self.nc = nc
        self.ops = {e: [] for e in self.ENGS}
        self.tinfo = {}
        self.recs = {}
        self.buckets = {}
        self.known = {e: {} for e in self.ENGS}
        self.known_dma = {e: {} for e in self.ENGS}
        self.snap = {}
        self.targets = {e: set() for e in self.ENGS}
        self.n_dma = 0
        self.n_dma_sems = n_dma_sems
        self.n_cc = 0
        self.uid = 0

    def sb(self, shape, dtype, offset):
        self.uid += 1
        h = self.nc.alloc_sbuf_tensor_at("t%d" % self.uid, list(shape), dtype, offset=offset)
        self.tinfo[h.name] = ("sb", offset, int(np.prod(shape[1:])) * _ESZ[dtype])
        return h

    def ps(self, name, shape, dtype=F32):
        h = self.nc.alloc_psum_tensor(name, list(shape), dtype)
        self.tinfo[h.name] = ("ps", 0, int(np.prod(shape[1:])) * _ESZ[dtype])
        return h

    def dram(self, name, shape, dtype, kind="Internal"):
        h = self.nc.dram_tensor(name, list(shape), dtype, kind=kind)
        self.tinfo[h.name] = ("dr:" + name, 0, None)
        return h

    def region(self, ap):
        space, base, psb = self.tinfo[ap.tensor.name]
        esz = _ESZ[ap.dtype]
        aps = ap.ap
        off = int(ap.offset) * esz
        if psb is None:
            span = sum((c - 1) * abs(s) for s, c in aps) * esz
            return (space, off, off + span + esz, 0, 1)
        p0 = off // psb
        fo = off % psb
        span = sum((c - 1) * abs(s) for s, c in aps[1:]) * esz
        pstep, pcnt = aps[0]
        nstep = max(1, (pstep * esz) // psb) if pstep else 1
        return (space, base + fo, base + fo + span + esz, p0, p0 + (pcnt - 1) * nstep + 1)

    def _bk(self, rg):
        if rg[0][0] == "d":
            return range(0, 1)
        return range(rg[1] // self.BK, (rg[2] - 1) // self.BK + 1)

    def add(self, eng, emit, reads=(), writes=(), kind="cmp"):
        rr = [self.region(a) for a in reads if a is not None and hasattr(a, "tensor")]
        ww = [self.region(a) for a in writes if a is not None]
        pa = [(g[0], g[1] // 2048 * 2048, ((g[2] - 1) // 2048 + 1) * 2048, 0, 128) for g in rr + ww if g[0] == "ps"]
        rr = [g for g in rr if g[0] != "ps"]
        ww = [g for g in ww if g[0] != "ps"] + pa
        seq = len(self.ops[eng])
        op = {"eng": eng, "emit": emit, "kind": kind, "seq": seq, "waits": [], "dmawaits": [], "sem": None}
        if kind == "dma":
            i = self.n_dma
            self.n_dma += 1
            P = self.n_dma_sems
            op["sem"] = ("d", i % P, 16 * (i // P + 1))
            if i >= P:
                self._need_dma(op, ("d", i % P, 16 * (i // P)))
        elif kind == "cc":
            i = self.n_cc
            self.n_cc += 1
            op["sem"] = ("cc", i, 1)
        wid = ("E", eng, seq) if kind == "cmp" else ("D",) + op["sem"]
        for rg in rr:
            for key in self._overlaps(rg):
                w = self.recs[key][0]
                if w is not None:
                    self._need(op, w)
        for rg in ww:
            for key in list(self._overlaps(rg)):
                rec = self.recs[key]
                if rec[0] is not None:
                    self._need(op, rec[0])
                for e2, s2 in rec[1].items():
                    self._need(op, ("E", e2, s2))
                for d in rec[2]:
                    self._need(op, d)
                if key[1] >= rg[1] and key[2] <= rg[2] and key[3] >= rg[3] and key[4] <= rg[4]:
                    self._del(key)
        for rg in rr:
            rec = self._get(rg)
            if kind == "cmp":
                rec[1][eng] = seq
            else:
                rec[2].append(wid)
        for rg in ww:
            rec = self._get(rg)
            rec[0] = wid
            rec[1] = {}
            rec[2] = []
        if kind == "cmp":
            self.snap[(eng, seq)] = dict(self.known[eng])
        self.ops[eng].append(op)
        return op

    def _get(self, rg):
        rec = self.recs.get(rg)
        if rec is None:
            rec = [None, {}, []]
            self.recs[rg] = rec
            for b in self._bk(rg):
                self.buckets.setdefault((rg[0], b), set()).add(rg)
        return rec

    def _del(self, key):
        del self.recs[key]
        for b in self._bk(key):
            self.buckets[(key[0], b)].discard(key)

    def _overlaps(self, rg):
        out = set()
        for b in self._bk(rg):
            for key in self.buckets.get((rg[0], b), ()):
                if key[1] < rg[2] and rg[1] < key[2] and key[3] < rg[4] and rg[3] < key[4]:
                    out.add(key)
        return out

    def _need(self, op, w):
        if w[0] == "E":
            self._need_eng(op, w[1], w[2])
        else:
            self._need_dma(op, w[1:])

    def _need_eng(self, op, f, s):
        e = op["eng"]
        if e == f and e == "pe":
            return
        if e == f and s >= op["seq"]:
            return
        kn = self.known[e]
        if kn.get(f, -1) >= s:
            return
        kn[f] = s
        for f2, s2 in self.snap.get((f, s), {}).items():
            if f2 != e and kn.get(f2, -1) < s2:
                kn[f2] = s2
        op["waits"].append((f, s))
        self.targets[f].add(s)

    def _need_dma(self, op, d):
        kd = self.known_dma[op["eng"]]
        key = (d[0], d[1])
        if kd.get(key, 0) >= d[2]:
            return
        kd[key] = d[2]
        op["dmawaits"].append(d)

    def finish(self):
        op = {"eng": "sp", "emit": lambda e: e.nop(), "kind": "cmp", "seq": len(self.ops["sp"]),
              "waits": [], "dmawaits": [], "sem": None}
        for e in self.ENGS:
            if e != "sp" and self.ops[e]:
                self._need_eng(op, e, len(self.ops[e]) - 1)
        P = self.n_dma_sems
        for i in range(max(0, self.n_dma - P), self.n_dma):
            self._need_dma(op, ("d", i % P, 16 * (i // P + 1)))
        for i in range(self.n_cc):
            self._need_dma(op, ("cc", i, 1))
        self.ops["sp"].append(op)

    def build(self):
        nc = self.nc
        with ExitStack() as es:
            sem_e = {e: es.enter_context(nc.semaphore("s_" + e)) for e in self.ENGS}
            sem_d = {}
            for i in range(self.n_dma_sems):
                sem_d[("d", i)] = es.enter_context(nc.semaphore("d%d" % i))
            for i in range(self.n_cc):
                sem_d[("cc", i)] = es.enter_context(nc.semaphore("cc%d" % i))
            rank = {}
            for e in self.ENGS:
                for r, s in enumerate(sorted(self.targets[e])):
                    rank[(e, s)] = r + 1
            block = es.enter_context(nc.Block())
            handles = {"pe": block.tensor, "act": block.scalar, "dve": block.vector,
                       "pool": block.gpsimd, "sp": block.sync}
            for e in self.ENGS:
                ops = self.ops[e]
                if not ops:
                    continue

                def run(eng, ops=ops, e=e):
                    for op in ops:
                        for f, s in op["waits"]:
                            eng.wait_ge(sem_e[f], rank[(f, s)])
                        for d in op["dmawaits"]:
                            eng.wait_ge(sem_d[(d[0], d[1])], d[2])
                        ins = op["emit"](eng)
                        if op["kind"] == "dma":
                            ins.then_inc(sem_d[(op["sem"][0], op["sem"][1])], 16)
                        elif op["kind"] == "cc":
                            ins.then_inc(sem_d[(op["sem"][0], op["sem"][1])])
                        elif (e, op["seq"]) in rank:
                            ins.then_inc(sem_e[e], 1)
                handles[e](run)

    def mm(self, out, lhsT, rhs, start=True, stop=True, **kw):
        return self.add("pe", lambda e: e.matmul(out, lhsT, rhs, start=start, stop=stop, **kw), [lhsT, rhs], [out])

    def act(self, out, in_, func, bias=0.0, scale=1.0):
        return self.add("act", lambda e: e.activation(out, in_, func, bias=bias, scale=scale), [in_, bias, scale], [out])

    def tt(self, out, in0, in1, op, eng="dve"):
        return self.add(eng, lambda e: e.tensor_tensor(out, in0, in1, op), [in0, in1], [out])

    def ts(self, out, in0, s1, s2=None, op0=ALU.mult, op1=None, eng="dve"):
        if op1 is None:
            return self.add(eng, lambda e: e.tensor_scalar(out, in0, s1, None, op0), [in0, s1], [out])
        return self.add(eng, lambda e: e.tensor_scalar(out, in0, s1, s2, op0, op1), [in0, s1, s2], [out])

    def stt(self, out, in0, scalar, in1, op0, op1, eng="dve"):
        return self.add(eng, lambda e: e.scalar_tensor_tensor(out, in0, scalar, in1, op0, op1), [in0, scalar, in1], [out])

    def copy(self, out, in_, eng="dve"):
        return self.add(eng, lambda e: e.tensor_copy(out, in_), [in_], [out])

    def memset(self, out, val, eng="dve"):
        return self.add(eng, lambda e: e.memset(out, val), [], [out])

    def recip(self, out, in_):
        return self.add("dve", lambda e: e.reciprocal(out, in_), [in_], [out])

    def rsum(self, out, in_):
        return self.add("dve", lambda e: e.reduce_sum(out, in_, mybir.AxisListType.X), [in_], [out])

    def scan(self, out, d0, d1, init):
        return self.add("dve", lambda e: e.tensor_tensor_scan(out, d0, d1, init, ALU.mult, ALU.add), [d0, d1, init], [out])

    def dma(self, out, in_):
        return self.add("sp", lambda e: e.dma_start(out=out, in_=in_), [in_], [out], kind="dma")

    def allgather(self, out, in_):
        return self.add("pool", lambda e: e.collective_compute(
            "AllGather", ALU.bypass, replica_groups=GROUPS, ins=[in_], outs=[out]), [in_], [out], kind="cc")


class Arena:
    def __init__(self, S, lo, hi):
        self.S, self.lo, self.hi, self.top = S, lo, hi, lo

    def alloc(self, shape, dt):
        nb = int(np.prod(shape[1:])) * _ESZ[dt]
        off = (self.top + 63) // 64 * 64
        assert off + nb <= self.hi, ("SBUF overflow", shape, off + nb - self.hi)
        self.top = off + nb
        return self.S.sb(shape, dt, off)

    def mark(self):
        return self.top

    def release(self, m):
        self.top = m


LP = {}
_o = 0
for _n, _w in (("gmix", 8), ("gffn", 8), ("gssm", 4), ("gsub", 1), ("dvec", 4), ("lamre", 16), ("lamim", 16),
               ("logstep", 16), ("wconv", 132), ("bconv", 44), ("lamqk", 256), ("gfin", 8), ("nlaminit", 1), ("oml", 1)):
    LP[_n] = (_o, _w)
    _o += _w
NLP = _o
CS = {}
_o = 0
for _n, _w in (("invf", 1), ("sgn", 1), ("tau", 17), ("bv", 129), ("rankbias", 3), ("ohq", 4), ("ohprev", 4)):
    CS[_n] = (_o, _w)
    _o += _w
NCS = _o
AGW = 16384


def build_program(L, final, debug=(), stop=None):
    nc = bass.Bass("TRN2", target_bir_lowering=False)
    S = Sched(nc)
    A = Arena(S, SB_LO, SB_HI)
    dbg_out = {}

    x_in = S.dram("x_in", [128, 8, T], F32, kind="ExternalInput")
    pos = S.dram("pos", [1, T], I32, kind="ExternalInput")
    cst_d = S.dram("cst", [128, NCS], F32, kind="ExternalInput")
    cbf_d = S.dram("cbf", [128, 256 + 2048], BF16, kind="ExternalInput")
    lp_d = S.dram("lp", [L, 128, NLP], F32, kind="ExternalInput")
    lpb_d = S.dram("lpb", [L, 128, 2048], F32, kind="ExternalInput")
    w_in_d = S.dram("w_in", [L, 8, 128, 8, 256], F32, kind="ExternalInput")
    w_glu_d = S.dram("w_glu", [L, 2, 128, 4, 512], F32, kind="ExternalInput")
    w_out_d = S.dram("w_out", [L, 4, 128, 8, 256], F32, kind="ExternalInput")
    w_up_d = S.dram("w_up", [L, NFC, 128, 8, 256], F32, kind="ExternalInput")
    w_dn_d = S.dram("w_dn", [L, 8, 2, 128, 11, 128], F32, kind="ExternalInput")
    out_d = S.dram("out", [128, 8, T], F32, kind="ExternalOutput")
    agin = [S.dram("agin%d" % h, [128, 4096], BF16) for h in range(4)]
    agout = [S.dram("agout%d" % h, [512, 4096], BF16) for h in range(4)]
    agEin = S.dram("agEin", [128, 32], F32)
    agEout = S.dram("agEout", [512, 32], F32)
    ag2in = S.dram("ag2in", [128, 16], F32)
    ag2out = S.dram("ag2out", [512, 16], F32)

    def dbg(name, ap):
        if name not in debug:
            return
        t = S.dram("dbg_" + name, list(ap.shape), ap.dtype, kind="ExternalOutput")
        S.dma(t.ap(), ap)
        dbg_out[name] = True

    X = A.alloc([128, 8, T], F32)
    uqkv_lo = (A.top + 63) // 64 * 64
    U = A.alloc([128, 4, T], BF16)
    Q = A.alloc([128, 4, T], BF16)
    K = A.alloc([128, 4, T], BF16)
    V = A.alloc([128, 4, 16, 128], BF16)
    A2 = Arena(S, uqkv_lo, A.top)
    CST = A.alloc([128, NCS], F32)
    CBF = A.alloc([128, 256 + 2048], BF16)
    ONES = A.alloc([128, 128], BF16)
    EPSC = A.alloc([128, 1], F32)
    LPT = A.alloc([128, NLP], F32)
    PS = S.ps("psum", [128, 4096], F32)

    def cs(name, a=0, b=None):
        o, w = CS[name]
        return CST[:, o + a:o + (w if b is None else b)]

    def lpc(name, a=0, b=None):
        o, w = LP[name]
        return LPT[:, o + a:o + (w if b is None else b)]

    IDENT = CBF[:, 0:128]
    ROTM = CBF[:, 128:256]

    def MASK(i):
        return CBF[:, 256 + 512 * i:256 + 512 * (i + 1)]

    bank_ctr = [0]

    def bank(n=1):
        b = bank_ctr[0] % 8
        bank_ctr[0] += 1
        return PS[:, 512 * b:512 * (b + 1)]

    def bankn(b):
        return PS[:, 512 * b:512 * (b + 1)]

    S.dma(CST[:, :], cst_d.ap())
    S.dma(CBF[:, :], cbf_d.ap())
    for q in range(4):
        S.dma(X[:, 2 * q:2 * q + 2, :], x_in.ap()[:, 2 * q:2 * q + 2, :])
    S.memset(ONES[:, :], 1.0)
    S.memset(EPSC[:, :], EPS)

    def reduce_angle(rout, ang, tmpf, tmpi):
        S.ts(tmpf, ang, 1.0 / TWO_PI, None, ALU.mult)
        S.copy(tmpi, tmpf)
        S.copy(tmpf, tmpi)
        S.stt(rout, tmpf, -C1, ang, ALU.mult, ALU.add)
        S.stt(rout, tmpf, -C2, rout, ALU.mult, ALU.add)

    def rmsnorm_tile(dst, src_of_dc, ndc, width, gcol, nfeat, sq_tmp, rs, dst_is_list=False):
        P = bank()
        for dc in range(ndc):
            sq = sq_tmp[dc % 2]
            S.act(sq, src_of_dc(dc), AF.Square)
            S.mm(P[:, 0:width], ONES[:, :], sq, start=(dc == 0), stop=(dc == ndc - 1))
        S.act(rs, P[:, 0:width], AF.Sqrt, bias=EPSC[:, 0:1], scale=1.0 / nfeat)
        S.recip(rs, rs)
        for dc in range(ndc):
            S.stt(dst(dc), src_of_dc(dc), gcol(dc), rs, ALU.mult, ALU.mult)

    def layer(li):
        S.dma(LPT[:, :], lp_d.ap()[li])
        m_layer = A.mark()

        WIN = A.alloc([128, 8, 2048], BF16)
        STG = A.alloc([128, 8, 256], F32)
        for g in range(8):
            S.dma(STG[:, :, :], w_in_d.ap()[li, g])
            S.copy(WIN[:, :, 256 * g:256 * (g + 1)], STG[:, :, :], eng="pool")
        HT = A.alloc([128, 8, 512], BF16)
        SQ = [A.alloc([128, 512], BF16) for _ in range(2)]
        RS = A.alloc([128, 512], F32)
        PI = A.alloc([128, 512], I32)
        ANG = A.alloc([128, 512], F32)
        TF = A.alloc([128, 512], F32)
        COSF = A.alloc([128, 512], F32)
        SINF = A.alloc([128, 512], F32)
        QB = A.alloc([128, 512], BF16)
        T1 = A.alloc([128, 512], F32)
        T2 = A.alloc([128, 512], F32)
        PARTS = set(stop.split(':')[1].split('+')) if (stop and ':' in stop) else {'norm', 'rope', 'proj', 'v'}
        for tt in range(NT):
            tok = slice(512 * tt, 512 * (tt + 1))
            rmsnorm_tile(lambda dc: HT[:, dc, :], lambda dc: X[:, dc, tok], 8, 512,
                         lambda dc: lpc("gmix", dc, dc + 1), D, [SQ[0][:, :], SQ[1][:, :]], RS[:, :])
            if 'rope' not in PARTS:
                continue
            S.dma(PI[:, :], pos.ap()[0:1, tok].partition_broadcast(128))
            S.copy(TF[:, :], PI[:, :])
            S.ts(ANG[:, :], TF[:, :], cs("invf"), None, ALU.mult)
            reduce_angle(ANG[:, :], ANG[:, :], TF[:, :], PI[:, :])
            S.act(SINF[:, :], ANG[:, :], AF.Sin, scale=cs("sgn"))
            S.act(COSF[:, :], ANG[:, :], AF.Sin, scale=0.5)
            S.tt(COSF[:, :], COSF[:, :], COSF[:, :], ALU.mult)
            S.ts(COSF[:, :], COSF[:, :], -2.0, 1.0, ALU.mult, ALU.add)
            for fc in range(12 if 'proj' in PARTS else 0):
                P = bank()
                for dc in range(8):
                    S.mm(P, WIN[:, dc, 128 * fc:128 * (fc + 1)], HT[:, dc, :], start=(dc == 0), stop=(dc == 7))
                if fc < 4:
                    S.act(U[:, fc, tok], P, AF.Copy)
                else:
                    dst = Q[:, fc - 4, tok] if fc < 8 else K[:, fc - 8, tok]
                    S.act(QB[:, :], P, AF.Copy)
                    PR = bank()
                    S.mm(PR, ROTM, QB[:, :])
                    S.tt(T1[:, :], P, COSF[:, :], ALU.mult)
                    S.tt(T2[:, :], PR, SINF[:, :], ALU.mult)
                    S.tt(dst, T1[:, :], T2[:, :], ALU.add, eng="pool")
            for s in range(4 if 'v' in PARTS else 0):
                P = bank()
                for dc in range(8):
                    S.mm(P, HT[:, dc, 128 * s:128 * (s + 1)], WIN[:, dc, 1536:2048], start=(dc == 0), stop=(dc == 7))
                S.act(V[:, :, 4 * tt + s, :], P.rearrange("p (h d) -> p h d", h=4), AF.Copy)
        dbg("U", U[:, :, :]); dbg("Q", Q[:, :, :]); dbg("K", K[:, :, :]); dbg("V", V[:, :, :, :])
        A.release(m_layer)
        if stop and (stop == "A" or stop.startswith("A:")):
            return

        for h in range(4):
            S.dma(agin[h].ap()[:, 0:2048], K[:, h, :])
            S.dma(agin[h].ap()[:, 2048:4096], V[:, h, :, :].rearrange("p a b -> p (a b)"))
            S.allgather(agout[h].ap().opt(), agin[h].ap().opt())

        AR = A.alloc([128, 17, 16], F32)
        AI = A.alloc([128, 17, 16], F32)
        NAI = A.alloc([128, 17, 16], F32)
        SLI = A.alloc([128, 16], F32)
        RHO = A.alloc([128, 16], F32)
        MT = A.alloc([128, 16], F32)
        CT128 = A.alloc([128, 16], F32)
        ST128 = A.alloc([128, 16], F32)
        BR = A.alloc([128, 512], F32)
        BI = A.alloc([128, 512], F32)
        CB = A.alloc([128, 2, 512], BF16)
        XRE = A.alloc([128, 16, NB], F32)
        XIM = A.alloc([128, 16, NB], F32)
        EL = A.alloc([128, 2, 16], F32)
        m_ssm = A.mark()
        if True:
            CRE = A.alloc([128, 512], F32)
            CIM = A.alloc([128, 512], F32)
            S.dma(CRE[:, :], lpb_d.ap()[li, :, 1024:1536])
            S.dma(CIM[:, :], lpb_d.ap()[li, :, 1536:2048])
            BRE = A.alloc([128, 512], F32)
            BIM = A.alloc([128, 512], F32)
            S.dma(BRE[:, :], lpb_d.ap()[li, :, 0:512])
            S.dma(BIM[:, :], lpb_d.ap()[li, :, 512:1024])
            STEP = A.alloc([128, 16], F32)
            SLR = A.alloc([128, 16], F32)
            W17 = [A.alloc([128, 17, 16], F32) for _ in range(4)]
            W17I = A.alloc([128, 17, 16], I32)
            S.act(STEP[:, :], lpc("logstep"), AF.Exp)
            S.tt(SLR[:, :], STEP[:, :], lpc("lamre"), ALU.mult)
            S.tt(SLI[:, :], STEP[:, :], lpc("lamim"), ALU.mult)
            S.act(RHO[:, :], SLR[:, :], AF.Exp, scale=float(R))
            S.act(MT[:, :], SLR[:, :], AF.Exp, scale=float(T))
            taub = cs("tau").unsqueeze(2).broadcast_to([128, 17, 16])
            S.tt(W17[0][:, :, :], SLR[:, :].unsqueeze(1).broadcast_to([128, 17, 16]), taub, ALU.mult)
            S.act(W17[0][:, :, :], W17[0][:, :, :], AF.Exp)
            S.tt(W17[1][:, :, :], SLI[:, :].unsqueeze(1).broadcast_to([128, 17, 16]), taub, ALU.mult)
            reduce_angle(W17[1][:, :, :], W17[1][:, :, :], W17[2][:, :, :], W17I[:, :, :])
            S.act(W17[2][:, :, :], W17[1][:, :, :], AF.Sin)
            S.act(W17[3][:, :, :], W17[1][:, :, :], AF.Sin, scale=0.5)
            S.tt(W17[3][:, :, :], W17[3][:, :, :], W17[3][:, :, :], ALU.mult)
            S.ts(W17[3][:, :, :], W17[3][:, :, :], -2.0, 1.0, ALU.mult, ALU.add)
            S.tt(AR[:, :, :], W17[0][:, :, :], W17[3][:, :, :], ALU.mult)
            S.tt(AI[:, :, :], W17[0][:, :, :], W17[2][:, :, :], ALU.mult)
            S.ts(NAI[:, :, :], AI[:, :, :], -1.0, None, ALU.mult)
            den, am1, fr, fi, t0 = [A.alloc([128, 16], F32) for _ in range(5)]
            S.tt(den[:, :], lpc("lamre"), lpc("lamre"), ALU.mult)
            S.tt(t0[:, :], lpc("lamim"), lpc("lamim"), ALU.mult)
            S.tt(den[:, :], den[:, :], t0[:, :], ALU.add)
            S.recip(den[:, :], den[:, :])
            S.ts(am1[:, :], AR[:, 1, :], -1.0, None, ALU.add)
            S.tt(fr[:, :], am1[:, :], lpc("lamre"), ALU.mult)
            S.tt(t0[:, :], AI[:, 1, :], lpc("lamim"), ALU.mult)
            S.tt(fr[:, :], fr[:, :], t0[:, :], ALU.add)
            S.tt(fr[:, :], fr[:, :], den[:, :], ALU.mult)
            S.tt(fi[:, :], AI[:, 1, :], lpc("lamre"), ALU.mult)
            S.tt(t0[:, :], am1[:, :], lpc("lamim"), ALU.mult)
            S.tt(fi[:, :], fi[:, :], t0[:, :], ALU.subtract)
            S.tt(fi[:, :], fi[:, :], den[:, :], ALU.mult)
            frb = fr[:, :].unsqueeze(2).broadcast_to([128, 16, 32])
            fib = fi[:, :].unsqueeze(2).broadcast_to([128, 16, 32])
            v3 = lambda t: t[:, :].rearrange("p (k c) -> p k c", k=16)
            TB1 = A.alloc([128, 512], F32)
            S.tt(v3(BR), frb, v3(BRE), ALU.mult)
            S.tt(v3(TB1), fib, v3(BIM), ALU.mult)
            S.tt(BR[:, :], BR[:, :], TB1[:, :], ALU.subtract)
            S.tt(v3(BI), frb, v3(BIM), ALU.mult)
            S.tt(v3(TB1), fib, v3(BRE), ALU.mult)
            S.tt(BI[:, :], BI[:, :], TB1[:, :], ALU.add)
            S.copy(CB[:, 0, :], CRE[:, :])
            S.ts(CB[:, 1, :], CIM[:, :], -1.0, None, ALU.mult)
        A.release(m_ssm)

        def pair_tables(ct, CT, ST, tf, ti):
            S.tt(CT[:, :, :], SLI[:, 4 * ct:4 * ct + 4].unsqueeze(2).broadcast_to([128, 4, 129]),
                 cs("bv").unsqueeze(1).broadcast_to([128, 4, 129]), ALU.mult)
            reduce_angle(CT[:, :, :], CT[:, :, :], tf[:, :, :], ti[:, :, :])
            S.act(ST[:, :, :], CT[:, :, :], AF.Sin)
            S.act(CT[:, :, :], CT[:, :, :], AF.Sin, scale=0.5)
            S.tt(CT[:, :, :], CT[:, :, :], CT[:, :, :], ALU.mult)
            S.ts(CT[:, :, :], CT[:, :, :], -2.0, 1.0, ALU.mult, ALU.add)

        def make_WE(ct, WE, ta, tb):
            for half in range(2):
                ts_ = slice(8 * half, 8 * half + 8)
                arb = AR[:, ts_, 4 * ct:4 * ct + 4].unsqueeze(3).broadcast_to([128, 8, 4, 32])
                aib = AI[:, ts_, 4 * ct:4 * ct + 4].unsqueeze(3).broadcast_to([128, 8, 4, 32])
                brb = BR[:, 128 * ct:128 * ct + 128].rearrange("p (i c) -> p i c", i=4).unsqueeze(1).broadcast_to([128, 8, 4, 32])
                bib = BI[:, 128 * ct:128 * ct + 128].rearrange("p (i c) -> p i c", i=4).unsqueeze(1).broadcast_to([128, 8, 4, 32])
                v4 = lambda t: t[:, :].rearrange("p (a i c) -> p a i c", a=8, i=4)
                S.tt(v4(ta), arb, brb, ALU.mult)
                S.tt(v4(tb), aib, bib, ALU.mult)
                S.tt(WE[:, 0, ts_, :].rearrange("p a n -> p (a n)"), ta[:, :], tb[:, :], ALU.subtract, eng="pool")
                S.tt(v4(ta), arb, bib, ALU.mult)
                S.tt(v4(tb), aib, brb, ALU.mult)
                S.tt(WE[:, 1, ts_, :].rearrange("p a n -> p (a n)"), ta[:, :], tb[:, :], ALU.add, eng="pool")

        for ct in range(4):
            m = A.mark()
            WE = A.alloc([128, 2, 16, 128], BF16)
            WDT = A.alloc([128, 2, 16, 128], BF16)
            ta = A.alloc([128, 1024], F32)
            tb = A.alloc([128, 1024], F32)
            CT = A.alloc([128, 4, 129], F32)
            ST = A.alloc([128, 4, 129], F32)
            TI = S.sb([128, 4, 129], I32, S.tinfo[tb.name][1])
            WS = A.alloc([128, 2, 4, NB], F32)
            make_WE(ct, WE, ta, tb)
            pair_tables(ct, CT, ST, ta[:, 0:516].rearrange("p (i b) -> p i b", i=4), TI)
            S.copy(CT128[:, 4 * ct:4 * ct + 4], CT[:, :, 128])
            S.copy(ST128[:, 4 * ct:4 * ct + 4], ST[:, :, 128])
            for x in range(2):
                for tg in range(4):
                    P = bank()
                    for t4 in range(4):
                        S.mm(P[:, 128 * t4:128 * (t4 + 1)], WE[:, x, 4 * tg + t4, :], IDENT)
                    S.act(WDT[:, x, 4 * tg:4 * tg + 4, :].rearrange("p a n -> p (a n)"), P, AF.Copy)
            for x in range(2):
                for j in range(R):
                    for i in range(4):
                        S.mm(PS[:, 512 * (4 * x + i):512 * (4 * x + i) + 128], WDT[32 * i:32 * i + 32, x, R - 1 - j, :],
                             U[32 * i:32 * i + 32, ct, :].rearrange("p (b j) -> p j b", j=R)[:, j, :],
                             start=(j == 0), stop=(j == R - 1), tile_position=(32 * i, 0))
            cc = CT[:, :, 1:129]
            ss = ST[:, :, 1:129]
            pre = PS[:, 0:2048].rearrange("p (i c) -> p i c", i=4)[:, :, 0:128]
            pim = PS[:, 2048:4096].rearrange("p (i c) -> p i c", i=4)[:, :, 0:128]
            v3b = lambda t: t[:, 0:512].rearrange("p (i b) -> p i b", i=4)
            S.tt(v3b(ta), pre, cc, ALU.mult)
            S.tt(v3b(tb), pim, ss, ALU.mult)
            S.tt(XRE[:, 4 * ct:4 * ct + 4, :], v3b(ta), v3b(tb), ALU.add, eng="pool")
            S.tt(v3b(ta), pim, cc, ALU.mult)
            S.tt(v3b(tb), pre, ss, ALU.mult)
            S.tt(XIM[:, 4 * ct:4 * ct + 4, :], v3b(ta), v3b(tb), ALU.subtract, eng="pool")
            for i in range(4):
                k = 4 * ct + i
                S.scan(WS[:, 0, i, :], RHO[:, k:k + 1].broadcast_to([128, NB]), XRE[:, k, :], 0.0)
                S.scan(WS[:, 1, i, :], RHO[:, k:k + 1].broadcast_to([128, NB]), XIM[:, k, :], 0.0)
            e1 = ta[:, 0:4]
            e2 = ta[:, 4:8]
            S.tt(e1, WS[:, 0, :, NB - 1], CT[:, :, 128], ALU.mult)
            S.tt(e2, WS[:, 1, :, NB - 1], ST[:, :, 128], ALU.mult)
            S.tt(EL[:, 0, 4 * ct:4 * ct + 4], e1, e2, ALU.subtract)
            S.tt(e1, WS[:, 1, :, NB - 1], CT[:, :, 128], ALU.mult)
            S.tt(e2, WS[:, 0, :, NB - 1], ST[:, :, 128], ALU.mult)
            S.tt(EL[:, 1, 4 * ct:4 * ct + 4], e1, e2, ALU.add)
            A.release(m)
        if stop == "S1":
            dbg("XRE", XRE[:, :, :]); dbg("EL", EL[:, :, :])
            A.release(m_layer)
            return
        S.dma(agEin.ap(), EL[:, :, :].rearrange("p a b -> p (a b)"))
        S.allgather(agEout.ap().opt(), agEin.ap().opt())
        if stop == "AG":
            A.release(m_layer)
            return

        m_att = A.mark()
        KVP = A.alloc([128, 3, 4096], BF16)
        PT = [A.alloc([128, 512], BF16) for _ in range(4)]
        F1, F2, F3, F4 = [A.alloc([128, 512], F32) for _ in range(4)]
        SQB = A.alloc([128, 512], BF16)
        LAMT = A.alloc([128, 64], F32)
        LS = A.alloc([128, 4], F32)
        NLAM = A.alloc([128, 1], F32)
        GS = A.alloc([128, 1], F32)
        ZB = A.alloc([128, 1], F32)
        S.memset(ZB[:, :], 0.0)
        for c in range(2):
            o = LP["lamqk"][0] + 128 * c
            S.tt(LAMT[:, :], LPT[:, o:o + 64], LPT[:, o + 64:o + 128], ALU.mult)
            S.rsum(LS[:, c:c + 1], LAMT[:, :])
        S.act(LS[:, 2:4], LS[:, 0:2], AF.Exp)
        S.tt(NLAM[:, :], LS[:, 3:4], LS[:, 2:3], ALU.subtract)
        S.tt(NLAM[:, :], NLAM[:, :], lpc("nlaminit"), ALU.add)
        S.tt(GS[:, :], lpc("gsub"), lpc("oml"), ALU.mult)
        pctr = 0
        for h in range(4):
            for r in range(3):
                S.dma(KVP[:, r, :], agout[h].ap().rearrange("(r p) n -> p r n", p=128)[:, r, :])
            for qt in range(4):
                qs = slice(512 * qt, 512 * (qt + 1))
                steps = []
                for kt in range(4 * qt + 4):
                    steps.append((K[:, h, 128 * kt:128 * (kt + 1)], V[:, h, kt, :], ZB[:, 0:1],
                                  (kt - 4 * qt) if kt >= 4 * qt else None))
                for r in range(3):
                    for kt in range(16):
                        steps.append((KVP[:, r, 128 * kt:128 * (kt + 1)], KVP[:, r, 2048 + 128 * kt:2048 + 128 * (kt + 1)],
                                      cs("rankbias", r, r + 1), None))
                O1, O2, D1, D2 = bankn(4), bankn(5), bankn(6), bankn(7)
                def emit_scores(idx):
                    kt_ap = steps[idx][0]
                    S.mm(bankn((2 * idx) % 4), kt_ap[0:64, :], Q[0:64, h, qs])
                    S.mm(bankn((2 * idx + 1) % 4), kt_ap[64:128, :], Q[64:128, h, qs])

                emit_scores(0)
                for idx, (kt_ap, v_ap, b_ap, mi) in enumerate(steps):
                    first, last = idx == 0, idx == len(steps) - 1
                    S1 = bankn((2 * idx) % 4)
                    S2 = bankn((2 * idx + 1) % 4)
                    if not last:
                        emit_scores(idx + 1)
                    P1 = PT[pctr % 4]
                    P2 = PT[(pctr + 1) % 4]
                    pctr += 2
                    S.act(P1[:, :], S1, AF.Exp, bias=b_ap, scale=0.125)
                    S.act(P2[:, :], S2, AF.Exp, bias=b_ap, scale=0.125)
                    if mi is not None:
                        S.tt(P1[:, :], P1[:, :], MASK(mi), ALU.mult, eng="pool")
                        S.tt(P2[:, :], P2[:, :], MASK(mi), ALU.mult, eng="pool")
                    S.mm(O1, v_ap, P1[:, :], start=first, stop=last)
                    S.mm(D1, ONES[:, :], P1[:, :], start=first, stop=last)
                    S.mm(O2, v_ap, P2[:, :], start=first, stop=last)
                    S.mm(D2, ONES[:, :], P2[:, :], start=first, stop=last)
                S.recip(F1[:, :], D1)
                S.recip(F2[:, :], D2)
                S.tt(F1[:, :], O1, F1[:, :], ALU.mult)
                S.tt(F2[:, :], O2, F2[:, :], ALU.mult)
                S.stt(F3[:, :], F2[:, :], NLAM[:, 0:1], F1[:, :], ALU.mult, ALU.add)
                S.act(SQB[:, :], F3[:, :], AF.Square)
                PN = bankn(0)
                S.mm(PN, ONES[:, :], SQB[:, :])
                S.act(F4[:, :], PN, AF.Sqrt, bias=EPSC[:, 0:1], scale=1.0 / 128)
                S.recip(F4[:, :], F4[:, :])
                S.stt(Q[:, h, qs], F3[:, :], GS[:, 0:1], F4[:, :], ALU.mult, ALU.mult)
        dbg("ATT", Q[:, :, :])
        A.release(m_att)
        if stop == "ATT":
            A.release(m_layer)
            return

        m2 = A.mark()
        EA = A.alloc([128, 4, 2, 16], F32)
        S.dma(EA[:, :, :, :].rearrange("p r a b -> p r (a b)"), agEout.ap().rearrange("(r p) n -> p r n", p=128))
        ATR, ATI, SCR, SCI, SNR, SNI, SINR, SINI, c1, c2 = [A.alloc([128, 16], F32) for _ in range(10)]
        CRE = A.alloc([128, 512], F32)
        CIM = A.alloc([128, 512], F32)
        KD = A.alloc([128, 16, 128], BF16)
        S.dma(CRE[:, :], lpb_d.ap()[li, :, 1024:1536])
        S.dma(CIM[:, :], lpb_d.ap()[li, :, 1536:2048])
        S.memset(KD[:, :, :], 0.0, eng="pool")
        S.tt(ATR[:, :], MT[:, :], CT128[:, :], ALU.mult)
        S.tt(ATI[:, :], MT[:, :], ST128[:, :], ALU.mult)
        S.copy(SCR[:, :], EA[:, 0, 0, :])
        S.copy(SCI[:, :], EA[:, 0, 1, :])
        S.ts(SINR[:, :], SCR[:, :], cs("ohq", 1, 2), None, ALU.mult)
        S.ts(SINI[:, :], SCI[:, :], cs("ohq", 1, 2), None, ALU.mult)
        for q in (1, 2):
            S.tt(c1[:, :], ATR[:, :], SCR[:, :], ALU.mult)
            S.tt(c2[:, :], ATI[:, :], SCI[:, :], ALU.mult)
            S.tt(SNR[:, :], c1[:, :], c2[:, :], ALU.subtract)
            S.tt(SNR[:, :], SNR[:, :], EA[:, q, 0, :], ALU.add)
            S.tt(c1[:, :], ATR[:, :], SCI[:, :], ALU.mult)
            S.tt(c2[:, :], ATI[:, :], SCR[:, :], ALU.mult)
            S.tt(SNI[:, :], c1[:, :], c2[:, :], ALU.add)
            S.tt(SNI[:, :], SNI[:, :], EA[:, q, 1, :], ALU.add)
            S.copy(SCR[:, :], SNR[:, :])
            S.copy(SCI[:, :], SNI[:, :])
            S.stt(SINR[:, :], SCR[:, :], cs("ohq", q + 1, q + 2), SINR[:, :], ALU.mult, ALU.add)
            S.stt(SINI[:, :], SCI[:, :], cs("ohq", q + 1, q + 2), SINI[:, :], ALU.mult, ALU.add)
        for ct in range(4):
            m = A.mark()
            WE = A.alloc([128, 2, 16, 128], BF16)
            CAE = A.alloc([128, 2, 16, 128], BF16)
            ta = A.alloc([128, 1024], F32)
            tb = A.alloc([128, 1024], F32)
            CT = A.alloc([128, 4, 129], F32)
            ST = A.alloc([128, 4, 129], F32)
            TI = S.sb([128, 4, 129], I32, S.tinfo[tb.name][1])
            WB = A.alloc([128, 2, 4, 129], F32)
            SS = A.alloc([128, 2, 4, NB], BF16)
            YF = tb
            make_WE(ct, WE, ta, tb)
            pair_tables(ct, CT, ST, ta[:, 0:516].rearrange("p (i b) -> p i b", i=4), TI)
            for half in range(2):
                js = slice(8 * half + 1, 8 * half + 9)
                jo = slice(8 * half, 8 * half + 8)
                arb = AR[:, js, 4 * ct:4 * ct + 4].unsqueeze(3).broadcast_to([128, 8, 4, 32])
                aib = AI[:, js, 4 * ct:4 * ct + 4].unsqueeze(3).broadcast_to([128, 8, 4, 32])
                naib = NAI[:, js, 4 * ct:4 * ct + 4].unsqueeze(3).broadcast_to([128, 8, 4, 32])
                crb = CRE[:, 128 * ct:128 * ct + 128].rearrange("p (i c) -> p i c", i=4).unsqueeze(1).broadcast_to([128, 8, 4, 32])
                cib = CIM[:, 128 * ct:128 * ct + 128].rearrange("p (i c) -> p i c", i=4).unsqueeze(1).broadcast_to([128, 8, 4, 32])
                v4 = lambda t: t[:, :].rearrange("p (a i c) -> p a i c", a=8, i=4)
                S.tt(v4(ta), arb, crb, ALU.mult)
                S.tt(v4(tb), aib, cib, ALU.mult)
                S.tt(CAE[:, 0, jo, :].rearrange("p a n -> p (a n)"), ta[:, :], tb[:, :], ALU.subtract, eng="pool")
                S.tt(v4(ta), naib, crb, ALU.mult)
                S.tt(v4(tb), arb, cib, ALU.mult)
                S.tt(CAE[:, 1, jo, :].rearrange("p a n -> p (a n)"), ta[:, :], tb[:, :], ALU.subtract, eng="pool")
            PK = bank()
            for tau in range(R):
                for i in range(4):
                    k = 4 * ct + i
                    for x in range(2):
                        S.mm(PK[32 * i:32 * i + 32, 32 * tau:32 * tau + 32], WE[:, x, tau, 32 * i:32 * i + 32],
                             CB[:, x, 32 * k:32 * k + 32], start=(x == 0), stop=(x == 1), tile_position=(0, 32 * i))
            for i in range(4):
                S.act(KD[32 * i:32 * i + 32, :, 32 * i:32 * i + 32],
                      PK[32 * i:32 * i + 32, :].rearrange("p (t c) -> p t c", t=R), AF.Copy)
            for i in range(4):
                k = 4 * ct + i
                S.copy(WB[:, 0, i, 0:1], SINR[:, k:k + 1])
                S.copy(WB[:, 1, i, 0:1], SINI[:, k:k + 1])
                S.scan(WB[:, 0, i, 1:129], RHO[:, k:k + 1].broadcast_to([128, NB]), XRE[:, k, :], SINR[:, k:k + 1])
                S.scan(WB[:, 1, i, 1:129], RHO[:, k:k + 1].broadcast_to([128, NB]), XIM[:, k, :], SINI[:, k:k + 1])
            cc = CT[:, :, 0:128]
            ss = ST[:, :, 0:128]
            v3b = lambda t: t[:, 0:512].rearrange("p (i b) -> p i b", i=4)
            S.tt(v3b(ta), WB[:, 0, :, 0:128], cc, ALU.mult)
            S.tt(v3b(tb), WB[:, 1, :, 0:128], ss, ALU.mult)
            S.tt(SS[:, 0, :, :], v3b(ta), v3b(tb), ALU.subtract, eng="pool")
            S.tt(v3b(ta), WB[:, 1, :, 0:128], cc, ALU.mult)
            S.tt(v3b(tb), WB[:, 0, :, 0:128], ss, ALU.mult)
            S.tt(SS[:, 1, :, :], v3b(ta), v3b(tb), ALU.add, eng="pool")
            yb = 4 * (ct % 2)
            Y = PS[:, 512 * yb:512 * (yb + 4)].rearrange("p (j b) -> p j b", j=R)
            Uv = U[:, ct, :].rearrange("p (b j) -> p j b", j=R)
            for j in range(R):
                for j2 in range(j + 1):
                    S.mm(Y[:, j, :], KD[:, j - j2, :], Uv[:, j2, :], start=(j2 == 0), stop=False)
                for i in range(4):
                    for x in range(2):
                        S.mm(Y[32 * i:32 * i + 32, j, :], CAE[:, x, j, 32 * i:32 * i + 32], SS[:, x, i, :],
                             start=False, stop=(x == 1), tile_position=(0, 32 * i))
            for hb in range(2):
                bs = slice(64 * hb, 64 * hb + 64)
                ts_ = slice(1024 * hb, 1024 * hb + 1024)
                S.stt(YF[:, :].rearrange("p (b j) -> p j b", j=R), Uv[:, :, bs], lpc("dvec", ct, ct + 1), Y[:, :, bs],
                      ALU.mult, ALU.add)
                if "YSSM" in debug:
                    if "_y" not in dbg_out:
                        dbg_out["_y"] = S.dram("dbg_YSSM", [128, 4, T], F32, kind="ExternalOutput")
                    S.dma(dbg_out["_y"].ap()[:, ct, ts_], YF[:, :])
                S.act(ta[:, :], YF[:, :], AF.Square)
                S.ts(ta[:, :], ta[:, :], 0.044715, 1.0, ALU.mult, ALU.add)
                S.tt(ta[:, :], ta[:, :], YF[:, :], ALU.mult)
                S.act(ta[:, :], ta[:, :], AF.Sigmoid, scale=2.0 * math.sqrt(2.0 / math.pi))
                S.tt(U[:, ct, ts_], YF[:, :], ta[:, :], ALU.mult, eng="pool")
            A.release(m)
        A.release(m2)
        A.release(m_layer)
        if stop == "S2":
            return

        WG = A.alloc([128, 4, 1024], BF16)
        WO = A.alloc([128, 8, 1024], BF16)
        STG = A.alloc([128, 2048], F32)
        for hf in range(2):
            sv = STG[:, :].rearrange("p (k n) -> p k n", k=4)
            S.dma(sv, w_glu_d.ap()[li, hf])
            S.copy(WG[:, :, 512 * hf:512 * (hf + 1)], sv, eng="pool")
        for g in range(4):
            sv = STG[:, :].rearrange("p (k n) -> p k n", k=8)
            S.dma(sv, w_out_d.ap()[li, g])
            S.copy(WO[:, :, 256 * g:256 * (g + 1)], sv, eng="pool")
        GL = A.alloc([128, 4, 512], F32)
        SG = A.alloc([128, 512], F32)
        SQ = [A.alloc([128, 512], BF16) for _ in range(2)]
        RS = A.alloc([128, 512], F32)
        for tt in range(NT):
            tok = slice(512 * tt, 512 * (tt + 1))
            for oc in range(4):
                PA, PB = bank(), bank()
                for kc in range(4):
                    S.mm(PB, WG[:, kc, 512 + 128 * oc:512 + 128 * (oc + 1)], U[:, kc, tok], start=(kc == 0), stop=(kc == 3))
                for kc in range(4):
                    S.mm(PA, WG[:, kc, 128 * oc:128 * (oc + 1)], U[:, kc, tok], start=(kc == 0), stop=(kc == 3))
                S.act(SG[:, :], PB, AF.Sigmoid)
                S.tt(GL[:, oc, :], PA, SG[:, :], ALU.mult)
            rmsnorm_tile(lambda oc: U[:, oc, tok], lambda oc: GL[:, oc, :], 4, 512,
                         lambda oc: lpc("gssm", oc, oc + 1), 512, [SQ[0][:, :], SQ[1][:, :]], RS[:, :])
        dbg("SSM", U[:, :, :])
        for tt in range(NT):
            tok = slice(512 * tt, 512 * (tt + 1))
            for dc in range(8):
                P = bank()
                for kc in range(8):
                    src = U[:, kc, tok] if kc < 4 else Q[:, kc - 4, tok]
                    S.mm(P, WO[:, kc, 128 * dc:128 * (dc + 1)], src, start=(kc == 0), stop=(kc == 7))
                S.tt(X[:, dc, tok], X[:, dc, tok], P, ALU.add)
        dbg("XMID", X[:, :, :])
        A.release(m_layer)
        if stop == "OUT":
            return

        S.dma(ag2in.ap().rearrange("p (c t) -> p c t", c=8), X[:, :, T - 2:T])
        S.allgather(ag2out.ap().opt(), ag2in.ap().opt())
        H4 = A.alloc([128, 4, 16], F32)
        XH = A.alloc([128, 16], F32)
        HH = A.alloc([128, 8, 2], BF16)
        HSQ = A.alloc([128, 16], BF16)
        HRS = A.alloc([128, 2], F32)
        HTMP = A.alloc([128, 16], F32)
        CARRY = [A.alloc([128, 44, 2], F32) for _ in range(5)]
        S.dma(H4[:, :, :], ag2out.ap().rearrange("(r p) n -> p r n", p=128))
        S.ts(XH[:, :], H4[:, 0, :], cs("ohprev", 0, 1), None, ALU.mult)
        for r in range(1, 4):
            S.stt(XH[:, :], H4[:, r, :], cs("ohprev", r, r + 1), XH[:, :], ALU.mult, ALU.add)
        XHv = XH[:, :].rearrange("p (c t) -> p c t", c=8)
        S.act(HSQ[:, :], XH[:, :], AF.Square)
        PHn = bank()
        for dc in range(8):
            S.mm(PHn[:, 0:2], ONES[:, :], HSQ[:, 2 * dc:2 * dc + 2], start=(dc == 0), stop=(dc == 7))
        S.act(HRS[:, :], PHn[:, 0:2], AF.Sqrt, bias=EPSC[:, 0:1], scale=1.0 / D)
        S.recip(HRS[:, :], HRS[:, :])
        S.tt(HTMP[:, :].rearrange("p (c t) -> p c t", c=8), XHv, HRS[:, :].unsqueeze(1).broadcast_to([128, 8, 2]), ALU.mult)
        S.tt(HH[:, :, :], HTMP[:, :].rearrange("p (c t) -> p c t", c=8),
             lpc("gffn").unsqueeze(2).broadcast_to([128, 8, 2]), ALU.mult)
        A2.release(A2.lo)
        HTF = A2.alloc([128, 8, 1024], BF16)
        ACTT = A2.alloc([128, NFC, 1024], BF16)
        SQ = [A.alloc([128, 512], BF16) for _ in range(2)]
        RS = A.alloc([128, 512], F32)
        STU = [A.alloc([128, 8, 256], F32) for _ in range(2)]
        WUB = [A.alloc([128, 8, 256], BF16) for _ in range(2)]
        STD = [A.alloc([128, 11, 128], F32) for _ in range(2)]
        WDB = [A.alloc([128, NFC, 128], BF16) for _ in range(2)]
        ACC = [A.alloc([128, 512], F32) for _ in range(6)]
        actr = [0]
        SGT = A.alloc([128, 512], F32)
        wctr = 0
        for hf in range(2):
            for t2 in range(2):
                tok = slice(1024 * hf + 512 * t2, 1024 * hf + 512 * (t2 + 1))
                rmsnorm_tile(lambda dc: HTF[:, dc, 512 * t2:512 * (t2 + 1)], lambda dc: X[:, dc, tok], 8, 512,
                             lambda dc: lpc("gffn", dc, dc + 1), D, [SQ[0][:, :], SQ[1][:, :]], RS[:, :])
            for fc in range(NFC):
                b = wctr % 2
                wctr += 1
                S.dma(STU[b][:, :, :], w_up_d.ap()[li, fc])
                S.copy(WUB[b][:, :, :], STU[b][:, :, :], eng="pool")
                for gv in range(2):
                    c = gv * NFC + fc
                    if hf == 0:
                        PH = bank()
                        for dc in range(8):
                            S.mm(PH[:, 0:2], WUB[b][:, dc, 128 * gv:128 * (gv + 1)], HH[:, dc, :], start=(dc == 0), stop=(dc == 7))
                        S.act(CARRY[0][:, c, :], PH[:, 0:2], AF.Copy)
                    for t2 in range(2):
                        gt = 2 * hf + t2
                        P = bank()
                        for dc in range(8):
                            S.mm(P, WUB[b][:, dc, 128 * gv:128 * (gv + 1)], HTF[:, dc, 512 * t2:512 * (t2 + 1)],
                                 start=(dc == 0), stop=(dc == 7))
                        acc = ACC[actr[0] % 6]
                        actr[0] += 1
                        w0 = lpc("wconv", c, c + 1)
                        w1 = lpc("wconv", 44 + c, 44 + c + 1)
                        w2 = lpc("wconv", 88 + c, 88 + c + 1)
                        if t2 == 1:
                            pass
                        S.act(acc[:, :], P, AF.Identity, bias=lpc("bconv", c, c + 1), scale=w2)
                        S.stt(acc[:, 1:512], P[:, 0:511], w1, acc[:, 1:512], ALU.mult, ALU.add)
                        S.stt(acc[:, 2:512], P[:, 0:510], w0, acc[:, 2:512], ALU.mult, ALU.add)
                        S.stt(acc[:, 0:2], CARRY[gt][:, c, 0:2], w0, acc[:, 0:2], ALU.mult, ALU.add)
                        S.stt(acc[:, 0:1], CARRY[gt][:, c, 1:2], w1, acc[:, 0:1], ALU.mult, ALU.add)
                        S.act(CARRY[gt + 1][:, c, :], P[:, 510:512], AF.Copy)
                        if gv == 0:
                            S.act(ACTT[:, fc, 512 * t2:512 * (t2 + 1)], acc[:, :], AF.Silu)
                        else:
                            S.tt(ACTT[:, fc, 512 * t2:512 * (t2 + 1)], ACTT[:, fc, 512 * t2:512 * (t2 + 1)], acc[:, :], ALU.mult)
            for dc in range(8):
                b = dc % 2
                for h2 in range(2):
                    S.dma(STD[h2][:, :, :], w_dn_d.ap()[li, dc, h2])
                    S.copy(WDB[b][:, 11 * h2:11 * h2 + 11, :], STD[h2][:, :, :], eng="pool")
                for t2 in range(2):
                    tok = slice(1024 * hf + 512 * t2, 1024 * hf + 512 * (t2 + 1))
                    P = bank()
                    for fc in range(NFC):
                        S.mm(P, WDB[b][:, fc, :], ACTT[:, fc, 512 * t2:512 * (t2 + 1)], start=(fc == 0), stop=(fc == NFC - 1))
                    S.tt(X[:, dc, tok], X[:, dc, tok], P, ALU.add)
        dbg("XOUT", X[:, :, :])
        A.release(m_layer)

    for li in range(L):
        layer(li)

    if final:
        OT = A.alloc([128, 8, 512], F32)
        SQ = [A.alloc([128, 512], BF16) for _ in range(2)]
        RS = A.alloc([128, 512], F32)
        for tt in range(NT):
            tok = slice(512 * tt, 512 * (tt + 1))
            rmsnorm_tile(lambda dc: OT[:, dc, :], lambda dc: X[:, dc, tok], 8, 512,
                         lambda dc: lpc("gfin", dc, dc + 1), D, [SQ[0][:, :], SQ[1][:, :]], RS[:, :])
            S.dma(out_d.ap()[:, :, tok], OT[:, :, :])
    else:
        for q in range(4):
            S.dma(out_d.ap()[:, 2 * q:2 * q + 2, :], X[:, 2 * q:2 * q + 2, :])
    S.finish()
    S.build()
    return nc


def _consts(core):
    qi = core % 4
    c = np.zeros((128, NCS), np.float32)
    inv_freq = (500000.0 ** (-(np.arange(0, 16, 2, dtype=np.float32) / 16.0))).astype(np.float32)
    for p in range(128):
        d = p % 64
        if d < 16:
            c[p, CS["invf"][0]] = inv_freq[d % 8]
            c[p, CS["sgn"][0]] = -1.0 if d < 8 else 1.0
    c[:, CS["tau"][0]:CS["tau"][0] + 17] = np.arange(17, dtype=np.float32)[None]
    c[:, CS["bv"][0]:CS["bv"][0] + 129] = (R * np.arange(129, dtype=np.float32))[None]
    for r in range(3):
        c[:, CS["rankbias"][0] + r] = 0.0 if r < qi else -30000.0
    c[:, CS["ohq"][0] + qi] = 1.0
    if qi > 0:
        c[:, CS["ohprev"][0] + qi - 1] = 1.0
    return c


def _cbf():
    c = np.zeros((128, 256 + 2048), np.float32)
    c[:, 0:128] = np.eye(128, dtype=np.float32)
    for m in range(128):
        d = m % 64
        if d < 8:
            c[m + 8, 128 + m] = 1.0
        elif d < 16:
            c[m - 8, 128 + m] = 1.0
    kk = np.arange(128)[:, None]
    qq = np.arange(512)[None, :]
    for i in range(4):
        c[:, 256 + 512 * i:256 + 512 * (i + 1)] = ((128 * i + kk) // 64 <= qq // 64).astype(np.float32)
    return c.astype(ml_dtypes.bfloat16)


def _layer_params(inp, l):
    lp = np.zeros((128, NLP), np.float32)

    def put(name, arr):
        o, w = LP[name]
        lp[:, o:o + w] = arr

    put("gmix", inp["norm_mix"][l].reshape(8, 128).T)
    put("gffn", inp["norm_ffn"][l].reshape(8, 128).T)
    put("gssm", inp["ssm_norm"][l].reshape(4, 128).T)
    put("gsub", inp["attn_subln"][l].reshape(128, 1))
    put("dvec", inp["ssm_d"][l].reshape(4, 128).T)
    pl = lambda a: a.reshape(16, 2, 64).transpose(1, 2, 0).reshape(128, 16)
    put("lamre", pl(inp["ssm_lambda_re"][l]))
    put("lamim", pl(inp["ssm_lambda_im"][l]))
    put("logstep", pl(np.repeat(inp["ssm_log_step"][l][:, None], 64, axis=1)))
    put("wconv", inp["w_conv"][l].reshape(3, 44, 128).transpose(2, 0, 1).reshape(128, 132))
    put("bconv", inp["b_conv"][l].reshape(44, 128).T)
    lam = np.concatenate([inp["lambda_q1"][l], inp["lambda_k1"][l], inp["lambda_q2"][l], inp["lambda_k2"][l]])
    put("lamqk", np.repeat(lam[None, :], 128, axis=0))
    put("gfin", inp["norm_final"].reshape(8, 128).T)
    lam_init = 0.8 - 0.6 * math.exp(-0.3 * l)
    put("nlaminit", np.full((128, 1), -lam_init, np.float32))
    put("oml", np.full((128, 1), 1.0 - lam_init, np.float32))
    lpb = np.zeros((128, 4, 16, 2, 16), np.float32)
    b_re = inp["ssm_b_re"][l].reshape(16, 2, 64, 16)
    b_im = inp["ssm_b_im"][l].reshape(16, 2, 64, 16)
    c_re = inp["ssm_c_re"][l].reshape(16, 2, 16, 64)
    c_im = inp["ssm_c_im"][l].reshape(16, 2, 16, 64)
    for g2 in range(2):
        rows = slice(64 * g2, 64 * g2 + 64)
        lpb[rows, 0, :, g2, :] = b_re[:, g2].transpose(1, 0, 2)
        lpb[rows, 1, :, g2, :] = b_im[:, g2].transpose(1, 0, 2)
        lpb[rows, 2, :, g2, :] = c_re[:, g2].transpose(2, 0, 1)
        lpb[rows, 3, :, g2, :] = c_im[:, g2].transpose(2, 0, 1)
    return lp, lpb.reshape(128, 2048)


def _layer_weights(inp, l):
    w_in = inp["w_in"][l].reshape(8, 128, 8, 256).transpose(2, 1, 0, 3)
    w_glu = inp["ssm_w_glu"][l].reshape(4, 128, 2, 512).transpose(2, 1, 0, 3)
    w_out = inp["w_out"][l].reshape(8, 128, 4, 256).transpose(2, 1, 0, 3)
    w_up = inp["w_up"][l].reshape(8, 128, 2, NFC, 128).transpose(3, 1, 0, 2, 4).reshape(NFC, 128, 8, 256)
    w_dn = inp["w_down"][l].reshape(2, 11, 128, 8, 128).transpose(3, 0, 2, 1, 4)
    return {k: np.ascontiguousarray(v, dtype=np.float32) for k, v in
            (("w_in", w_in), ("w_glu", w_glu), ("w_out", w_out), ("w_up", w_up), ("w_dn", w_dn))}


_PROG = {}


def _get_prog(L, final):
    key = (L, final)
    if key not in _PROG:
        _PROG[key] = build_program(L, final)
    return _PROG[key]


FUSED = True


def kernel(**inp):
    inp = {k: np.asarray(v) for k, v in inp.items()}
    x = inp["x"].astype(np.float32)
    xs = []
    for c in range(8):
        b, qi = c // 4, c % 4
        xs.append(np.ascontiguousarray(x[b, qi * T:(qi + 1) * T, :].T.reshape(8, 128, T).transpose(1, 0, 2)))
    poss = [np.ascontiguousarray(inp["positions"][c // 4, (c % 4) * T:(c % 4 + 1) * T].astype(np.int32)[None]) for c in range(8)]
    csts = [_consts(c) for c in range(8)]
    cbf = _cbf()
    lps = [_layer_params(inp, l) for l in range(DEPTH)]
    wts = [_layer_weights(inp, l) for l in range(DEPTH)]

    def stack(ls):
        maps = {"lp": np.stack([lps[l][0] for l in ls]), "lpb": np.stack([lps[l][1] for l in ls])}
        for k in ("w_in", "w_glu", "w_out", "w_up", "w_dn"):
            maps[k] = np.stack([wts[l][k] for l in ls])
        return maps

    if FUSED:
        launches = [(list(range(DEPTH)), True)]
    else:
        launches = [([l], l == DEPTH - 1) for l in range(DEPTH)]
    for ls, final in launches:
        nc = _get_prog(len(ls), final)
        shared = stack(ls)
        in_maps = []
        for c in range(8):
            m = {"x_in": xs[c], "pos": poss[c], "cst": csts[c], "cbf": cbf}
            m.update(shared)
            in_maps.append(m)
        res = run_bass_kernel_spmd(nc, in_maps, core_ids=list(range(8)))
        xs = [np.asarray(r["out"], dtype=np.float32) for r in res.results]
    out = np.zeros((2, 8192, D), np.float32)
    for c in range(8):
        b, qi = c // 4, c % 4
        out[b, qi * T:(qi + 1) * T, :] = xs[c].transpose(1, 0, 2).reshape(D, T).T
    return out
```

```python
import math
from contextlib import ExitStack

import numpy as np
import ml_dtypes
import concourse.bass as bass
import concourse.mybir as mybir
from concourse.bass_utils import run_bass_kernel_spmd

F32 = mybir.dt.float32
BF16 = mybir.dt.bfloat16
I32 = mybir.dt.int32
ALU = mybir.AluOpType
AF = mybir.ActivationFunctionType
_ESZ = {F32: 4, BF16: 2, I32: 4}

D = 1024
T = 2048
NT = 4
DEPTH = 4
FFN = 2816
NFC = 22
R = 16
NB = T // R
EPS = 1e-6
TWO_PI = 2.0 * math.pi
C1 = 6.28125
C2 = TWO_PI - C1
SB_LO = 16512
SB_HI = 229344
GROUPS = [[0, 1, 2, 3], [4, 5, 6, 7]]


class Sched:
    ENGS = ("pe", "act", "dve", "pool", "sp")
    BK = 2048

    def __init__(self, nc, n_dma_sems=32):
        self.nc = nc
        self.ops = {e: [] for e in self.ENGS}
        self.tinfo = {}
        self.recs = {}
        self.buckets = {}
        self.known = {e: {} for e in self.ENGS}
        self.known_dma = {e: {} for e in self.ENGS}
        self.snap = {}
        self.targets = {e: set() for e in self.ENGS}
        self.n_dma = 0
        self.n_dma_sems = n_dma_sems
        self.n_cc = 0
        self.uid = 0

    def sb(self, shape, dtype, offset):
        self.uid += 1
        h = self.nc.alloc_sbuf_tensor_at("t%d" % self.uid, list(shape), dtype, offset=offset)
        self.tinfo[h.name] = ("sb", offset, int(np.prod(shape[1:])) * _ESZ[dtype])
        return h

    def ps(self, name, shape, dtype=F32):
        h = self.nc.alloc_psum_tensor(name, list(shape), dtype)
        self.tinfo[h.name] = ("ps", 0, int(np.prod(shape[1:])) * _ESZ[dtype])
        return h

    def dram(self, name, shape, dtype, kind="Internal"):
        h = self.nc.dram_tensor(name, list(shape), dtype, kind=kind)
        self.tinfo[h.name] = ("dr:" + name, 0, None)
        return h

    def region(self, ap):
        space, base, psb = self.tinfo[ap.tensor.name]
        esz = _ESZ[ap.dtype]
        aps = ap.ap
        off = int(ap.offset) * esz
        if psb is None:
            span = sum((c - 1) * abs(s) for s, c in aps) * esz
            return (space, off, off + span + esz, 0, 1)
        p0 = off // psb
        fo = off % psb
        span = sum((c - 1) * abs(s) for s, c in aps[1:]) * esz
        pstep, pcnt = aps[0]
        nstep = max(1, (pstep * esz) // psb) if pstep else 1
        return (space, base + fo, base + fo + span + esz, p0, p0 + (pcnt - 1) * nstep + 1)

    def _bk(self, rg):
        if rg[0][0] == "d":
            return range(0, 1)
        return range(rg[1] // self.BK, (rg[2] - 1) // self.BK + 1)

    def add(self, eng, emit, reads=(), writes=(), kind="cmp"):
        rr = [self.region(a) for a in reads if a is not None and hasattr(a, "tensor")]
        ww = [self.region(a) for a in writes if a is not None]
        pa = [(g[0], g[1] // 2048 * 2048, ((g[2] - 1) // 2048 + 1) * 2048, 0, 128) for g in rr + ww if g[0] == "ps"]
        rr = [g for g in rr if g[0] != "ps"]
        ww = [g for g in ww if g[0] != "ps"] + pa
        seq = len(self.ops[eng])
        op = {"eng": eng, "emit": emit, "kind": kind, "seq": seq, "waits": [], "dmawaits": [], "sem": None}
        if kind == "dma":
            i = self.n_dma
            self.n_dma += 1
            P = self.n_dma_sems
            op["sem"] = ("d", i % P, 16 * (i // P + 1))
            if i >= P:
                self._need_dma(op, ("d", i % P, 16 * (i // P)))
        elif kind == "cc":
            i = self.n_cc
            self.n_cc += 1
            op["sem"] = ("cc", i, 1)
        wid = ("E", eng, seq) if kind == "cmp" else ("D",) + op["sem"]
        for rg in rr:
            for key in self._overlaps(rg):
                w = self.recs[key][0]
                if w is not None:
                    self._need(op, w)
        for rg in ww:
            for key in list(self._overlaps(rg)):
                rec = self.recs[key]
                if rec[0] is not None:
                    self._need(op, rec[0])
                for e2, s2 in rec[1].items():
                    self._need(op, ("E", e2, s2))
                for d in rec[2]:
                    self._need(op, d)
                if key[1] >= rg[1] and key[2] <= rg[2] and key[3] >= rg[3] and key[4] <= rg[4]:
                    self._del(key)
        for rg in rr:
            rec = self._get(rg)
            if kind == "cmp":
                rec[1][eng] = seq
            else:
                rec[2].append(wid)
        for rg in ww:
            rec = self._get(rg)
            rec[0] = wid
            rec[1] = {}
            rec[2] = []
        if kind == "cmp":
            self.snap[(eng, seq)] = dict(self.known[eng])
        self.ops[eng].append(op)
        return op

    def _get(self, rg):
        rec = self.recs.get(rg)
        if rec is None:
            rec = [None, {}, []]
            self.recs[rg] = rec
            for b in self._bk(rg):
                self.buckets.setdefault((rg[0], b), set()).add(rg)
        return rec

    def _del(self, key):
        del self.recs[key]
        for b in self._bk(key):
            self.buckets[(key[0], b)].discard(key)

    def _overlaps(self, rg):
        out = set()
        for b in self._bk(rg):
            for key in self.buckets.get((rg[0], b), ()):
                if key[1] < rg[2] and rg[1] < key[2] and key[3] < rg[4] and rg[3] < key[4]:
                    out.add(key)
        return out

    def _need(self, op, w):
        if w[0] == "E":
            self._need_eng(op, w[1], w[2])
        else:
            self._need_dma(op, w[1:])

    def _need_eng(self, op, f, s):
        e = op["eng"]
        if e == f and e == "pe":
            return
        if e == f and s >= op["seq"]:
            return
        kn = self.known[e]
        if kn.get(f, -1) >= s:
            return
        kn[f] = s
        for f2, s2 in self.snap.get((f, s), {}).items():
            if f2 != e and kn.get(f2, -1) < s2:
                kn[f2] = s2
        op["waits"].append((f, s))
        self.targets[f].add(s)

    def _need_dma(self, op, d):
        kd = self.known_dma[op["eng"]]
        key = (d[0], d[1])
        if kd.get(key, 0) >= d[2]:
            return
        kd[key] = d[2]
        op["dmawaits"].append(d)

    def finish(self):
        op = {"eng": "sp", "emit": lambda e: e.nop(), "kind": "cmp", "seq": len(self.ops["sp"]),
              "waits": [], "dmawaits": [], "sem": None}
        for e in self.ENGS:
            if e != "sp" and self.ops[e]:
                self._need_eng(op, e, len(self.ops[e]) - 1)
        P = self.n_dma_sems
        for i in range(max(0, self.n_dma - P), self.n_dma):
            self._need_dma(op, ("d", i % P, 16 * (i // P + 1)))
        for i in range(self.n_cc):
            self._need_dma(op, ("cc", i, 1))
        self.ops["sp"].append(op)

    def build(self):
        nc = self.nc
        with ExitStack() as es:
            sem_e = {e: es.enter_context(nc.semaphore("s_" + e)) for e in self.ENGS}
            sem_d = {}
            for i in range(self.n_dma_sems):
                sem_d[("d", i)] = es.enter_context(nc.semaphore("d%d" % i))
            for i in range(self.n_cc):
                sem_d[("cc", i)] = es.enter_context(nc.semaphore("cc%d" % i))
            rank = {}
            for e in self.ENGS:
                for r, s in enumerate(sorted(self.targets[e])):
                    rank[(e, s)] = r + 1
            block = es.enter_context(nc.Block())
            handles = {"pe": block.tensor, "act": block.scalar, "dve": block.vector,
                       "pool": block.gpsimd, "sp": block.sync}
            for e in self.ENGS:
                ops = self.ops[e]
                if not ops:
                    continue

                def run(eng, ops=ops, e=e):
                    for op in ops:
                        for f, s in op["waits"]:
                            eng.wait_ge(sem_e[f], rank[(f, s)])
                        for d in op["dmawaits"]:
                            eng.wait_ge(sem_d[(d[0], d[1])], d[2])
                        ins = op["emit"](eng)
                        if op["kind"] == "dma":
                            ins.then_inc(sem_d[(op["sem"][0], op["sem"][1])], 16)
                        elif op["kind"] == "cc":
                            ins.then_inc(sem_d[(op["sem"][0], op["sem"][1])])
                        elif (e, op["seq"]) in rank:
                            ins.then_inc(sem_e[e], 1)
                handles[e](run)

    def mm(self, out, lhsT, rhs, start=True, stop=True, **kw):
        return self.add("pe", lambda e: e.matmul(out, lhsT, rhs, start=start, stop=stop, **kw), [lhsT, rhs], [out])

    def act(self, out, in_, func, bias=0.0, scale=1.0):
        return self.add("act", lambda e: e.activation(out, in_, func, bias=bias, scale=scale), [in_, bias, scale], [out])

    def tt(self, out, in0, in1, op, eng="dve"):
        return self.add(eng, lambda e: e.tensor_tensor(out, in0, in1, op), [in0, in1], [out])

    def ts(self, out, in0, s1, s2=None, op0=ALU.mult, op1=None, eng="dve"):
        if op1 is None:
            return self.add(eng, lambda e: e.tensor_scalar(out, in0, s1, None, op0), [in0, s1], [out])
        return self.add(eng, lambda e: e.tensor_scalar(out, in0, s1, s2, op0, op1), [in0, s1, s2], [out])

    def stt(self, out, in0, scalar, in1, op0, op1, eng="dve"):
        return self.add(eng, lambda e: e.scalar_tensor_tensor(out, in0, scalar, in1, op0, op1), [in0, scalar, in1], [out])

    def copy(self, out, in_, eng="dve"):
        return self.add(eng, lambda e: e.tensor_copy(out, in_), [in_], [out])

    def memset(self, out, val, eng="dve"):
        return self.add(eng, lambda e: e.memset(out, val), [], [out])

    def recip(self, out, in_):
        return self.add("dve", lambda e: e.reciprocal(out, in_), [in_], [out])

    def rsum(self, out, in_):
        return self.add("dve", lambda e: e.reduce_sum(out, in_, mybir.AxisListType.X), [in_], [out])

    def scan(self, out, d0, d1, init):
        return self.add("dve", lambda e: e.tensor_tensor_scan(out, d0, d1, init, ALU.mult, ALU.add), [d0, d1, init], [out])

    def dma(self, out, in_):
        return self.add("sp", lambda e: e.dma_start(out=out, in_=in_), [in_], [out], kind="dma")

    def allgather(self, out, in_):
        return self.add("pool", lambda e: e.collective_compute(
            "AllGather", ALU.bypass, replica_groups=GROUPS, ins=[in_], outs=[out]), [in_], [out], kind="cc")


class Arena:
    def __init__(self, S, lo, hi):
        self.S, self.lo, self.hi, self.top = S, lo, hi, lo

    def alloc(self, shape, dt):
        nb = int(np.prod(shape[1:])) * _ESZ[dt]
        off = (self.top + 63) // 64 * 64
        assert off + nb <= self.hi, ("SBUF overflow", shape, off + nb - self.hi)
        self.top = off + nb
        return self.S.sb(shape, dt, off)

    def mark(self):
        return self.top

    def release(self, m):
        self.top = m


LP = {}
_o = 0
for _n, _w in (("gmix", 8), ("gffn", 8), ("gssm", 4), ("gsub", 1), ("dvec", 4), ("lamre", 16), ("lamim", 16),
               ("logstep", 16), ("wconv", 132), ("bconv", 44), ("lamqk", 256), ("gfin", 8), ("nlaminit", 1), ("oml", 1)):
    LP[_n] = (_o, _w)
    _o += _w
NLP = _o
CS = {}
_o = 0
for _n, _w in (("invf", 1), ("sgn", 1), ("tau", 17), ("bv", 129), ("rankbias", 3), ("ohq", 4), ("ohprev", 4)):
    CS[_n] = (_o, _w)
    _o += _w
NCS = _o
AGW = 16384


def build_program(L, final, debug=(), stop=None):
    nc = bass.Bass("TRN2", target_bir_lowering=False)
    S = Sched(nc)
    A = Arena(S, SB_LO, SB_HI)
    dbg_out = {}

    x_in = S.dram("x_in", [128, 8, T], F32, kind="ExternalInput")
    pos = S.dram("pos", [1, T], I32, kind="ExternalInput")
    cst_d = S.dram("cst", [128, NCS], F32, kind="ExternalInput")
    cbf_d = S.dram("cbf", [128, 256 + 2048], BF16, kind="ExternalInput")
    lp_d = S.dram("lp", [L, 128, NLP], F32, kind="ExternalInput")
    lpb_d = S.dram("lpb", [L, 128, 2048], F32, kind="ExternalInput")
    w_in_d = S.dram("w_in", [L, 8, 128, 8, 256], F32, kind="ExternalInput")
    w_glu_d = S.dram("w_glu", [L, 2, 128, 4, 512], F32, kind="ExternalInput")
    w_out_d = S.dram("w_out", [L, 4, 128, 8, 256], F32, kind="ExternalInput")
    w_up_d = S.dram("w_up", [L, NFC, 128, 8, 256], F32, kind="ExternalInput")
    w_dn_d = S.dram("w_dn", [L, 8, 2, 128, 11, 128], F32, kind="ExternalInput")
    out_d = S.dram("out", [128, 8, T], F32, kind="ExternalOutput")
    agin = [S.dram("agin%d" % h, [128, 4096], BF16) for h in range(4)]
    agout = [S.dram("agout%d" % h, [512, 4096], BF16) for h in range(4)]
    agEin = S.dram("agEin", [128, 32], F32)
    agEout = S.dram("agEout", [512, 32], F32)
    ag2in = S.dram("ag2in", [128, 16], F32)
    ag2out = S.dram("ag2out", [512, 16], F32)

    def dbg(name, ap):
        if name not in debug:
            return
        t = S.dram("dbg_" + name, list(ap.shape), ap.dtype, kind="ExternalOutput")
        S.dma(t.ap(), ap)
        dbg_out[name] = True

    X = A.alloc([128, 8, T], F32)
    uqkv_lo = (A.top + 63) // 64 * 64
    U = A.alloc([128, 4, T], BF16)
    Q = A.alloc([128, 4, T], BF16)
    K = A.alloc([128, 4, T], BF16)
    V = A.alloc([128, 4, 16, 128], BF16)
    A2 = Arena(S, uqkv_lo, A.top)
    CST = A.alloc([128, NCS], F32)
    CBF = A.alloc([128, 256 + 2048], BF16)
    ONES = A.alloc([128, 128], BF16)
    EPSC = A.alloc([128, 1], F32)
    LPT = A.alloc([128, NLP], F32)
    PS = S.ps("psum", [128, 4096], F32)

    def cs(name, a=0, b=None):
        o, w = CS[name]
        return CST[:, o + a:o + (w if b is None else b)]

    def lpc(name, a=0, b=None):
        o, w = LP[name]
        return LPT[:, o + a:o + (w if b is None else b)]

    IDENT = CBF[:, 0:128]
    ROTM = CBF[:, 128:256]

    def MASK(i):
        return CBF[:, 256 + 512 * i:256 + 512 * (i + 1)]

    bank_ctr = [0]

    def bank(n=1):
        b = bank_ctr[0] % 8
        bank_ctr[0] += 1
        return PS[:, 512 * b:512 * (b + 1)]

    def bankn(b):
        return PS[:, 512 * b:512 * (b + 1)]

    S.dma(CST[:, :], cst_d.ap())
    S.dma(CBF[:, :], cbf_d.ap())
    for q in range(4):
        S.dma(X[:, 2 * q:2 * q + 2, :], x_in.ap()[:, 2 * q:2 * q + 2, :])
    S.memset(ONES[:, :], 1.0)
    S.memset(EPSC[:, :], EPS)

    def reduce_angle(rout, ang, tmpf, tmpi):
        S.ts(tmpf, ang, 1.0 / TWO_PI, None, ALU.mult)
        S.copy(tmpi, tmpf)
        S.copy(tmpf, tmpi)
        S.stt(rout, tmpf, -C1, ang, ALU.mult, ALU.add)
        S.stt(rout, tmpf, -C2, rout, ALU.mult, ALU.add)

    def rmsnorm_tile(dst, src_of_dc, ndc, width, gcol, nfeat, sq_tmp, rs, dst_is_list=False):
        P = bank()
        for dc in range(ndc):
            sq = sq_tmp[dc % 2]
            S.act(sq, src_of_dc(dc), AF.Square)
            S.mm(P[:, 0:width], ONES[:, :], sq, start=(dc == 0), stop=(dc == ndc - 1))
        S.act(rs, P[:, 0:width], AF.Sqrt, bias=EPSC[:, 0:1], scale=1.0 / nfeat)
        S.recip(rs, rs)
        for dc in range(ndc):
            S.stt(dst(dc), src_of_dc(dc), gcol(dc), rs, ALU.mult, ALU.mult)

    def layer(li):
        S.dma(LPT[:, :], lp_d.ap()[li])
        m_layer = A.mark()

        WIN = A.alloc([128, 8, 2048], BF16)
        STG = A.alloc([128, 8, 256], F32)
        for g in range(8):
            S.dma(STG[:, :, :], w_in_d.ap()[li, g])
            S.copy(WIN[:, :, 256 * g:256 * (g + 1)], STG[:, :, :], eng="pool")
        HT = A.alloc([128, 8, 512], BF16)
        SQ = [A.alloc([128, 512], BF16) for _ in range(2)]
        RS = A.alloc([128, 512], F32)
        PI = A.alloc([128, 512], I32)
        ANG = A.alloc([128, 512], F32)
        TF = A.alloc([128, 512], F32)
        COSF = A.alloc([128, 512], F32)
        SINF = A.alloc([128, 512], F32)
        QB = A.alloc([128, 512], BF16)
        T1 = A.alloc([128, 512], F32)
        T2 = A.alloc([128, 512], F32)
        PARTS = set(stop.split(':')[1].split('+')) if (stop and ':' in stop) else {'norm', 'rope', 'proj', 'v'}
        for tt in range(NT):
            tok = slice(512 * tt, 512 * (tt + 1))
            rmsnorm_tile(lambda dc: HT[:, dc, :], lambda dc: X[:, dc, tok], 8, 512,
                         lambda dc: lpc("gmix", dc, dc + 1), D, [SQ[0][:, :], SQ[1][:, :]], RS[:, :])
            if 'rope' not in PARTS:
                continue
            S.dma(PI[:, :], pos.ap()[0:1, tok].partition_broadcast(128))
            S.copy(TF[:, :], PI[:, :])
            S.ts(ANG[:, :], TF[:, :], cs("invf"), None, ALU.mult)
            reduce_angle(ANG[:, :], ANG[:, :], TF[:, :], PI[:, :])
            S.act(SINF[:, :], ANG[:, :], AF.Sin, scale=cs("sgn"))
            S.act(COSF[:, :], ANG[:, :], AF.Sin, scale=0.5)
            S.tt(COSF[:, :], COSF[:, :], COSF[:, :], ALU.mult)
            S.ts(COSF[:, :], COSF[:, :], -2.0, 1.0, ALU.mult, ALU.add)
            for fc in range(12 if 'proj' in PARTS else 0):
                P = bank()
                for dc in range(8):
                    S.mm(P, WIN[:, dc, 128 * fc:128 * (fc + 1)], HT[:, dc, :], start=(dc == 0), stop=(dc == 7))
                if fc < 4:
                    S.act(U[:, fc, tok], P, AF.Copy)
                else:
                    dst = Q[:, fc - 4, tok] if fc < 8 else K[:, fc - 8, tok]
                    S.act(QB[:, :], P, AF.Copy)
                    PR = bank()
                    S.mm(PR, ROTM, QB[:, :])
                    S.tt(T1[:, :], P, COSF[:, :], ALU.mult)
                    S.tt(T2[:, :], PR, SINF[:, :], ALU.mult)
                    S.tt(dst, T1[:, :], T2[:, :], ALU.add, eng="pool")
            for s in range(4 if 'v' in PARTS else 0):
                P = bank()
                for dc in range(8):
                    S.mm(P, HT[:, dc, 128 * s:128 * (s + 1)], WIN[:, dc, 1536:2048], start=(dc == 0), stop=(dc == 7))
                S.act(V[:, :, 4 * tt + s, :], P.rearrange("p (h d) -> p h d", h=4), AF.Copy)
        dbg("U", U[:, :, :]); dbg("Q", Q[:, :, :]); dbg("K", K[:, :, :]); dbg("V", V[:, :, :, :])
        A.release(m_layer)
        if stop and (stop == "A" or stop.startswith("A:")):
            return

        for h in range(4):
            S.dma(agin[h].ap()[:, 0:2048], K[:, h, :])
            S.dma(agin[h].ap()[:, 2048:4096], V[:, h, :, :].rearrange("p a b -> p (a b)"))
            S.allgather(agout[h].ap().opt(), agin[h].ap().opt())

        AR = A.alloc([128, 17, 16], F32)
        AI = A.alloc([128, 17, 16], F32)
        NAI = A.alloc([128, 17, 16], F32)
        SLI = A.alloc([128, 16], F32)
        RHO = A.alloc([128, 16], F32)
        MT = A.alloc([128, 16], F32)
        CT128 = A.alloc([128, 16], F32)
        ST128 = A.alloc([128, 16], F32)
        BR = A.alloc([128, 512], F32)
        BI = A.alloc([128, 512], F32)
        CB = A.alloc([128, 2, 512], BF16)
        XRE = A.alloc([128, 16, NB], F32)
        XIM = A.alloc([128, 16, NB], F32)
        EL = A.alloc([128, 2, 16], F32)
        m_ssm = A.mark()
        if True:
            CRE = A.alloc([128, 512], F32)
            CIM = A.alloc([128, 512], F32)
            S.dma(CRE[:, :], lpb_d.ap()[li, :, 1024:1536])
            S.dma(CIM[:, :], lpb_d.ap()[li, :, 1536:2048])
            BRE = A.alloc([128, 512], F32)
            BIM = A.alloc([128, 512], F32)
            S.dma(BRE[:, :], lpb_d.ap()[li, :, 0:512])
            S.dma(BIM[:, :], lpb_d.ap()[li, :, 512:1024])
            STEP = A.alloc([128, 16], F32)
            SLR = A.alloc([128, 16], F32)
            W17 = [A.alloc([128, 17, 16], F32) for _ in range(4)]
            W17I = A.alloc([128, 17, 16], I32)
            S.act(STEP[:, :], lpc("logstep"), AF.Exp)
            S.tt(SLR[:, :], STEP[:, :], lpc("lamre"), ALU.mult)
            S.tt(SLI[:, :], STEP[:, :], lpc("lamim"), ALU.mult)
            S.act(RHO[:, :], SLR[:, :], AF.Exp, scale=float(R))
            S.act(MT[:, :], SLR[:, :], AF.Exp, scale=float(T))
            taub = cs("tau").unsqueeze(2).broadcast_to([128, 17, 16])
            S.tt(W17[0][:, :, :], SLR[:, :].unsqueeze(1).broadcast_to([128, 17, 16]), taub, ALU.mult)
            S.act(W17[0][:, :, :], W17[0][:, :, :], AF.Exp)
            S.tt(W17[1][:, :, :], SLI[:, :].unsqueeze(1).broadcast_to([128, 17, 16]), taub, ALU.mult)
            reduce_angle(W17[1][:, :, :], W17[1][:, :, :], W17[2][:, :, :], W17I[:, :, :])
            S.act(W17[2][:, :, :], W17[1][:, :, :], AF.Sin)
            S.act(W17[3][:, :, :], W17[1][:, :, :], AF.Sin, scale=0.5)
            S.tt(W17[3][:, :, :], W17[3][:, :, :], W17[3][:, :, :], ALU.mult)
            S.ts(W17[3][:, :, :], W17[3][:, :, :], -2.0, 1.0, ALU.mult, ALU.add)
            S.tt(AR[:, :, :], W17[0][:, :, :], W17[3][:, :, :], ALU.mult)
            S.tt(AI[:, :, :], W17[0][:, :, :], W17[2][:, :, :], ALU.mult)
            S.ts(NAI[:, :, :], AI[:, :, :], -1.0, None, ALU.mult)
            den, am1, fr, fi, t0 = [A.alloc([128, 16], F32) for _ in range(5)]
            S.tt(den[:, :], lpc("lamre"), lpc("lamre"), ALU.mult)
            S.tt(t0[:, :], lpc("lamim"), lpc("lamim"), ALU.mult)
            S.tt(den[:, :], den[:, :], t0[:, :], ALU.add)
            S.recip(den[:, :], den[:, :])
            S.ts(am1[:, :], AR[:, 1, :], -1.0, None, ALU.add)
            S.tt(fr[:, :], am1[:, :], lpc("lamre"), ALU.mult)
            S.tt(t0[:, :], AI[:, 1, :], lpc("lamim"), ALU.mult)
            S.tt(fr[:, :], fr[:, :], t0[:, :], ALU.add)
            S.tt(fr[:, :], fr[:, :], den[:, :], ALU.mult)
            S.tt(fi[:, :], AI[:, 1, :], lpc("lamre"), ALU.mult)
            S.tt(t0[:, :], am1[:, :], lpc("lamim"), ALU.mult)
            S.tt(fi[:, :], fi[:, :], t0[:, :], ALU.subtract)
            S.tt(fi[:, :], fi[:, :], den[:, :], ALU.mult)
            frb = fr[:, :].unsqueeze(2).broadcast_to([128, 16, 32])
            fib = fi[:, :].unsqueeze(2).broadcast_to([128, 16, 32])
            v3 = lambda t: t[:, :].rearrange("p (k c) -> p k c", k=16)
            TB1 = A.alloc([128, 512], F32)
            S.tt(v3(BR), frb, v3(BRE), ALU.mult)
            S.tt(v3(TB1), fib, v3(BIM), ALU.mult)
            S.tt(BR[:, :], BR[:, :], TB1[:, :], ALU.subtract)
            S.tt(v3(BI), frb, v3(BIM), ALU.mult)
            S.tt(v3(TB1), fib, v3(BRE), ALU.mult)
            S.tt(BI[:, :], BI[:, :], TB1[:, :], ALU.add)
            S.copy(CB[:, 0, :], CRE[:, :])
            S.ts(CB[:, 1, :], CIM[:, :], -1.0, None, ALU.mult)
        A.release(m_ssm)

        def pair_tables(ct, CT, ST, tf, ti):
            S.tt(CT[:, :, :], SLI[:, 4 * ct:4 * ct + 4].unsqueeze(2).broadcast_to([128, 4, 129]),
                 cs("bv").unsqueeze(1).broadcast_to([128, 4, 129]), ALU.mult)
            reduce_angle(CT[:, :, :], CT[:, :, :], tf[:, :, :], ti[:, :, :])
            S.act(ST[:, :, :], CT[:, :, :], AF.Sin)
            S.act(CT[:, :, :], CT[:, :, :], AF.Sin, scale=0.5)
            S.tt(CT[:, :, :], CT[:, :, :], CT[:, :, :], ALU.mult)
            S.ts(CT[:, :, :], CT[:, :, :], -2.0, 1.0, ALU.mult, ALU.add)

        def make_WE(ct, WE, ta, tb):
            for half in range(2):
                ts_ = slice(8 * half, 8 * half + 8)
                arb = AR[:, ts_, 4 * ct:4 * ct + 4].unsqueeze(3).broadcast_to([128, 8, 4, 32])
                aib = AI[:, ts_, 4 * ct:4 * ct + 4].unsqueeze(3).broadcast_to([128, 8, 4, 32])
                brb = BR[:, 128 * ct:128 * ct + 128].rearrange("p (i c) -> p i c", i=4).unsqueeze(1).broadcast_to([128, 8, 4, 32])
                bib = BI[:, 128 * ct:128 * ct + 128].rearrange("p (i c) -> p i c", i=4).unsqueeze(1).broadcast_to([128, 8, 4, 32])
                v4 = lambda t: t[:, :].rearrange("p (a i c) -> p a i c", a=8, i=4)
                S.tt(v4(ta), arb, brb, ALU.mult)
                S.tt(v4(tb), aib, bib, ALU.mult)
                S.tt(WE[:, 0, ts_, :].rearrange("p a n -> p (a n)"), ta[:, :], tb[:, :], ALU.subtract, eng="pool")
                S.tt(v4(ta), arb, bib, ALU.mult)
                S.tt(v4(tb), aib, brb, ALU.mult)
                S.tt(WE[:, 1, ts_, :].rearrange("p a n -> p (a n)"), ta[:, :], tb[:, :], ALU.add, eng="pool")

        for ct in range(4):
            m = A.mark()
            WE = A.alloc([128, 2, 16, 128], BF16)
            WDT = A.alloc([128, 2, 16, 128], BF16)
            ta = A.alloc([128, 1024], F32)
            tb = A.alloc([128, 1024], F32)
            CT = A.alloc([128, 4, 129], F32)
            ST = A.alloc([128, 4, 129], F32)
            TI = S.sb([128, 4, 129], I32, S.tinfo[tb.name][1])
            WS = A.alloc([128, 2, 4, NB], F32)
            make_WE(ct, WE, ta, tb)
            pair_tables(ct, CT, ST, ta[:, 0:516].rearrange("p (i b) -> p i b", i=4), TI)
            S.copy(CT128[:, 4 * ct:4 * ct + 4], CT[:, :, 128])
            S.copy(ST128[:, 4 * ct:4 * ct + 4], ST[:, :, 128])
            for x in range(2):
                for tg in range(4):
                    P = bank()
                    for t4 in range(4):
                        S.mm(P[:, 128 * t4:128 * (t4 + 1)], WE[:, x, 4 * tg + t4, :], IDENT)
                    S.act(WDT[:, x, 4 * tg:4 * tg + 4, :].rearrange("p a n -> p (a n)"), P, AF.Copy)
            for x in range(2):
                for j in range(R):
                    for i in range(4):
                        S.mm(PS[:, 512 * (4 * x + i):512 * (4 * x + i) + 128], WDT[32 * i:32 * i + 32, x, R - 1 - j, :],
                             U[32 * i:32 * i + 32, ct, :].rearrange("p (b j) -> p j b", j=R)[:, j, :],
                             start=(j == 0), stop=(j == R - 1), tile_position=(32 * i, 0))
            cc = CT[:, :, 1:129]
            ss = ST[:, :, 1:129]
            pre = PS[:, 0:2048].rearrange("p (i c) -> p i c", i=4)[:, :, 0:128]
            pim = PS[:, 2048:4096].rearrange("p (i c) -> p i c", i=4)[:, :, 0:128]
            v3b = lambda t: t[:, 0:512].rearrange("p (i b) -> p i b", i=4)
            S.tt(v3b(ta), pre, cc, ALU.mult)
            S.tt(v3b(tb), pim, ss, ALU.mult)
            S.tt(XRE[:, 4 * ct:4 * ct + 4, :], v3b(ta), v3b(tb), ALU.add, eng="pool")
            S.tt(v3b(ta), pim, cc, ALU.mult)
            S.tt(v3b(tb), pre, ss, ALU.mult)
            S.tt(XIM[:, 4 * ct:4 * ct + 4, :], v3b(ta), v3b(tb), ALU.subtract, eng="pool")
            for i in range(4):
                k = 4 * ct + i
                S.scan(WS[:, 0, i, :], RHO[:, k:k + 1].broadcast_to([128, NB]), XRE[:, k, :], 0.0)
                S.scan(WS[:, 1, i, :], RHO[:, k:k + 1].broadcast_to([128, NB]), XIM[:, k, :], 0.0)
            e1 = ta[:, 0:4]
            e2 = ta[:, 4:8]
            S.tt(e1, WS[:, 0, :, NB - 1], CT[:, :, 128], ALU.mult)
            S.tt(e2, WS[:, 1, :, NB - 1], ST[:, :, 128], ALU.mult)
            S.tt(EL[:, 0, 4 * ct:4 * ct + 4], e1, e2, ALU.subtract)
            S.tt(e1, WS[:, 1, :, NB - 1], CT[:, :, 128], ALU.mult)
            S.tt(e2, WS[:, 0, :, NB - 1], ST[:, :, 128], ALU.mult)
            S.tt(EL[:, 1, 4 * ct:4 * ct + 4], e1, e2, ALU.add)
            A.release(m)
        if stop == "S1":
            dbg("XRE", XRE[:, :, :]); dbg("EL", EL[:, :, :])
            A.release(m_layer)
            return
        S.dma(agEin.ap(), EL[:, :, :].rearrange("p a b -> p (a b)"))
        S.allgather(agEout.ap().opt(), agEin.ap().opt())
        if stop == "AG":
            A.release(m_layer)
            return

        m_att = A.mark()
        KVP = A.alloc([128, 3, 4096], BF16)
        PT = [A.alloc([128, 512], BF16) for _ in range(4)]
        F1, F2, F3, F4 = [A.alloc([128, 512], F32) for _ in range(4)]
        SQB = A.alloc([128, 512], BF16)
        LAMT = A.alloc([128, 64], F32)
        LS = A.alloc([128, 4], F32)
        NLAM = A.alloc([128, 1], F32)
        GS = A.alloc([128, 1], F32)
        ZB = A.alloc([128, 1], F32)
        S.memset(ZB[:, :], 0.0)
        for c in range(2):
            o = LP["lamqk"][0] + 128 * c
            S.tt(LAMT[:, :], LPT[:, o:o + 64], LPT[:, o + 64:o + 128], ALU.mult)
            S.rsum(LS[:, c:c + 1], LAMT[:, :])
        S.act(LS[:, 2:4], LS[:, 0:2], AF.Exp)
        S.tt(NLAM[:, :], LS[:, 3:4], LS[:, 2:3], ALU.subtract)
        S.tt(NLAM[:, :], NLAM[:, :], lpc("nlaminit"), ALU.add)
        S.tt(GS[:, :], lpc("gsub"), lpc("oml"), ALU.mult)
        pctr = 0
        pre_emitted = False
        for h in range(4):
            for r in range(3):
                S.dma(KVP[:, r, :], agout[h].ap().rearrange("(r p) n -> p r n", p=128)[:, r, :])
            for qt in range(4):
                qs = slice(512 * qt, 512 * (qt + 1))
                steps = []
                for kt in range(4 * qt + 4):
                    steps.append((K[:, h, 128 * kt:128 * (kt + 1)], V[:, h, kt, :], ZB[:, 0:1],
                                  (kt - 4 * qt) if kt >= 4 * qt else None))
                for r in range(3):
                    for kt in range(16):
                        steps.append((KVP[:, r, 128 * kt:128 * (kt + 1)], KVP[:, r, 2048 + 128 * kt:2048 + 128 * (kt + 1)],
                                      cs("rankbias", r, r + 1), None))
                O1, O2, D1, D2 = bankn(4), bankn(5), bankn(6), bankn(7)
                def emit_scores(idx):
                    kt_ap = steps[idx][0]
                    S.mm(bankn((2 * idx) % 4), kt_ap[0:64, :], Q[0:64, h, qs])
                    S.mm(bankn((2 * idx + 1) % 4), kt_ap[64:128, :], Q[64:128, h, qs])

                if not pre_emitted:
                    emit_scores(0)
                for idx, (kt_ap, v_ap, b_ap, mi) in enumerate(steps):
                    first, last = idx == 0, idx == len(steps) - 1
                    S1 = bankn((2 * idx) % 4)
                    S2 = bankn((2 * idx + 1) % 4)
                    if not last:
                        emit_scores(idx + 1)
                    P1 = PT[pctr % 4]
                    P2 = PT[(pctr + 1) % 4]
                    pctr += 2
                    S.act(P1[:, :], S1, AF.Exp, bias=b_ap, scale=0.125)
                    S.act(P2[:, :], S2, AF.Exp, bias=b_ap, scale=0.125)
                    if mi is not None:
                        S.tt(P1[:, :], P1[:, :], MASK(mi), ALU.mult, eng="pool")
                        S.tt(P2[:, :], P2[:, :], MASK(mi), ALU.mult, eng="pool")
                    S.mm(O1, v_ap, P1[:, :], start=first, stop=last)
                    S.mm(D1, ONES[:, :], P1[:, :], start=first, stop=last)
                    S.mm(O2, v_ap, P2[:, :], start=first, stop=last)
                    S.mm(D2, ONES[:, :], P2[:, :], start=first, stop=last)
                nh, nqt = (h, qt + 1) if qt < 3 else (h + 1, 0)
                pre_emitted = nh < 4
                if pre_emitted:
                    nqs = slice(512 * nqt, 512 * (nqt + 1))
                    S.mm(bankn(0), K[0:64, nh, 0:128], Q[0:64, nh, nqs])
                    S.mm(bankn(1), K[64:128, nh, 0:128], Q[64:128, nh, nqs])
                S.recip(F1[:, :], D1)
                S.recip(F2[:, :], D2)
                S.tt(F1[:, :], O1, F1[:, :], ALU.mult)
                S.tt(F2[:, :], O2, F2[:, :], ALU.mult)
                S.stt(F3[:, :], F2[:, :], NLAM[:, 0:1], F1[:, :], ALU.mult, ALU.add)
                S.act(SQB[:, :], F3[:, :], AF.Square)
                PN = bankn(2)
                S.mm(PN, ONES[:, :], SQB[:, :])
                S.act(F4[:, :], PN, AF.Sqrt, bias=EPSC[:, 0:1], scale=1.0 / 128)
                S.recip(F4[:, :], F4[:, :])
                S.stt(Q[:, h, qs], F3[:, :], GS[:, 0:1], F4[:, :], ALU.mult, ALU.mult)
        dbg("ATT", Q[:, :, :])
        A.release(m_att)
        if stop == "ATT":
            A.release(m_layer)
            return

        m2 = A.mark()
        EA = A.alloc([128, 4, 2, 16], F32)
        S.dma(EA[:, :, :, :].rearrange("p r a b -> p r (a b)"), agEout.ap().rearrange("(r p) n -> p r n", p=128))
        ATR, ATI, SCR, SCI, SNR, SNI, SINR, SINI, c1, c2 = [A.alloc([128, 16], F32) for _ in range(10)]
        CRE = A.alloc([128, 512], F32)
        CIM = A.alloc([128, 512], F32)
        KD = A.alloc([128, 16, 128], BF16)
        S.dma(CRE[:, :], lpb_d.ap()[li, :, 1024:1536])
        S.dma(CIM[:, :], lpb_d.ap()[li, :, 1536:2048])
        S.memset(KD[:, :, :], 0.0, eng="pool")
        S.tt(ATR[:, :], MT[:, :], CT128[:, :], ALU.mult)
        S.tt(ATI[:, :], MT[:, :], ST128[:, :], ALU.mult)
        S.copy(SCR[:, :], EA[:, 0, 0, :])
        S.copy(SCI[:, :], EA[:, 0, 1, :])
        S.ts(SINR[:, :], SCR[:, :], cs("ohq", 1, 2), None, ALU.mult)
        S.ts(SINI[:, :], SCI[:, :], cs("ohq", 1, 2), None, ALU.mult)
        for q in (1, 2):
            S.tt(c1[:, :], ATR[:, :], SCR[:, :], ALU.mult)
            S.tt(c2[:, :], ATI[:, :], SCI[:, :], ALU.mult)
            S.tt(SNR[:, :], c1[:, :], c2[:, :], ALU.subtract)
            S.tt(SNR[:, :], SNR[:, :], EA[:, q, 0, :], ALU.add)
            S.tt(c1[:, :], ATR[:, :], SCI[:, :], ALU.mult)
            S.tt(c2[:, :], ATI[:, :], SCR[:, :], ALU.mult)
            S.tt(SNI[:, :], c1[:, :], c2[:, :], ALU.add)
            S.tt(SNI[:, :], SNI[:, :], EA[:, q, 1, :], ALU.add)
            S.copy(SCR[:, :], SNR[:, :])
            S.copy(SCI[:, :], SNI[:, :])
            S.stt(SINR[:, :], SCR[:, :], cs("ohq", q + 1, q + 2), SINR[:, :], ALU.mult, ALU.add)
            S.stt(SINI[:, :], SCI[:, :], cs("ohq", q + 1, q + 2), SINI[:, :], ALU.mult, ALU.add)
        for ct in range(4):
            m = A.mark()
            WE = A.alloc([128, 2, 16, 128], BF16)
            CAE = A.alloc([128, 2, 16, 128], BF16)
            ta = A.alloc([128, 1024], F32)
            tb = A.alloc([128, 1024], F32)
            CT = A.alloc([128, 4, 129], F32)
            ST = A.alloc([128, 4, 129], F32)
            TI = S.sb([128, 4, 129], I32, S.tinfo[tb.name][1])
            WB = A.alloc([128, 2, 4, 129], F32)
            SS = A.alloc([128, 2, 4, NB], BF16)
            YF = tb
            make_WE(ct, WE, ta, tb)
            pair_tables(ct, CT, ST, ta[:, 0:516].rearrange("p (i b) -> p i b", i=4), TI)
            for half in range(2):
                js = slice(8 * half + 1, 8 * half + 9)
                jo = slice(8 * half, 8 * half + 8)
                arb = AR[:, js, 4 * ct:4 * ct + 4].unsqueeze(3).broadcast_to([128, 8, 4, 32])
                aib = AI[:, js, 4 * ct:4 * ct + 4].unsqueeze(3).broadcast_to([128, 8, 4, 32])
                naib = NAI[:, js, 4 * ct:4 * ct + 4].unsqueeze(3).broadcast_to([128, 8, 4, 32])
                crb = CRE[:, 128 * ct:128 * ct + 128].rearrange("p (i c) -> p i c", i=4).unsqueeze(1).broadcast_to([128, 8, 4, 32])
                cib = CIM[:, 128 * ct:128 * ct + 128].rearrange("p (i c) -> p i c", i=4).unsqueeze(1).broadcast_to([128, 8, 4, 32])
                v4 = lambda t: t[:, :].rearrange("p (a i c) -> p a i c", a=8, i=4)
                S.tt(v4(ta), arb, crb, ALU.mult)
                S.tt(v4(tb), aib, cib, ALU.mult)
                S.tt(CAE[:, 0, jo, :].rearrange("p a n -> p (a n)"), ta[:, :], tb[:, :], ALU.subtract, eng="pool")
                S.tt(v4(ta), naib, crb, ALU.mult)
                S.tt(v4(tb), arb, cib, ALU.mult)
                S.tt(CAE[:, 1, jo, :].rearrange("p a n -> p (a n)"), ta[:, :], tb[:, :], ALU.subtract, eng="pool")
            PK = bank()
            for tau in range(R):
                for i in range(4):
                    k = 4 * ct + i
                    for x in range(2):
                        S.mm(PK[32 * i:32 * i + 32, 32 * tau:32 * tau + 32], WE[:, x, tau, 32 * i:32 * i + 32],
                             CB[:, x, 32 * k:32 * k + 32], start=(x == 0), stop=(x == 1), tile_position=(0, 32 * i))
            for i in range(4):
                S.act(KD[32 * i:32 * i + 32, :, 32 * i:32 * i + 32],
                      PK[32 * i:32 * i + 32, :].rearrange("p (t c) -> p t c", t=R), AF.Copy)
            for i in range(4):
                k = 4 * ct + i
                S.copy(WB[:, 0, i, 0:1], SINR[:, k:k + 1])
                S.copy(WB[:, 1, i, 0:1], SINI[:, k:k + 1])
                S.scan(WB[:, 0, i, 1:129], RHO[:, k:k + 1].broadcast_to([128, NB]), XRE[:, k, :], SINR[:, k:k + 1])
                S.scan(WB[:, 1, i, 1:129], RHO[:, k:k + 1].broadcast_to([128, NB]), XIM[:, k, :], SINI[:, k:k + 1])
            cc = CT[:, :, 0:128]
            ss = ST[:, :, 0:128]
            v3b = lambda t: t[:, 0:512].rearrange("p (i b) -> p i b", i=4)
            S.tt(v3b(ta), WB[:, 0, :, 0:128], cc, ALU.mult)
            S.tt(v3b(tb), WB[:, 1, :, 0:128], ss, ALU.mult)
            S.tt(SS[:, 0, :, :], v3b(ta), v3b(tb), ALU.subtract, eng="pool")
            S.tt(v3b(ta), WB[:, 1, :, 0:128], cc, ALU.mult)
            S.tt(v3b(tb), WB[:, 0, :, 0:128], ss, ALU.mult)
            S.tt(SS[:, 1, :, :], v3b(ta), v3b(tb), ALU.add, eng="pool")
            yb = 4 * (ct % 2)
            Y = PS[:, 512 * yb:512 * (yb + 4)].rearrange("p (j b) -> p j b", j=R)
            Uv = U[:, ct, :].rearrange("p (b j) -> p j b", j=R)
            for j in range(R):
                for j2 in range(j + 1):
                    S.mm(Y[:, j, :], KD[:, j - j2, :], Uv[:, j2, :], start=(j2 == 0), stop=False)
                for i in range(4):
                    for x in range(2):
                        S.mm(Y[32 * i:32 * i + 32, j, :], CAE[:, x, j, 32 * i:32 * i + 32], SS[:, x, i, :],
                             start=False, stop=(x == 1), tile_position=(0, 32 * i))
            for hb in range(2):
                bs = slice(64 * hb, 64 * hb + 64)
                ts_ = slice(1024 * hb, 1024 * hb + 1024)
                S.stt(YF[:, :].rearrange("p (b j) -> p j b", j=R), Uv[:, :, bs], lpc("dvec", ct, ct + 1), Y[:, :, bs],
                      ALU.mult, ALU.add)
                if "YSSM" in debug:
                    if "_y" not in dbg_out:
                        dbg_out["_y"] = S.dram("dbg_YSSM", [128, 4, T], F32, kind="ExternalOutput")
                    S.dma(dbg_out["_y"].ap()[:, ct, ts_], YF[:, :])
                S.act(ta[:, :], YF[:, :], AF.Square)
                S.ts(ta[:, :], ta[:, :], 0.044715, 1.0, ALU.mult, ALU.add)
                S.tt(ta[:, :], ta[:, :], YF[:, :], ALU.mult)
                S.act(ta[:, :], ta[:, :], AF.Sigmoid, scale=2.0 * math.sqrt(2.0 / math.pi))
                S.tt(U[:, ct, ts_], YF[:, :], ta[:, :], ALU.mult, eng="pool")
            A.release(m)
        A.release(m2)
        A.release(m_layer)
        if stop == "S2":
            return

        WG = A.alloc([128, 4, 1024], BF16)
        WO = A.alloc([128, 8, 1024], BF16)
        STG = A.alloc([128, 2048], F32)
        for hf in range(2):
            sv = STG[:, :].rearrange("p (k n) -> p k n", k=4)
            S.dma(sv, w_glu_d.ap()[li, hf])
            S.copy(WG[:, :, 512 * hf:512 * (hf + 1)], sv, eng="pool")
        for g in range(4):
            sv = STG[:, :].rearrange("p (k n) -> p k n", k=8)
            S.dma(sv, w_out_d.ap()[li, g])
            S.copy(WO[:, :, 256 * g:256 * (g + 1)], sv, eng="pool")
        GL = A.alloc([128, 4, 512], F32)
        SG = A.alloc([128, 512], F32)
        SQ = [A.alloc([128, 512], BF16) for _ in range(2)]
        RS = A.alloc([128, 512], F32)
        for tt in range(NT):
            tok = slice(512 * tt, 512 * (tt + 1))
            for oc in range(4):
                PA, PB = bank(), bank()
                for kc in range(4):
                    S.mm(PB, WG[:, kc, 512 + 128 * oc:512 + 128 * (oc + 1)], U[:, kc, tok], start=(kc == 0), stop=(kc == 3))
                for kc in range(4):
                    S.mm(PA, WG[:, kc, 128 * oc:128 * (oc + 1)], U[:, kc, tok], start=(kc == 0), stop=(kc == 3))
                S.act(SG[:, :], PB, AF.Sigmoid)
                S.tt(GL[:, oc, :], PA, SG[:, :], ALU.mult)
            rmsnorm_tile(lambda oc: U[:, oc, tok], lambda oc: GL[:, oc, :], 4, 512,
                         lambda oc: lpc("gssm", oc, oc + 1), 512, [SQ[0][:, :], SQ[1][:, :]], RS[:, :])
        dbg("SSM", U[:, :, :])
        for tt in range(NT):
            tok = slice(512 * tt, 512 * (tt + 1))
            for dc in range(8):
                P = bank()
                for kc in range(8):
                    src = U[:, kc, tok] if kc < 4 else Q[:, kc - 4, tok]
                    S.mm(P, WO[:, kc, 128 * dc:128 * (dc + 1)], src, start=(kc == 0), stop=(kc == 7))
                S.tt(X[:, dc, tok], X[:, dc, tok], P, ALU.add)
        dbg("XMID", X[:, :, :])
        A.release(m_layer)
        if stop == "OUT":
            return

        S.dma(ag2in.ap().rearrange("p (c t) -> p c t", c=8), X[:, :, T - 2:T])
        S.allgather(ag2out.ap().opt(), ag2in.ap().opt())
        H4 = A.alloc([128, 4, 16], F32)
        XH = A.alloc([128, 16], F32)
        HH = A.alloc([128, 8, 2], BF16)
        HSQ = A.alloc([128, 16], BF16)
        HRS = A.alloc([128, 2], F32)
        HTMP = A.alloc([128, 16], F32)
        CARRY = [A.alloc([128, 44, 2], F32) for _ in range(5)]
        S.dma(H4[:, :, :], ag2out.ap().rearrange("(r p) n -> p r n", p=128))
        S.ts(XH[:, :], H4[:, 0, :], cs("ohprev", 0, 1), None, ALU.mult)
        for r in range(1, 4):
            S.stt(XH[:, :], H4[:, r, :], cs("ohprev", r, r + 1), XH[:, :], ALU.mult, ALU.add)
        XHv = XH[:, :].rearrange("p (c t) -> p c t", c=8)
        S.act(HSQ[:, :], XH[:, :], AF.Square)
        PHn = bank()
        for dc in range(8):
            S.mm(PHn[:, 0:2], ONES[:, :], HSQ[:, 2 * dc:2 * dc + 2], start=(dc == 0), stop=(dc == 7))
        S.act(HRS[:, :], PHn[:, 0:2], AF.Sqrt, bias=EPSC[:, 0:1], scale=1.0 / D)
        S.recip(HRS[:, :], HRS[:, :])
        S.tt(HTMP[:, :].rearrange("p (c t) -> p c t", c=8), XHv, HRS[:, :].unsqueeze(1).broadcast_to([128, 8, 2]), ALU.mult)
        S.tt(HH[:, :, :], HTMP[:, :].rearrange("p (c t) -> p c t", c=8),
             lpc("gffn").unsqueeze(2).broadcast_to([128, 8, 2]), ALU.mult)
        A2.release(A2.lo)
        HTF = A2.alloc([128, 8, 1024], BF16)
        ACTT = A2.alloc([128, NFC, 1024], BF16)
        SQ = [A.alloc([128, 512], BF16) for _ in range(2)]
        RS = A.alloc([128, 512], F32)
        STU = [A.alloc([128, 8, 256], F32) for _ in range(2)]
        WUB = [A.alloc([128, 8, 256], BF16) for _ in range(2)]
        STD = [A.alloc([128, 11, 128], F32) for _ in range(2)]
        WDB = [A.alloc([128, NFC, 128], BF16) for _ in range(2)]
        ACC = [A.alloc([128, 512], F32) for _ in range(6)]
        actr = [0]
        SGT = A.alloc([128, 512], F32)
        wctr = 0
        for hf in range(2):
            for t2 in range(2):
                tok = slice(1024 * hf + 512 * t2, 1024 * hf + 512 * (t2 + 1))
                rmsnorm_tile(lambda dc: HTF[:, dc, 512 * t2:512 * (t2 + 1)], lambda dc: X[:, dc, tok], 8, 512,
                             lambda dc: lpc("gffn", dc, dc + 1), D, [SQ[0][:, :], SQ[1][:, :]], RS[:, :])
            for fc in range(NFC):
                b = wctr % 2
                wctr += 1
                S.dma(STU[b][:, :, :], w_up_d.ap()[li, fc])
                S.copy(WUB[b][:, :, :], STU[b][:, :, :], eng="pool")
                for gv in range(2):
                    c = gv * NFC + fc
                    if hf == 0:
                        PH = bank()
                        for dc in range(8):
                            S.mm(PH[:, 0:2], WUB[b][:, dc, 128 * gv:128 * (gv + 1)], HH[:, dc, :], start=(dc == 0), stop=(dc == 7))
                        S.act(CARRY[0][:, c, :], PH[:, 0:2], AF.Copy)
                    for t2 in range(2):
                        gt = 2 * hf + t2
                        P = bank()
                        for dc in range(8):
                            S.mm(P, WUB[b][:, dc, 128 * gv:128 * (gv + 1)], HTF[:, dc, 512 * t2:512 * (t2 + 1)],
                                 start=(dc == 0), stop=(dc == 7))
                        acc = ACC[actr[0] % 6]
                        actr[0] += 1
                        w0 = lpc("wconv", c, c + 1)
                        w1 = lpc("wconv", 44 + c, 44 + c + 1)
                        w2 = lpc("wconv", 88 + c, 88 + c + 1)
                        if t2 == 1:
                            pass
                        S.act(acc[:, :], P, AF.Identity, bias=lpc("bconv", c, c + 1), scale=w2)
                        S.stt(acc[:, 1:512], P[:, 0:511], w1, acc[:, 1:512], ALU.mult, ALU.add)
                        S.stt(acc[:, 2:512], P[:, 0:510], w0, acc[:, 2:512], ALU.mult, ALU.add)
                        S.stt(acc[:, 0:2], CARRY[gt][:, c, 0:2], w0, acc[:, 0:2], ALU.mult, ALU.add)
                        S.stt(acc[:, 0:1], CARRY[gt][:, c, 1:2], w1, acc[:, 0:1], ALU.mult, ALU.add)
                        S.act(CARRY[gt + 1][:, c, :], P[:, 510:512], AF.Copy)
                        if gv == 0:
                            S.act(ACTT[:, fc, 512 * t2:512 * (t2 + 1)], acc[:, :], AF.Silu)
                        else:
                            S.tt(ACTT[:, fc, 512 * t2:512 * (t2 + 1)], ACTT[:, fc, 512 * t2:512 * (t2 + 1)], acc[:, :], ALU.mult)
            for dc in range(8):
                b = dc % 2
                for h2 in range(2):
                    S.dma(STD[h2][:, :, :], w_dn_d.ap()[li, dc, h2])
                    S.copy(WDB[b][:, 11 * h2:11 * h2 + 11, :], STD[h2][:, :, :], eng="pool")
                for t2 in range(2):
                    tok = slice(1024 * hf + 512 * t2, 1024 * hf + 512 * (t2 + 1))
                    P = bank()
                    for fc in range(NFC):
                        S.mm(P, WDB[b][:, fc, :], ACTT[:, fc, 512 * t2:512 * (t2 + 1)], start=(fc == 0), stop=(fc == NFC - 1))
                    S.tt(X[:, dc, tok], X[:, dc, tok], P, ALU.add)
        dbg("XOUT", X[:, :, :])
        A.release(m_layer)

    for li in range(L):
        layer(li)

    if final:
        OT = A.alloc([128, 8, 512], F32)
        SQ = [A.alloc([128, 512], BF16) for _ in range(2)]
        RS = A.alloc([128, 512], F32)
        for tt in range(NT):
            tok = slice(512 * tt, 512 * (tt + 1))
            rmsnorm_tile(lambda dc: OT[:, dc, :], lambda dc: X[:, dc, tok], 8, 512,
                         lambda dc: lpc("gfin", dc, dc + 1), D, [SQ[0][:, :], SQ[1][:, :]], RS[:, :])
            S.dma(out_d.ap()[:, :, tok], OT[:, :, :])
    else:
        for q in range(4):
            S.dma(out_d.ap()[:, 2 * q:2 * q + 2, :], X[:, 2 * q:2 * q + 2, :])
    S.finish()
    S.build()
    return nc


def _consts(core):
    qi = core % 4
    c = np.zeros((128, NCS), np.float32)
    inv_freq = (500000.0 ** (-(np.arange(0, 16, 2, dtype=np.float32) / 16.0))).astype(np.float32)
    for p in range(128):
        d = p % 64
        if d < 16:
            c[p, CS["invf"][0]] = inv_freq[d % 8]
            c[p, CS["sgn"][0]] = -1.0 if d < 8 else 1.0
    c[:, CS["tau"][0]:CS["tau"][0] + 17] = np.arange(17, dtype=np.float32)[None]
    c[:, CS["bv"][0]:CS["bv"][0] + 129] = (R * np.arange(129, dtype=np.float32))[None]
    for r in range(3):
        c[:, CS["rankbias"][0] + r] = 0.0 if r < qi else -30000.0
    c[:, CS["ohq"][0] + qi] = 1.0
    if qi > 0:
        c[:, CS["ohprev"][0] + qi - 1] = 1.0
    return c


def _cbf():
    c = np.zeros((128, 256 + 2048), np.float32)
    c[:, 0:128] = np.eye(128, dtype=np.float32)
    for m in range(128):
        d = m % 64
        if d < 8:
            c[m + 8, 128 + m] = 1.0
        elif d < 16:
            c[m - 8, 128 + m] = 1.0
    kk = np.arange(128)[:, None]
    qq = np.arange(512)[None, :]
    for i in range(4):
        c[:, 256 + 512 * i:256 + 512 * (i + 1)] = ((128 * i + kk) // 64 <= qq // 64).astype(np.float32)
    return c.astype(ml_dtypes.bfloat16)


def _layer_params(inp, l):
    lp = np.zeros((128, NLP), np.float32)

    def put(name, arr):
        o, w = LP[name]
        lp[:, o:o + w] = arr

    put("gmix", inp["norm_mix"][l].reshape(8, 128).T)
    put("gffn", inp["norm_ffn"][l].reshape(8, 128).T)
    put("gssm", inp["ssm_norm"][l].reshape(4, 128).T)
    put("gsub", inp["attn_subln"][l].reshape(128, 1))
    put("dvec", inp["ssm_d"][l].reshape(4, 128).T)
    pl = lambda a: a.reshape(16, 2, 64).transpose(1, 2, 0).reshape(128, 16)
    put("lamre", pl(inp["ssm_lambda_re"][l]))
    put("lamim", pl(inp["ssm_lambda_im"][l]))
    put("logstep", pl(np.repeat(inp["ssm_log_step"][l][:, None], 64, axis=1)))
    put("wconv", inp["w_conv"][l].reshape(3, 44, 128).transpose(2, 0, 1).reshape(128, 132))
    put("bconv", inp["b_conv"][l].reshape(44, 128).T)
    lam = np.concatenate([inp["lambda_q1"][l], inp["lambda_k1"][l], inp["lambda_q2"][l], inp["lambda_k2"][l]])
    put("lamqk", np.repeat(lam[None, :], 128, axis=0))
    put("gfin", inp["norm_final"].reshape(8, 128).T)
    lam_init = 0.8 - 0.6 * math.exp(-0.3 * l)
    put("nlaminit", np.full((128, 1), -lam_init, np.float32))
    put("oml", np.full((128, 1), 1.0 - lam_init, np.float32))
    lpb = np.zeros((128, 4, 16, 2, 16), np.float32)
    b_re = inp["ssm_b_re"][l].reshape(16, 2, 64, 16)
    b_im = inp["ssm_b_im"][l].reshape(16, 2, 64, 16)
    c_re = inp["ssm_c_re"][l].reshape(16, 2, 16, 64)
    c_im = inp["ssm_c_im"][l].reshape(16, 2, 16, 64)
    for g2 in range(2):
        rows = slice(64 * g2, 64 * g2 + 64)
        lpb[rows, 0, :, g2, :] = b_re[:, g2].transpose(1, 0, 2)
        lpb[rows, 1, :, g2, :] = b_im[:, g2].transpose(1, 0, 2)
        lpb[rows, 2, :, g2, :] = c_re[:, g2].transpose(2, 0, 1)
        lpb[rows, 3, :, g2, :] = c_im[:, g2].transpose(2, 0, 1)
    return lp, lpb.reshape(128, 2048)


def _layer_weights(inp, l):
    w_in = inp["w_in"][l].reshape(8, 128, 8, 256).transpose(2, 1, 0, 3)
    w_glu = inp["ssm_w_glu"][l].reshape(4, 128, 2, 512).transpose(2, 1, 0, 3)
    w_out = inp["w_out"][l].reshape(8, 128, 4, 256).transpose(2, 1, 0, 3)
    w_up = inp["w_up"][l].reshape(8, 128, 2, NFC, 128).transpose(3, 1, 0, 2, 4).reshape(NFC, 128, 8, 256)
    w_dn = inp["w_down"][l].reshape(2, 11, 128, 8, 128).transpose(3, 0, 2, 1, 4)
    return {k: np.ascontiguousarray(v, dtype=np.float32) for k, v in
            (("w_in", w_in), ("w_glu", w_glu), ("w_out", w_out), ("w_up", w_up), ("w_dn", w_dn))}


_PROG = {}


def _get_prog(L, final):
    key = (L, final)
    if key not in _PROG:
        _PROG[key] = build_program(L, final)
    return _PROG[key]


FUSED = True


def kernel(**inp):
    inp = {k: np.asarray(v) for k, v in inp.items()}
    x = inp["x"].astype(np.float32)
    xs = []
    for c in range(8):
        b, qi = c // 4, c % 4
        xs.append(np.ascontiguousarray(x[b, qi * T:(qi + 1) * T, :].T.reshape(8, 128, T).transpose(1, 0, 2)))
    poss = [np.ascontiguousarray(inp["positions"][c // 4, (c % 4) * T:(c % 4 + 1) * T].astype(np.int32)[None]) for c in range(8)]
    csts = [_consts(c) for c in range(8)]
    cbf = _cbf()
    lps = [_layer_params(inp, l) for l in range(DEPTH)]
    wts = [_layer_weights(inp, l) for l in range(DEPTH)]

    def stack(ls):
        maps = {"lp": np.stack([lps[l][0] for l in ls]), "lpb": np.stack([lps[l][1] for l in ls])}
        for k in ("w_in", "w_glu", "w_out", "w_up", "w_dn"):
            maps[k] = np.stack([wts[l][k] for l in ls])
        return maps

    if FUSED:
        launches = [(list(range(DEPTH)), True)]
    else:
        launches = [([l], l == DEPTH - 1) for l in range(DEPTH)]
    for ls, final in launches:
        nc = _get_prog(len(ls), final)
        shared = stack(ls)
        in_maps = []
        for c in range(8):
            m = {"x_in": xs[c], "pos": poss[c], "cst": csts[c], "cbf": cbf}
            m.update(shared)
            in_maps.append(m)
        res = run_bass_kernel_spmd(nc, in_maps, core_ids=list(range(8)))
        xs = [np.asarray(r["out"], dtype=np.float32) for r in res.results]
    out = np.zeros((2, 8192, D), np.float32)
    for c in range(8):
        b, qi = c // 4, c % 4
        out[b, qi * T:(qi + 1) * T, :] = xs[c].transpose(1, 0, 2).reshape(D, T).T
    return out
```

```python
import math
from contextlib import ExitStack

import numpy as np
import ml_dtypes
import concourse.bass as bass
import concourse.mybir as mybir
from concourse.bass_utils import run_bass_kernel_spmd

F32 = mybir.dt.float32
BF16 = mybir.dt.bfloat16
I32 = mybir.dt.int32
ALU = mybir.AluOpType
AF = mybir.ActivationFunctionType
_ESZ = {F32: 4, BF16: 2, I32: 4}

D = 1024
T = 2048
NT = 4
DEPTH = 4
FFN = 2816
NFC = 22
R = 16
NB = T // R
EPS = 1e-6
TWO_PI = 2.0 * math.pi
C1 = 6.28125
C2 = TWO_PI - C1
SB_LO = 16512
SB_HI = 229344
GROUPS = [[0, 1, 2, 3], [4, 5, 6, 7]]


class Sched:
    ENGS = ("pe", "act", "dve", "pool", "sp")
    BK = 2048

    def __init__(self, nc, n_dma_sems=32):
        self.nc = nc
        self.ops = {e: [] for e in self.ENGS}
        self.tinfo = {}
        self.recs = {}
        self.buckets = {}
        self.known = {e: {} for e in self.ENGS}
        self.known_dma = {e: {} for e in self.ENGS}
        self.snap = {}
        self.targets = {e: set() for e in self.ENGS}
        self.n_dma = 0
        self.n_dma_sems = n_dma_sems
        self.n_cc = 0
        self.uid = 0

    def sb(self, shape, dtype, offset):
        self.uid += 1
        h = self.nc.alloc_sbuf_tensor_at("t%d" % self.uid, list(shape), dtype, offset=offset)
        self.tinfo[h.name] = ("sb", offset, int(np.prod(shape[1:])) * _ESZ[dtype])
        return h

    def ps(self, name, shape, dtype=F32):
        h = self.nc.alloc_psum_tensor(name, list(shape), dtype)
        self.tinfo[h.name] = ("ps", 0, int(np.prod(shape[1:])) * _ESZ[dtype])
        return h

    def dram(self, name, shape, dtype, kind="Internal"):
        h = self.nc.dram_tensor(name, list(shape), dtype, kind=kind)
        self.tinfo[h.name] = ("dr:" + name, 0, None)
        return h

    def region(self, ap):
        space, base, psb = self.tinfo[ap.tensor.name]
        esz = _ESZ[ap.dtype]
        aps = ap.ap
        off = int(ap.offset) * esz
        if psb is None:
            span = sum((c - 1) * abs(s) for s, c in aps) * esz
            return (space, off, off + span + esz, 0, 1)
        p0 = off // psb
        fo = off % psb
        span = sum((c - 1) * abs(s) for s, c in aps[1:]) * esz
        pstep, pcnt = aps[0]
        nstep = max(1, (pstep * esz) // psb) if pstep else 1
        return (space, base + fo, base + fo + span + esz, p0, p0 + (pcnt - 1) * nstep + 1)

    def _bk(self, rg):
        if rg[0][0] == "d":
            return range(0, 1)
        return range(rg[1] // self.BK, (rg[2] - 1) // self.BK + 1)

    def add(self, eng, emit, reads=(), writes=(), kind="cmp"):
        rr = [self.region(a) for a in reads if a is not None and hasattr(a, "tensor")]
        ww = [self.region(a) for a in writes if a is not None]
        pa = [(g[0], g[1] // 2048 * 2048, ((g[2] - 1) // 2048 + 1) * 2048, 0, 128) for g in rr + ww if g[0] == "ps"]
        rr = [g for g in rr if g[0] != "ps"]
        ww = [g for g in ww if g[0] != "ps"] + pa
        seq = len(self.ops[eng])
        op = {"eng": eng, "emit": emit, "kind": kind, "seq": seq, "waits": [], "dmawaits": [], "sem": None}
        if kind == "dma":
            i = self.n_dma
            self.n_dma += 1
            P = self.n_dma_sems
            op["sem"] = ("d", i % P, 16 * (i // P + 1))
            if i >= P:
                self._need_dma(op, ("d", i % P, 16 * (i // P)))
        elif kind == "cc":
            i = self.n_cc
            self.n_cc += 1
            op["sem"] = ("cc", i, 1)
        wid = ("E", eng, seq) if kind == "cmp" else ("D",) + op["sem"]
        for rg in rr:
            for key in self._overlaps(rg):
                w = self.recs[key][0]
                if w is not None:
                    self._need(op, w)
        for rg in ww:
            for key in list(self._overlaps(rg)):
                rec = self.recs[key]
                if rec[0] is not None:
                    self._need(op, rec[0])
                for e2, s2 in rec[1].items():
                    self._need(op, ("E", e2, s2))
                for d in rec[2]:
                    self._need(op, d)
                if key[1] >= rg[1] and key[2] <= rg[2] and key[3] >= rg[3] and key[4] <= rg[4]:
                    self._del(key)
        for rg in rr:
            rec = self._get(rg)
            if kind == "cmp":
                rec[1][eng] = seq
            else:
                rec[2].append(wid)
        for rg in ww:
            rec = self._get(rg)
            rec[0] = wid
            rec[1] = {}
            rec[2] = []
        if kind == "cmp":
            self.snap[(eng, seq)] = dict(self.known[eng])
        self.ops[eng].append(op)
        return op

    def _get(self, rg):
        rec = self.recs.get(rg)
        if rec is None:
            rec = [None, {}, []]
            self.recs[rg] = rec
            for b in self._bk(rg):
                self.buckets.setdefault((rg[0], b), set()).add(rg)
        return rec

    def _del(self, key):
        del self.recs[key]
        for b in self._bk(key):
            self.buckets[(key[0], b)].discard(key)

    def _overlaps(self, rg):
        out = set()
        for b in self._bk(rg):
            for key in self.buckets.get((rg[0], b), ()):
                if key[1] < rg[2] and rg[1] < key[2] and key[3] < rg[4] and rg[3] < key[4]:
                    out.add(key)
        return out

    def _need(self, op, w):
        if w[0] == "E":
            self._need_eng(op, w[1], w[2])
        else:
            self._need_dma(op, w[1:])

    def _need_eng(self, op, f, s):
        e = op["eng"]
        if e == f and e == "pe":
            return
        if e == f and s >= op["seq"]:
            return
        kn = self.known[e]
        if kn.get(f, -1) >= s:
            return
        kn[f] = s
        for f2, s2 in self.snap.get((f, s), {}).items():
            if f2 != e and kn.get(f2, -1) < s2:
                kn[f2] = s2
        op["waits"].append((f, s))
        self.targets[f].add(s)

    def _need_dma(self, op, d):
        kd = self.known_dma[op["eng"]]
        key = (d[0], d[1])
        if kd.get(key, 0) >= d[2]:
            return
        kd[key] = d[2]
        op["dmawaits"].append(d)

    def finish(self):
        op = {"eng": "sp", "emit": lambda e: e.nop(), "kind": "cmp", "seq": len(self.ops["sp"]),
              "waits": [], "dmawaits": [], "sem": None}
        for e in self.ENGS:
            if e != "sp" and self.ops[e]:
                self._need_eng(op, e, len(self.ops[e]) - 1)
        P = self.n_dma_sems
        for i in range(max(0, self.n_dma - P), self.n_dma):
            self._need_dma(op, ("d", i % P, 16 * (i // P + 1)))
        for i in range(self.n_cc):
            self._need_dma(op, ("cc", i, 1))
        self.ops["sp"].append(op)

    def build(self):
        nc = self.nc
        with ExitStack() as es:
            sem_e = {e: es.enter_context(nc.semaphore("s_" + e)) for e in self.ENGS}
            sem_d = {}
            for i in range(self.n_dma_sems):
                sem_d[("d", i)] = es.enter_context(nc.semaphore("d%d" % i))
            for i in range(self.n_cc):
                sem_d[("cc", i)] = es.enter_context(nc.semaphore("cc%d" % i))
            rank = {}
            for e in self.ENGS:
                for r, s in enumerate(sorted(self.targets[e])):
                    rank[(e, s)] = r + 1
            block = es.enter_context(nc.Block())
            handles = {"pe": block.tensor, "act": block.scalar, "dve": block.vector,
                       "pool": block.gpsimd, "sp": block.sync}
            for e in self.ENGS:
                ops = self.ops[e]
                if not ops:
                    continue

                def run(eng, ops=ops, e=e):
                    for op in ops:
                        for f, s in op["waits"]:
                            eng.wait_ge(sem_e[f], rank[(f, s)])
                        for d in op["dmawaits"]:
                            eng.wait_ge(sem_d[(d[0], d[1])], d[2])
                        ins = op["emit"](eng)
                        if op["kind"] == "dma":
                            ins.then_inc(sem_d[(op["sem"][0], op["sem"][1])], 16)
                        elif op["kind"] == "cc":
                            ins.then_inc(sem_d[(op["sem"][0], op["sem"][1])])
                        elif (e, op["seq"]) in rank:
                            ins.then_inc(sem_e[e], 1)
                handles[e](run)

    def mm(self, out, lhsT, rhs, start=True, stop=True, **kw):
        return self.add("pe", lambda e: e.matmul(out, lhsT, rhs, start=start, stop=stop, **kw), [lhsT, rhs], [out])

    def act(self, out, in_, func, bias=0.0, scale=1.0):
        return self.add("act", lambda e: e.activation(out, in_, func, bias=bias, scale=scale), [in_, bias, scale], [out])

    def tt(self, out, in0, in1, op, eng="dve"):
        return self.add(eng, lambda e: e.tensor_tensor(out, in0, in1, op), [in0, in1], [out])

    def ts(self, out, in0, s1, s2=None, op0=ALU.mult, op1=None, eng="dve"):
        if op1 is None:
            return self.add(eng, lambda e: e.tensor_scalar(out, in0, s1, None, op0), [in0, s1], [out])
        return self.add(eng, lambda e: e.tensor_scalar(out, in0, s1, s2, op0, op1), [in0, s1, s2], [out])

    def stt(self, out, in0, scalar, in1, op0, op1, eng="dve"):
        return self.add(eng, lambda e: e.scalar_tensor_tensor(out, in0, scalar, in1, op0, op1), [in0, scalar, in1], [out])

    def copy(self, out, in_, eng="dve"):
        return self.add(eng, lambda e: e.tensor_copy(out, in_), [in_], [out])

    def memset(self, out, val, eng="dve"):
        return self.add(eng, lambda e: e.memset(out, val), [], [out])

    def recip(self, out, in_):
        return self.add("dve", lambda e: e.reciprocal(out, in_), [in_], [out])

    def rsum(self, out, in_):
        return self.add("dve", lambda e: e.reduce_sum(out, in_, mybir.AxisListType.X), [in_], [out])

    def scan(self, out, d0, d1, init):
        return self.add("dve", lambda e: e.tensor_tensor_scan(out, d0, d1, init, ALU.mult, ALU.add), [d0, d1, init], [out])

    def dma(self, out, in_):
        return self.add("sp", lambda e: e.dma_start(out=out, in_=in_), [in_], [out], kind="dma")

    def allgather(self, out, in_):
        return self.add("pool", lambda e: e.collective_compute(
            "AllGather", ALU.bypass, replica_groups=GROUPS, ins=[in_], outs=[out]), [in_], [out], kind="cc")


class Arena:
    def __init__(self, S, lo, hi):
        self.S, self.lo, self.hi, self.top = S, lo, hi, lo

    def alloc(self, shape, dt):
        nb = int(np.prod(shape[1:])) * _ESZ[dt]
        off = (self.top + 63) // 64 * 64
        assert off + nb <= self.hi, ("SBUF overflow", shape, off + nb - self.hi)
        self.top = off + nb
        return self.S.sb(shape, dt, off)

    def mark(self):
        return self.top

    def release(self, m):
        self.top = m


LP = {}
_o = 0
for _n, _w in (("gmix", 8), ("gffn", 8), ("gssm", 4), ("gsub", 1), ("dvec", 4), ("lamre", 16), ("lamim", 16),
               ("logstep", 16), ("wconv", 132), ("bconv", 44), ("lamqk", 256), ("gfin", 8), ("nlaminit", 1), ("oml", 1)):
    LP[_n] = (_o, _w)
    _o += _w
NLP = _o
CS = {}
_o = 0
for _n, _w in (("invf", 1), ("sgn", 1), ("tau", 17), ("bv", 129), ("rankbias", 3), ("ohq", 4), ("ohprev", 4)):
    CS[_n] = (_o, _w)
    _o += _w
NCS = _o
AGW = 16384


def build_program(L, final, debug=(), stop=None):
    nc = bass.Bass("TRN2", target_bir_lowering=False)
    S = Sched(nc)
    A = Arena(S, SB_LO, SB_HI)
    dbg_out = {}

    x_in = S.dram("x_in", [128, 8, T], F32, kind="ExternalInput")
    pos = S.dram("pos", [1, T], I32, kind="ExternalInput")
    cst_d = S.dram("cst", [128, NCS], F32, kind="ExternalInput")
    cbf_d = S.dram("cbf", [128, 256 + 2048], BF16, kind="ExternalInput")
    lp_d = S.dram("lp", [L, 128, NLP], F32, kind="ExternalInput")
    lpb_d = S.dram("lpb", [L, 128, 2048], F32, kind="ExternalInput")
    w_in_d = S.dram("w_in", [L, 8, 128, 8, 256], F32, kind="ExternalInput")
    w_glu_d = S.dram("w_glu", [L, 2, 128, 4, 512], F32, kind="ExternalInput")
    w_out_d = S.dram("w_out", [L, 4, 128, 8, 256], F32, kind="ExternalInput")
    w_up_d = S.dram("w_up", [L, NFC, 128, 8, 256], F32, kind="ExternalInput")
    w_dn_d = S.dram("w_dn", [L, 8, 2, 128, 11, 128], F32, kind="ExternalInput")
    out_d = S.dram("out", [128, 8, T], F32, kind="ExternalOutput")
    agin = [S.dram("agin%d" % h, [128, 4096], BF16) for h in range(4)]
    agout = [S.dram("agout%d" % h, [512, 4096], BF16) for h in range(4)]
    agEin = S.dram("agEin", [128, 32], F32)
    agEout = S.dram("agEout", [512, 32], F32)
    ag2in = S.dram("ag2in", [128, 16], F32)
    ag2out = S.dram("ag2out", [512, 16], F32)

    def dbg(name, ap):
        if name not in debug:
            return
        t = S.dram("dbg_" + name, list(ap.shape), ap.dtype, kind="ExternalOutput")
        S.dma(t.ap(), ap)
        dbg_out[name] = True

    X = A.alloc([128, 8, T], F32)
    uqkv_lo = (A.top + 63) // 64 * 64
    U = A.alloc([128, 4, T], BF16)
    Q = A.alloc([128, 4, T], BF16)
    K = A.alloc([128, 4, T], BF16)
    V = A.alloc([128, 4, 16, 128], BF16)
    A2 = Arena(S, uqkv_lo, A.top)
    CST = A.alloc([128, NCS], F32)
    CBF = A.alloc([128, 256 + 2048], BF16)
    ONES = A.alloc([128, 128], BF16)
    EPSC = A.alloc([128, 1], F32)
    LPT = A.alloc([128, NLP], F32)
    PS = S.ps("psum", [128, 4096], F32)

    def cs(name, a=0, b=None):
        o, w = CS[name]
        return CST[:, o + a:o + (w if b is None else b)]

    def lpc(name, a=0, b=None):
        o, w = LP[name]
        return LPT[:, o + a:o + (w if b is None else b)]

    IDENT = CBF[:, 0:128]
    ROTM = CBF[:, 128:256]

    def MASK(i):
        return CBF[:, 256 + 512 * i:256 + 512 * (i + 1)]

    bank_ctr = [0]

    def bank(n=1):
        b = bank_ctr[0] % 8
        bank_ctr[0] += 1
        return PS[:, 512 * b:512 * (b + 1)]

    def bankn(b):
        return PS[:, 512 * b:512 * (b + 1)]

    S.dma(CST[:, :], cst_d.ap())
    S.dma(CBF[:, :], cbf_d.ap())
    for q in range(4):
        S.dma(X[:, 2 * q:2 * q + 2, :], x_in.ap()[:, 2 * q:2 * q + 2, :])
    S.memset(ONES[:, :], 1.0)
    S.memset(EPSC[:, :], EPS)

    def reduce_angle(rout, ang, tmpf, tmpi):
        S.ts(tmpf, ang, 1.0 / TWO_PI, None, ALU.mult)
        S.copy(tmpi, tmpf)
        S.copy(tmpf, tmpi)
        S.stt(rout, tmpf, -C1, ang, ALU.mult, ALU.add)
        S.stt(rout, tmpf, -C2, rout, ALU.mult, ALU.add)

    def rmsnorm_tile(dst, src_of_dc, ndc, width, gcol, nfeat, sq_tmp, rs, dst_is_list=False):
        P = bank()
        for dc in range(ndc):
            sq = sq_tmp[dc % 2]
            S.act(sq, src_of_dc(dc), AF.Square)
            S.mm(P[:, 0:width], ONES[:, :], sq, start=(dc == 0), stop=(dc == ndc - 1))
        S.act(rs, P[:, 0:width], AF.Sqrt, bias=EPSC[:, 0:1], scale=1.0 / nfeat)
        S.recip(rs, rs)
        for dc in range(ndc):
            S.stt(dst(dc), src_of_dc(dc), gcol(dc), rs, ALU.mult, ALU.mult)

    def layer(li):
        S.dma(LPT[:, :], lp_d.ap()[li])
        m_layer = A.mark()

        WIN = A.alloc([128, 8, 2048], BF16)
        STG = A.alloc([128, 8, 256], F32)
        for g in range(8):
            S.dma(STG[:, :, :], w_in_d.ap()[li, g])
            S.copy(WIN[:, :, 256 * g:256 * (g + 1)], STG[:, :, :], eng="pool")
        HT = A.alloc([128, 8, 512], BF16)
        SQ = [A.alloc([128, 512], BF16) for _ in range(2)]
        RS = A.alloc([128, 512], F32)
        PI = A.alloc([128, 512], I32)
        ANG = A.alloc([128, 512], F32)
        TF = A.alloc([128, 512], F32)
        COSF = A.alloc([128, 512], F32)
        SINF = A.alloc([128, 512], F32)
        QBs = [A.alloc([128, 512], BF16) for _ in range(2)]
        T1s = [A.alloc([128, 512], F32) for _ in range(2)]
        T2s = [A.alloc([128, 512], F32) for _ in range(2)]
        rctr = [0]
        PARTS = set(stop.split(':')[1].split('+')) if (stop and ':' in stop) else {'norm', 'rope', 'proj', 'v'}
        for tt in range(NT):
            tok = slice(512 * tt, 512 * (tt + 1))
            rmsnorm_tile(lambda dc: HT[:, dc, :], lambda dc: X[:, dc, tok], 8, 512,
                         lambda dc: lpc("gmix", dc, dc + 1), D, [SQ[0][:, :], SQ[1][:, :]], RS[:, :])
            if 'rope' not in PARTS:
                continue
            S.dma(PI[:, :], pos.ap()[0:1, tok].partition_broadcast(128))
            S.copy(TF[:, :], PI[:, :])
            S.ts(ANG[:, :], TF[:, :], cs("invf"), None, ALU.mult)
            reduce_angle(ANG[:, :], ANG[:, :], TF[:, :], PI[:, :])
            S.act(SINF[:, :], ANG[:, :], AF.Sin, scale=cs("sgn"))
            S.act(COSF[:, :], ANG[:, :], AF.Sin, scale=0.5)
            S.tt(COSF[:, :], COSF[:, :], COSF[:, :], ALU.mult)
            S.ts(COSF[:, :], COSF[:, :], -2.0, 1.0, ALU.mult, ALU.add)
            for fc in range(12 if 'proj' in PARTS else 0):
                P = bank()
                for dc in range(8):
                    S.mm(P, WIN[:, dc, 128 * fc:128 * (fc + 1)], HT[:, dc, :], start=(dc == 0), stop=(dc == 7))
                if fc < 4:
                    S.act(U[:, fc, tok], P, AF.Copy)
                else:
                    dst = Q[:, fc - 4, tok] if fc < 8 else K[:, fc - 8, tok]
                    QB, T1, T2 = QBs[rctr[0] % 2], T1s[rctr[0] % 2], T2s[rctr[0] % 2]
                    rctr[0] += 1
                    S.act(QB[:, :], P, AF.Copy)
                    PR = bank()
                    S.mm(PR, ROTM, QB[:, :])
                    S.tt(T1[:, :], P, COSF[:, :], ALU.mult)
                    S.tt(T2[:, :], PR, SINF[:, :], ALU.mult)
                    S.tt(dst, T1[:, :], T2[:, :], ALU.add, eng="pool")
            for s in range(4 if 'v' in PARTS else 0):
                P = bank()
                for dc in range(8):
                    S.mm(P, HT[:, dc, 128 * s:128 * (s + 1)], WIN[:, dc, 1536:2048], start=(dc == 0), stop=(dc == 7))
                S.act(V[:, :, 4 * tt + s, :], P.rearrange("p (h d) -> p h d", h=4), AF.Copy)
        dbg("U", U[:, :, :]); dbg("Q", Q[:, :, :]); dbg("K", K[:, :, :]); dbg("V", V[:, :, :, :])
        A.release(m_layer)
        if stop and (stop == "A" or stop.startswith("A:")):
            return

        for h in range(4):
            S.dma(agin[h].ap()[:, 0:2048], K[:, h, :])
            S.dma(agin[h].ap()[:, 2048:4096], V[:, h, :, :].rearrange("p a b -> p (a b)"))
            S.allgather(agout[h].ap().opt(), agin[h].ap().opt())

        AR = A.alloc([128, 17, 16], F32)
        AI = A.alloc([128, 17, 16], F32)
        NAI = A.alloc([128, 17, 16], F32)
        SLI = A.alloc([128, 16], F32)
        RHO = A.alloc([128, 16], F32)
        MT = A.alloc([128, 16], F32)
        CT128 = A.alloc([128, 16], F32)
        ST128 = A.alloc([128, 16], F32)
        BR = A.alloc([128, 512], F32)
        BI = A.alloc([128, 512], F32)
        CB = A.alloc([128, 2, 512], BF16)
        XRE = A.alloc([128, 16, NB], F32)
        XIM = A.alloc([128, 16, NB], F32)
        EL = A.alloc([128, 2, 16], F32)
        m_ssm = A.mark()
        if True:
            CRE = A.alloc([128, 512], F32)
            CIM = A.alloc([128, 512], F32)
            S.dma(CRE[:, :], lpb_d.ap()[li, :, 1024:1536])
            S.dma(CIM[:, :], lpb_d.ap()[li, :, 1536:2048])
            BRE = A.alloc([128, 512], F32)
            BIM = A.alloc([128, 512], F32)
            S.dma(BRE[:, :], lpb_d.ap()[li, :, 0:512])
            S.dma(BIM[:, :], lpb_d.ap()[li, :, 512:1024])
            STEP = A.alloc([128, 16], F32)
            SLR = A.alloc([128, 16], F32)
            W17 = [A.alloc([128, 17, 16], F32) for _ in range(4)]
            W17I = A.alloc([128, 17, 16], I32)
            S.act(STEP[:, :], lpc("logstep"), AF.Exp)
            S.tt(SLR[:, :], STEP[:, :], lpc("lamre"), ALU.mult)
            S.tt(SLI[:, :], STEP[:, :], lpc("lamim"), ALU.mult)
            S.act(RHO[:, :], SLR[:, :], AF.Exp, scale=float(R))
            S.act(MT[:, :], SLR[:, :], AF.Exp, scale=float(T))
            taub = cs("tau").unsqueeze(2).broadcast_to([128, 17, 16])
            S.tt(W17[0][:, :, :], SLR[:, :].unsqueeze(1).broadcast_to([128, 17, 16]), taub, ALU.mult)
            S.act(W17[0][:, :, :], W17[0][:, :, :], AF.Exp)
            S.tt(W17[1][:, :, :], SLI[:, :].unsqueeze(1).broadcast_to([128, 17, 16]), taub, ALU.mult)
            reduce_angle(W17[1][:, :, :], W17[1][:, :, :], W17[2][:, :, :], W17I[:, :, :])
            S.act(W17[2][:, :, :], W17[1][:, :, :], AF.Sin)
            S.act(W17[3][:, :, :], W17[1][:, :, :], AF.Sin, scale=0.5)
            S.tt(W17[3][:, :, :], W17[3][:, :, :], W17[3][:, :, :], ALU.mult)
            S.ts(W17[3][:, :, :], W17[3][:, :, :], -2.0, 1.0, ALU.mult, ALU.add)
            S.tt(AR[:, :, :], W17[0][:, :, :], W17[3][:, :, :], ALU.mult)
            S.tt(AI[:, :, :], W17[0][:, :, :], W17[2][:, :, :], ALU.mult)
            S.ts(NAI[:, :, :], AI[:, :, :], -1.0, None, ALU.mult)
            den, am1, fr, fi, t0 = [A.alloc([128, 16], F32) for _ in range(5)]
            S.tt(den[:, :], lpc("lamre"), lpc("lamre"), ALU.mult)
            S.tt(t0[:, :], lpc("lamim"), lpc("lamim"), ALU.mult)
            S.tt(den[:, :], den[:, :], t0[:, :], ALU.add)
            S.recip(den[:, :], den[:, :])
            S.ts(am1[:, :], AR[:, 1, :], -1.0, None, ALU.add)
            S.tt(fr[:, :], am1[:, :], lpc("lamre"), ALU.mult)
            S.tt(t0[:, :], AI[:, 1, :], lpc("lamim"), ALU.mult)
            S.tt(fr[:, :], fr[:, :], t0[:, :], ALU.add)
            S.tt(fr[:, :], fr[:, :], den[:, :], ALU.mult)
            S.tt(fi[:, :], AI[:, 1, :], lpc("lamre"), ALU.mult)
            S.tt(t0[:, :], am1[:, :], lpc("lamim"), ALU.mult)
            S.tt(fi[:, :], fi[:, :], t0[:, :], ALU.subtract)
            S.tt(fi[:, :], fi[:, :], den[:, :], ALU.mult)
            frb = fr[:, :].unsqueeze(2).broadcast_to([128, 16, 32])
            fib = fi[:, :].unsqueeze(2).broadcast_to([128, 16, 32])
            v3 = lambda t: t[:, :].rearrange("p (k c) -> p k c", k=16)
            TB1 = A.alloc([128, 512], F32)
            S.tt(v3(BR), frb, v3(BRE), ALU.mult)
            S.tt(v3(TB1), fib, v3(BIM), ALU.mult)
            S.tt(BR[:, :], BR[:, :], TB1[:, :], ALU.subtract)
            S.tt(v3(BI), frb, v3(BIM), ALU.mult)
            S.tt(v3(TB1), fib, v3(BRE), ALU.mult)
            S.tt(BI[:, :], BI[:, :], TB1[:, :], ALU.add)
            S.copy(CB[:, 0, :], CRE[:, :])
            S.ts(CB[:, 1, :], CIM[:, :], -1.0, None, ALU.mult)
        A.release(m_ssm)

        def pair_tables(ct, CT, ST, tf, ti):
            S.tt(CT[:, :, :], SLI[:, 4 * ct:4 * ct + 4].unsqueeze(2).broadcast_to([128, 4, 129]),
                 cs("bv").unsqueeze(1).broadcast_to([128, 4, 129]), ALU.mult)
            reduce_angle(CT[:, :, :], CT[:, :, :], tf[:, :, :], ti[:, :, :])
            S.act(ST[:, :, :], CT[:, :, :], AF.Sin)
            S.act(CT[:, :, :], CT[:, :, :], AF.Sin, scale=0.5)
            S.tt(CT[:, :, :], CT[:, :, :], CT[:, :, :], ALU.mult)
            S.ts(CT[:, :, :], CT[:, :, :], -2.0, 1.0, ALU.mult, ALU.add)

        def make_WE(ct, WE, ta, tb):
            for half in range(2):
                ts_ = slice(8 * half, 8 * half + 8)
                arb = AR[:, ts_, 4 * ct:4 * ct + 4].unsqueeze(3).broadcast_to([128, 8, 4, 32])
                aib = AI[:, ts_, 4 * ct:4 * ct + 4].unsqueeze(3).broadcast_to([128, 8, 4, 32])
                brb = BR[:, 128 * ct:128 * ct + 128].rearrange("p (i c) -> p i c", i=4).unsqueeze(1).broadcast_to([128, 8, 4, 32])
                bib = BI[:, 128 * ct:128 * ct + 128].rearrange("p (i c) -> p i c", i=4).unsqueeze(1).broadcast_to([128, 8, 4, 32])
                v4 = lambda t: t[:, :].rearrange("p (a i c) -> p a i c", a=8, i=4)
                S.tt(v4(ta), arb, brb, ALU.mult)
                S.tt(v4(tb), aib, bib, ALU.mult)
                S.tt(WE[:, 0, ts_, :].rearrange("p a n -> p (a n)"), ta[:, :], tb[:, :], ALU.subtract, eng="pool")
                S.tt(v4(ta), arb, bib, ALU.mult)
                S.tt(v4(tb), aib, brb, ALU.mult)
                S.tt(WE[:, 1, ts_, :].rearrange("p a n -> p (a n)"), ta[:, :], tb[:, :], ALU.add, eng="pool")

        for ct in range(4):
            m = A.mark()
            WE = A.alloc([128, 2, 16, 128], BF16)
            WDT = A.alloc([128, 2, 16, 128], BF16)
            ta = A.alloc([128, 1024], F32)
            tb = A.alloc([128, 1024], F32)
            CT = A.alloc([128, 4, 129], F32)
            ST = A.alloc([128, 4, 129], F32)
            TI = S.sb([128, 4, 129], I32, S.tinfo[tb.name][1])
            WS = A.alloc([128, 2, 4, NB], F32)
            make_WE(ct, WE, ta, tb)
            pair_tables(ct, CT, ST, ta[:, 0:516].rearrange("p (i b) -> p i b", i=4), TI)
            S.copy(CT128[:, 4 * ct:4 * ct + 4], CT[:, :, 128])
            S.copy(ST128[:, 4 * ct:4 * ct + 4], ST[:, :, 128])
            for x in range(2):
                for tg in range(4):
                    P = bank()
                    for t4 in range(4):
                        S.mm(P[:, 128 * t4:128 * (t4 + 1)], WE[:, x, 4 * tg + t4, :], IDENT)
                    S.act(WDT[:, x, 4 * tg:4 * tg + 4, :].rearrange("p a n -> p (a n)"), P, AF.Copy)
            for x in range(2):
                for j in range(R):
                    for i in range(4):
                        S.mm(PS[:, 512 * (4 * x + i):512 * (4 * x + i) + 128], WDT[32 * i:32 * i + 32, x, R - 1 - j, :],
                             U[32 * i:32 * i + 32, ct, :].rearrange("p (b j) -> p j b", j=R)[:, j, :],
                             start=(j == 0), stop=(j == R - 1), tile_position=(32 * i, 0))
            cc = CT[:, :, 1:129]
            ss = ST[:, :, 1:129]
            pre = PS[:, 0:2048].rearrange("p (i c) -> p i c", i=4)[:, :, 0:128]
            pim = PS[:, 2048:4096].rearrange("p (i c) -> p i c", i=4)[:, :, 0:128]
            v3b = lambda t: t[:, 0:512].rearrange("p (i b) -> p i b", i=4)
            S.tt(v3b(ta), pre, cc, ALU.mult)
            S.tt(v3b(tb), pim, ss, ALU.mult)
            S.tt(XRE[:, 4 * ct:4 * ct + 4, :], v3b(ta), v3b(tb), ALU.add, eng="pool")
            S.tt(v3b(ta), pim, cc, ALU.mult)
            S.tt(v3b(tb), pre, ss, ALU.mult)
            S.tt(XIM[:, 4 * ct:4 * ct + 4, :], v3b(ta), v3b(tb), ALU.subtract, eng="pool")
            for i in range(4):
                k = 4 * ct + i
                S.scan(WS[:, 0, i, :], RHO[:, k:k + 1].broadcast_to([128, NB]), XRE[:, k, :], 0.0)
                S.scan(WS[:, 1, i, :], RHO[:, k:k + 1].broadcast_to([128, NB]), XIM[:, k, :], 0.0)
            e1 = ta[:, 0:4]
            e2 = ta[:, 4:8]
            S.tt(e1, WS[:, 0, :, NB - 1], CT[:, :, 128], ALU.mult)
            S.tt(e2, WS[:, 1, :, NB - 1], ST[:, :, 128], ALU.mult)
            S.tt(EL[:, 0, 4 * ct:4 * ct + 4], e1, e2, ALU.subtract)
            S.tt(e1, WS[:, 1, :, NB - 1], CT[:, :, 128], ALU.mult)
            S.tt(e2, WS[:, 0, :, NB - 1], ST[:, :, 128], ALU.mult)
            S.tt(EL[:, 1, 4 * ct:4 * ct + 4], e1, e2, ALU.add)
            A.release(m)
        if stop == "S1":
            dbg("XRE", XRE[:, :, :]); dbg("EL", EL[:, :, :])
            A.release(m_layer)
            return
        S.dma(agEin.ap(), EL[:, :, :].rearrange("p a b -> p (a b)"))
        S.allgather(agEout.ap().opt(), agEin.ap().opt())
        if stop == "AG":
            A.release(m_layer)
            return

        m_att = A.mark()
        KVP = A.alloc([128, 3, 4096], BF16)
        PT = [A.alloc([128, 512], BF16) for _ in range(4)]
        F1, F2, F3, F4 = [A.alloc([128, 512], F32) for _ in range(4)]
        SQB = A.alloc([128, 512], BF16)
        LAMT = A.alloc([128, 64], F32)
        LS = A.alloc([128, 4], F32)
        NLAM = A.alloc([128, 1], F32)
        GS = A.alloc([128, 1], F32)
        ZB = A.alloc([128, 1], F32)
        S.memset(ZB[:, :], 0.0)
        for c in range(2):
            o = LP["lamqk"][0] + 128 * c
            S.tt(LAMT[:, :], LPT[:, o:o + 64], LPT[:, o + 64:o + 128], ALU.mult)
            S.rsum(LS[:, c:c + 1], LAMT[:, :])
        S.act(LS[:, 2:4], LS[:, 0:2], AF.Exp)
        S.tt(NLAM[:, :], LS[:, 3:4], LS[:, 2:3], ALU.subtract)
        S.tt(NLAM[:, :], NLAM[:, :], lpc("nlaminit"), ALU.add)
        S.tt(GS[:, :], lpc("gsub"), lpc("oml"), ALU.mult)
        pctr = 0
        for h in range(4):
            for r in range(3):
                S.dma(KVP[:, r, :], agout[h].ap().rearrange("(r p) n -> p r n", p=128)[:, r, :])
            for qt in range(4):
                qs = slice(512 * qt, 512 * (qt + 1))
                steps = []
                for kt in range(4 * qt + 4):
                    steps.append((K[:, h, 128 * kt:128 * (kt + 1)], V[:, h, kt, :], ZB[:, 0:1],
                                  (kt - 4 * qt) if kt >= 4 * qt else None))
                for r in range(3):
                    for kt in range(16):
                        steps.append((KVP[:, r, 128 * kt:128 * (kt + 1)], KVP[:, r, 2048 + 128 * kt:2048 + 128 * (kt + 1)],
                                      cs("rankbias", r, r + 1), None))
                O1, O2, D1, D2 = bankn(4), bankn(5), bankn(6), bankn(7)
                def emit_scores(idx):
                    kt_ap = steps[idx][0]
                    S.mm(bankn((2 * idx) % 4), kt_ap[0:64, :], Q[0:64, h, qs])
                    S.mm(bankn((2 * idx + 1) % 4), kt_ap[64:128, :], Q[64:128, h, qs])

                emit_scores(0)
                for idx, (kt_ap, v_ap, b_ap, mi) in enumerate(steps):
                    first, last = idx == 0, idx == len(steps) - 1
                    S1 = bankn((2 * idx) % 4)
                    S2 = bankn((2 * idx + 1) % 4)
                    if not last:
                        emit_scores(idx + 1)
                    P1 = PT[pctr % 4]
                    P2 = PT[(pctr + 1) % 4]
                    pctr += 2
                    S.act(P1[:, :], S1, AF.Exp, bias=b_ap, scale=0.125)
                    S.act(P2[:, :], S2, AF.Exp, bias=b_ap, scale=0.125)
                    if mi is not None:
                        S.tt(P1[:, :], P1[:, :], MASK(mi), ALU.mult, eng="pool")
                        S.tt(P2[:, :], P2[:, :], MASK(mi), ALU.mult, eng="pool")
                    S.mm(O1, v_ap, P1[:, :], start=first, stop=last)
                    S.mm(D1, ONES[:, :], P1[:, :], start=first, stop=last)
                    S.mm(O2, v_ap, P2[:, :], start=first, stop=last)
                    S.mm(D2, ONES[:, :], P2[:, :], start=first, stop=last)
                S.recip(F1[:, :], D1)
                S.recip(F2[:, :], D2)
                S.tt(F1[:, :], O1, F1[:, :], ALU.mult)
                S.tt(F2[:, :], O2, F2[:, :], ALU.mult)
                S.stt(F3[:, :], F2[:, :], NLAM[:, 0:1], F1[:, :], ALU.mult, ALU.add)
                S.act(SQB[:, :], F3[:, :], AF.Square)
                PN = bankn(0)
                S.mm(PN, ONES[:, :], SQB[:, :])
                S.act(F4[:, :], PN, AF.Sqrt, bias=EPSC[:, 0:1], scale=1.0 / 128)
                S.recip(F4[:, :], F4[:, :])
                S.stt(Q[:, h, qs], F3[:, :], GS[:, 0:1], F4[:, :], ALU.mult, ALU.mult)
        dbg("ATT", Q[:, :, :])
        A.release(m_att)
        if stop == "ATT":
            A.release(m_layer)
            return

        m2 = A.mark()
        EA = A.alloc([128, 4, 2, 16], F32)
        S.dma(EA[:, :, :, :].rearrange("p r a b -> p r (a b)"), agEout.ap().rearrange("(r p) n -> p r n", p=128))
        ATR, ATI, SCR, SCI, SNR, SNI, SINR, SINI, c1, c2 = [A.alloc([128, 16], F32) for _ in range(10)]
        CRE = A.alloc([128, 512], F32)
        CIM = A.alloc([128, 512], F32)
        KD = A.alloc([128, 16, 128], BF16)
        S.dma(CRE[:, :], lpb_d.ap()[li, :, 1024:1536])
        S.dma(CIM[:, :], lpb_d.ap()[li, :, 1536:2048])
        S.memset(KD[:, :, :], 0.0, eng="pool")
        S.tt(ATR[:, :], MT[:, :], CT128[:, :], ALU.mult)
        S.tt(ATI[:, :], MT[:, :], ST128[:, :], ALU.mult)
        S.copy(SCR[:, :], EA[:, 0, 0, :])
        S.copy(SCI[:, :], EA[:, 0, 1, :])
        S.ts(SINR[:, :], SCR[:, :], cs("ohq", 1, 2), None, ALU.mult)
        S.ts(SINI[:, :], SCI[:, :], cs("ohq", 1, 2), None, ALU.mult)
        for q in (1, 2):
            S.tt(c1[:, :], ATR[:, :], SCR[:, :], ALU.mult)
            S.tt(c2[:, :], ATI[:, :], SCI[:, :], ALU.mult)
            S.tt(SNR[:, :], c1[:, :], c2[:, :], ALU.subtract)
            S.tt(SNR[:, :], SNR[:, :], EA[:, q, 0, :], ALU.add)
            S.tt(c1[:, :], ATR[:, :], SCI[:, :], ALU.mult)
            S.tt(c2[:, :], ATI[:, :], SCR[:, :], ALU.mult)
            S.tt(SNI[:, :], c1[:, :], c2[:, :], ALU.add)
            S.tt(SNI[:, :], SNI[:, :], EA[:, q, 1, :], ALU.add)
            S.copy(SCR[:, :], SNR[:, :])
            S.copy(SCI[:, :], SNI[:, :])
            S.stt(SINR[:, :], SCR[:, :], cs("ohq", q + 1, q + 2), SINR[:, :], ALU.mult, ALU.add)
            S.stt(SINI[:, :], SCI[:, :], cs("ohq", q + 1, q + 2), SINI[:, :], ALU.mult, ALU.add)
        for ct in range(4):
            m = A.mark()
            WE = A.alloc([128, 2, 16, 128], BF16)
            CAE = A.alloc([128, 2, 16, 128], BF16)
            ta = A.alloc([128, 1024], F32)
            tb = A.alloc([128, 1024], F32)
            CT = A.alloc([128, 4, 129], F32)
            ST = A.alloc([128, 4, 129], F32)
            TI = S.sb([128, 4, 129], I32, S.tinfo[tb.name][1])
            WB = A.alloc([128, 2, 4, 129], F32)
            SS = A.alloc([128, 2, 4, NB], BF16)
            YF = tb
            make_WE(ct, WE, ta, tb)
            pair_tables(ct, CT, ST, ta[:, 0:516].rearrange("p (i b) -> p i b", i=4), TI)
            for half in range(2):
                js = slice(8 * half + 1, 8 * half + 9)
                jo = slice(8 * half, 8 * half + 8)
                arb = AR[:, js, 4 * ct:4 * ct + 4].unsqueeze(3).broadcast_to([128, 8, 4, 32])
                aib = AI[:, js, 4 * ct:4 * ct + 4].unsqueeze(3).broadcast_to([128, 8, 4, 32])
                naib = NAI[:, js, 4 * ct:4 * ct + 4].unsqueeze(3).broadcast_to([128, 8, 4, 32])
                crb = CRE[:, 128 * ct:128 * ct + 128].rearrange("p (i c) -> p i c", i=4).unsqueeze(1).broadcast_to([128, 8, 4, 32])
                cib = CIM[:, 128 * ct:128 * ct + 128].rearrange("p (i c) -> p i c", i=4).unsqueeze(1).broadcast_to([128, 8, 4, 32])
                v4 = lambda t: t[:, :].rearrange("p (a i c) -> p a i c", a=8, i=4)
                S.tt(v4(ta), arb, crb, ALU.mult)
                S.tt(v4(tb), aib, cib, ALU.mult)
                S.tt(CAE[:, 0, jo, :].rearrange("p a n -> p (a n)"), ta[:, :], tb[:, :], ALU.subtract, eng="pool")
                S.tt(v4(ta), naib, crb, ALU.mult)
                S.tt(v4(tb), arb, cib, ALU.mult)
                S.tt(CAE[:, 1, jo, :].rearrange("p a n -> p (a n)"), ta[:, :], tb[:, :], ALU.subtract, eng="pool")
            PK = bank()
            for tau in range(R):
                for i in range(4):
                    k = 4 * ct + i
                    for x in range(2):
                        S.mm(PK[32 * i:32 * i + 32, 32 * tau:32 * tau + 32], WE[:, x, tau, 32 * i:32 * i + 32],
                             CB[:, x, 32 * k:32 * k + 32], start=(x == 0), stop=(x == 1), tile_position=(0, 32 * i))
            for i in range(4):
                S.act(KD[32 * i:32 * i + 32, :, 32 * i:32 * i + 32],
                      PK[32 * i:32 * i + 32, :].rearrange("p (t c) -> p t c", t=R), AF.Copy)
            for i in range(4):
                k = 4 * ct + i
                S.copy(WB[:, 0, i, 0:1], SINR[:, k:k + 1])
                S.copy(WB[:, 1, i, 0:1], SINI[:, k:k + 1])
                S.scan(WB[:, 0, i, 1:129], RHO[:, k:k + 1].broadcast_to([128, NB]), XRE[:, k, :], SINR[:, k:k + 1])
                S.scan(WB[:, 1, i, 1:129], RHO[:, k:k + 1].broadcast_to([128, NB]), XIM[:, k, :], SINI[:, k:k + 1])
            cc = CT[:, :, 0:128]
            ss = ST[:, :, 0:128]
            v3b = lambda t: t[:, 0:512].rearrange("p (i b) -> p i b", i=4)
            S.tt(v3b(ta), WB[:, 0, :, 0:128], cc, ALU.mult)
            S.tt(v3b(tb), WB[:, 1, :, 0:128], ss, ALU.mult)
            S.tt(SS[:, 0, :, :], v3b(ta), v3b(tb), ALU.subtract, eng="pool")
            S.tt(v3b(ta), WB[:, 1, :, 0:128], cc, ALU.mult)
            S.tt(v3b(tb), WB[:, 0, :, 0:128], ss, ALU.mult)
            S.tt(SS[:, 1, :, :], v3b(ta), v3b(tb), ALU.add, eng="pool")
            yb = 4 * (ct % 2)
            Y = PS[:, 512 * yb:512 * (yb + 4)].rearrange("p (j b) -> p j b", j=R)
            Uv = U[:, ct, :].rearrange("p (b j) -> p j b", j=R)
            for j in range(R):
                for j2 in range(j + 1):
                    S.mm(Y[:, j, :], KD[:, j - j2, :], Uv[:, j2, :], start=(j2 == 0), stop=False)
                for i in range(4):
                    for x in range(2):
                        S.mm(Y[32 * i:32 * i + 32, j, :], CAE[:, x, j, 32 * i:32 * i + 32], SS[:, x, i, :],
                             start=False, stop=(x == 1), tile_position=(0, 32 * i))
            for hb in range(2):
                bs = slice(64 * hb, 64 * hb + 64)
                ts_ = slice(1024 * hb, 1024 * hb + 1024)
                S.stt(YF[:, :].rearrange("p (b j) -> p j b", j=R), Uv[:, :, bs], lpc("dvec", ct, ct + 1), Y[:, :, bs],
                      ALU.mult, ALU.add)
                if "YSSM" in debug:
                    if "_y" not in dbg_out:
                        dbg_out["_y"] = S.dram("dbg_YSSM", [128, 4, T], F32, kind="ExternalOutput")
                    S.dma(dbg_out["_y"].ap()[:, ct, ts_], YF[:, :])
                S.act(ta[:, :], YF[:, :], AF.Square)
                S.ts(ta[:, :], ta[:, :], 0.044715, 1.0, ALU.mult, ALU.add)
                S.tt(ta[:, :], ta[:, :], YF[:, :], ALU.mult)
                S.act(ta[:, :], ta[:, :], AF.Sigmoid, scale=2.0 * math.sqrt(2.0 / math.pi))
                S.tt(U[:, ct, ts_], YF[:, :], ta[:, :], ALU.mult, eng="pool")
            A.release(m)
        A.release(m2)
        A.release(m_layer)
        if stop == "S2":
            return

        WG = A.alloc([128, 4, 1024], BF16)
        WO = A.alloc([128, 8, 1024], BF16)
        STG = A.alloc([128, 2048], F32)
        for hf in range(2):
            sv = STG[:, :].rearrange("p (k n) -> p k n", k=4)
            S.dma(sv, w_glu_d.ap()[li, hf])
            S.copy(WG[:, :, 512 * hf:512 * (hf + 1)], sv, eng="pool")
        for g in range(4):
            sv = STG[:, :].rearrange("p (k n) -> p k n", k=8)
            S.dma(sv, w_out_d.ap()[li, g])
            S.copy(WO[:, :, 256 * g:256 * (g + 1)], sv, eng="pool")
        GL = A.alloc([128, 4, 512], F32)
        SGs = [A.alloc([128, 512], F32) for _ in range(2)]
        SQ = [A.alloc([128, 512], BF16) for _ in range(2)]
        RS = A.alloc([128, 512], F32)
        for tt in range(NT):
            tok = slice(512 * tt, 512 * (tt + 1))
            for oc in range(4):
                PA, PB = bank(), bank()
                for kc in range(4):
                    S.mm(PB, WG[:, kc, 512 + 128 * oc:512 + 128 * (oc + 1)], U[:, kc, tok], start=(kc == 0), stop=(kc == 3))
                for kc in range(4):
                    S.mm(PA, WG[:, kc, 128 * oc:128 * (oc + 1)], U[:, kc, tok], start=(kc == 0), stop=(kc == 3))
                SG = SGs[oc % 2]
                S.act(SG[:, :], PB, AF.Sigmoid)
                S.tt(GL[:, oc, :], PA, SG[:, :], ALU.mult)
            rmsnorm_tile(lambda oc: U[:, oc, tok], lambda oc: GL[:, oc, :], 4, 512,
                         lambda oc: lpc("gssm", oc, oc + 1), 512, [SQ[0][:, :], SQ[1][:, :]], RS[:, :])
        dbg("SSM", U[:, :, :])
        for tt in range(NT):
            tok = slice(512 * tt, 512 * (tt + 1))
            for dc in range(8):
                P = bank()
                for kc in range(8):
                    src = U[:, kc, tok] if kc < 4 else Q[:, kc - 4, tok]
                    S.mm(P, WO[:, kc, 128 * dc:128 * (dc + 1)], src, start=(kc == 0), stop=(kc == 7))
                S.tt(X[:, dc, tok], X[:, dc, tok], P, ALU.add)
        dbg("XMID", X[:, :, :])
        A.release(m_layer)
        if stop == "OUT":
            return

        S.dma(ag2in.ap().rearrange("p (c t) -> p c t", c=8), X[:, :, T - 2:T])
        S.allgather(ag2out.ap().opt(), ag2in.ap().opt())
        H4 = A.alloc([128, 4, 16], F32)
        XH = A.alloc([128, 16], F32)
        HH = A.alloc([128, 8, 2], BF16)
        HSQ = A.alloc([128, 16], BF16)
        HRS = A.alloc([128, 2], F32)
        HTMP = A.alloc([128, 16], F32)
        CARRY = [A.alloc([128, 44, 2], F32) for _ in range(5)]
        S.dma(H4[:, :, :], ag2out.ap().rearrange("(r p) n -> p r n", p=128))
        S.ts(XH[:, :], H4[:, 0, :], cs("ohprev", 0, 1), None, ALU.mult)
        for r in range(1, 4):
            S.stt(XH[:, :], H4[:, r, :], cs("ohprev", r, r + 1), XH[:, :], ALU.mult, ALU.add)
        XHv = XH[:, :].rearrange("p (c t) -> p c t", c=8)
        S.act(HSQ[:, :], XH[:, :], AF.Square)
        PHn = bank()
        for dc in range(8):
            S.mm(PHn[:, 0:2], ONES[:, :], HSQ[:, 2 * dc:2 * dc + 2], start=(dc == 0), stop=(dc == 7))
        S.act(HRS[:, :], PHn[:, 0:2], AF.Sqrt, bias=EPSC[:, 0:1], scale=1.0 / D)
        S.recip(HRS[:, :], HRS[:, :])
        S.tt(HTMP[:, :].rearrange("p (c t) -> p c t", c=8), XHv, HRS[:, :].unsqueeze(1).broadcast_to([128, 8, 2]), ALU.mult)
        S.tt(HH[:, :, :], HTMP[:, :].rearrange("p (c t) -> p c t", c=8),
             lpc("gffn").unsqueeze(2).broadcast_to([128, 8, 2]), ALU.mult)
        A2.release(A2.lo)
        HTF = A2.alloc([128, 8, 1024], BF16)
        ACTT = A2.alloc([128, NFC, 1024], BF16)
        SQ = [A.alloc([128, 512], BF16) for _ in range(2)]
        RS = A.alloc([128, 512], F32)
        STU = [A.alloc([128, 8, 256], F32) for _ in range(2)]
        WUB = [A.alloc([128, 8, 256], BF16) for _ in range(2)]
        STD = [A.alloc([128, 11, 128], F32) for _ in range(2)]
        WDB = [A.alloc([128, NFC, 128], BF16) for _ in range(2)]
        ACC = [A.alloc([128, 512], F32) for _ in range(6)]
        actr = [0]
        SGT = A.alloc([128, 512], F32)
        wctr = 0
        for hf in range(2):
            for t2 in range(2):
                tok = slice(1024 * hf + 512 * t2, 1024 * hf + 512 * (t2 + 1))
                rmsnorm_tile(lambda dc: HTF[:, dc, 512 * t2:512 * (t2 + 1)], lambda dc: X[:, dc, tok], 8, 512,
                             lambda dc: lpc("gffn", dc, dc + 1), D, [SQ[0][:, :], SQ[1][:, :]], RS[:, :])
            for fc in range(NFC):
                b = wctr % 2
                wctr += 1
                S.dma(STU[b][:, :, :], w_up_d.ap()[li, fc])
                S.copy(WUB[b][:, :, :], STU[b][:, :, :], eng="pool")
                for gv in range(2):
                    c = gv * NFC + fc
                    if hf == 0:
                        PH = bank()
                        for dc in range(8):
                            S.mm(PH[:, 0:2], WUB[b][:, dc, 128 * gv:128 * (gv + 1)], HH[:, dc, :], start=(dc == 0), stop=(dc == 7))
                        S.act(CARRY[0][:, c, :], PH[:, 0:2], AF.Copy)
                    for t2 in range(2):
                        gt = 2 * hf + t2
                        P = bank()
                        for dc in range(8):
                            S.mm(P, WUB[b][:, dc, 128 * gv:128 * (gv + 1)], HTF[:, dc, 512 * t2:512 * (t2 + 1)],
                                 start=(dc == 0), stop=(dc == 7))
                        acc = ACC[actr[0] % 6]
                        actr[0] += 1
                        w0 = lpc("wconv", c, c + 1)
                        w1 = lpc("wconv", 44 + c, 44 + c + 1)
                        w2 = lpc("wconv", 88 + c, 88 + c + 1)
                        if t2 == 1:
                            pass
                        S.act(acc[:, :], P, AF.Identity, bias=lpc("bconv", c, c + 1), scale=w2)
                        S.stt(acc[:, 1:512], P[:, 0:511], w1, acc[:, 1:512], ALU.mult, ALU.add)
                        S.stt(acc[:, 2:512], P[:, 0:510], w0, acc[:, 2:512], ALU.mult, ALU.add)
                        S.stt(acc[:, 0:2], CARRY[gt][:, c, 0:2], w0, acc[:, 0:2], ALU.mult, ALU.add)
                        S.stt(acc[:, 0:1], CARRY[gt][:, c, 1:2], w1, acc[:, 0:1], ALU.mult, ALU.add)
                        S.act(CARRY[gt + 1][:, c, :], P[:, 510:512], AF.Copy)
                        if gv == 0:
                            S.act(ACTT[:, fc, 512 * t2:512 * (t2 + 1)], acc[:, :], AF.Silu)
                        else:
                            S.tt(ACTT[:, fc, 512 * t2:512 * (t2 + 1)], ACTT[:, fc, 512 * t2:512 * (t2 + 1)], acc[:, :], ALU.mult)
            for dc in range(8):
                b = dc % 2
                for h2 in range(2):
                    S.dma(STD[h2][:, :, :], w_dn_d.ap()[li, dc, h2])
                    S.copy(WDB[b][:, 11 * h2:11 * h2 + 11, :], STD[h2][:, :, :], eng="pool")
                for t2 in range(2):
                    tok = slice(1024 * hf + 512 * t2, 1024 * hf + 512 * (t2 + 1))
                    P = bank()
                    for fc in range(NFC):
                        S.mm(P, WDB[b][:, fc, :], ACTT[:, fc, 512 * t2:512 * (t2 + 1)], start=(fc == 0), stop=(fc == NFC - 1))
                    S.tt(X[:, dc, tok], X[:, dc, tok], P, ALU.add)
        dbg("XOUT", X[:, :, :])
        A.release(m_layer)

    for li in range(L):
        layer(li)

    if final:
        OT = A.alloc([128, 8, 512], F32)
        SQ = [A.alloc([128, 512], BF16) for _ in range(2)]
        RS = A.alloc([128, 512], F32)
        for tt in range(NT):
            tok = slice(512 * tt, 512 * (tt + 1))
            rmsnorm_tile(lambda dc: OT[:, dc, :], lambda dc: X[:, dc, tok], 8, 512,
                         lambda dc: lpc("gfin", dc, dc + 1), D, [SQ[0][:, :], SQ[1][:, :]], RS[:, :])
            S.dma(out_d.ap()[:, :, tok], OT[:, :, :])
    else:
        for q in range(4):
            S.dma(out_d.ap()[:, 2 * q:2 * q + 2, :], X[:, 2 * q:2 * q + 2, :])
    S.finish()
    S.build()
    return nc


def _consts(core):
    qi = core % 4
    c = np.zeros((128, NCS), np.float32)
    inv_freq = (500000.0 ** (-(np.arange(0, 16, 2, dtype=np.float32) / 16.0))).astype(np.float32)
    for p in range(128):
        d = p % 64
        if d < 16:
            c[p, CS["invf"][0]] = inv_freq[d % 8]
            c[p, CS["sgn"][0]] = -1.0 if d < 8 else 1.0
    c[:, CS["tau"][0]:CS["tau"][0] + 17] = np.arange(17, dtype=np.float32)[None]
    c[:, CS["bv"][0]:CS["bv"][0] + 129] = (R * np.arange(129, dtype=np.float32))[None]
    for r in range(3):
        c[:, CS["rankbias"][0] + r] = 0.0 if r < qi else -30000.0
    c[:, CS["ohq"][0] + qi] = 1.0
    if qi > 0:
        c[:, CS["ohprev"][0] + qi - 1] = 1.0
    return c


def _cbf():
    c = np.zeros((128, 256 + 2048), np.float32)
    c[:, 0:128] = np.eye(128, dtype=np.float32)
    for m in range(128):
        d = m % 64
        if d < 8:
            c[m + 8, 128 + m] = 1.0
        elif d < 16:
            c[m - 8, 128 + m] = 1.0
    kk = np.arange(128)[:, None]
    qq = np.arange(512)[None, :]
    for i in range(4):
        c[:, 256 + 512 * i:256 + 512 * (i + 1)] = ((128 * i + kk) // 64 <= qq // 64).astype(np.float32)
    return c.astype(ml_dtypes.bfloat16)


def _layer_params(inp, l):
    lp = np.zeros((128, NLP), np.float32)

    def put(name, arr):
        o, w = LP[name]
        lp[:, o:o + w] = arr

    put("gmix", inp["norm_mix"][l].reshape(8, 128).T)
    put("gffn", inp["norm_ffn"][l].reshape(8, 128).T)
    put("gssm", inp["ssm_norm"][l].reshape(4, 128).T)
    put("gsub", inp["attn_subln"][l].reshape(128, 1))
    put("dvec", inp["ssm_d"][l].reshape(4, 128).T)
    pl = lambda a: a.reshape(16, 2, 64).transpose(1, 2, 0).reshape(128, 16)
    put("lamre", pl(inp["ssm_lambda_re"][l]))
    put("lamim", pl(inp["ssm_lambda_im"][l]))
    put("logstep", pl(np.repeat(inp["ssm_log_step"][l][:, None], 64, axis=1)))
    put("wconv", inp["w_conv"][l].reshape(3, 44, 128).transpose(2, 0, 1).reshape(128, 132))
    put("bconv", inp["b_conv"][l].reshape(44, 128).T)
    lam = np.concatenate([inp["lambda_q1"][l], inp["lambda_k1"][l], inp["lambda_q2"][l], inp["lambda_k2"][l]])
    put("lamqk", np.repeat(lam[None, :], 128, axis=0))
    put("gfin", inp["norm_final"].reshape(8, 128).T)
    lam_init = 0.8 - 0.6 * math.exp(-0.3 * l)
    put("nlaminit", np.full((128, 1), -lam_init, np.float32))
    put("oml", np.full((128, 1), 1.0 - lam_init, np.float32))
    lpb = np.zeros((128, 4, 16, 2, 16), np.float32)
    b_re = inp["ssm_b_re"][l].reshape(16, 2, 64, 16)
    b_im = inp["ssm_b_im"][l].reshape(16, 2, 64, 16)
    c_re = inp["ssm_c_re"][l].reshape(16, 2, 16, 64)
    c_im = inp["ssm_c_im"][l].reshape(16, 2, 16, 64)
    for g2 in range(2):
        rows = slice(64 * g2, 64 * g2 + 64)
        lpb[rows, 0, :, g2, :] = b_re[:, g2].transpose(1, 0, 2)
        lpb[rows, 1, :, g2, :] = b_im[:, g2].transpose(1, 0, 2)
        lpb[rows, 2, :, g2, :] = c_re[:, g2].transpose(2, 0, 1)
        lpb[rows, 3, :, g2, :] = c_im[:, g2].transpose(2, 0, 1)
    return lp, lpb.reshape(128, 2048)


def _layer_weights(inp, l):
    w_in = inp["w_in"][l].reshape(8, 128, 8, 256).transpose(2, 1, 0, 3)
    w_glu = inp["ssm_w_glu"][l].reshape(4, 128, 2, 512).transpose(2, 1, 0, 3)
    w_out = inp["w_out"][l].reshape(8, 128, 4, 256).transpose(2, 1, 0, 3)
    w_up = inp["w_up"][l].reshape(8, 128, 2, NFC, 128).transpose(3, 1, 0, 2, 4).reshape(NFC, 128, 8, 256)
    w_dn = inp["w_down"][l].reshape(2, 11, 128, 8, 128).transpose(3, 0, 2, 1, 4)
    return {k: np.ascontiguousarray(v, dtype=np.float32) for k, v in
            (("w_in", w_in), ("w_glu", w_glu), ("w_out", w_out), ("w_up", w_up), ("w_dn", w_dn))}


_PROG = {}


def _get_prog(L, final):
    key = (L, final)
    if key not in _PROG:
        _PROG[key] = build_program(L, final)
    return _PROG[key]


FUSED = True


def kernel(**inp):
    inp = {k: np.asarray(v) for k, v in inp.items()}
    x = inp["x"].astype(np.float32)
    xs = []
    for c in range(8):
        b, qi = c // 4, c % 4
        xs.append(np.ascontiguousarray(x[b, qi * T:(qi + 1) * T, :].T.reshape(8, 128, T).transpose(1, 0, 2)))
    poss = [np.ascontiguousarray(inp["positions"][c // 4, (c % 4) * T:(c % 4 + 1) * T].astype(np.int32)[None]) for c in range(8)]
    csts = [_consts(c) for c in range(8)]
    cbf = _cbf()
    lps = [_layer_params(inp, l) for l in range(DEPTH)]
    wts = [_layer_weights(inp, l) for l in range(DEPTH)]

    def stack(ls):
        maps = {"lp": np.stack([lps[l][0] for l in ls]), "lpb": np.stack([lps[l][1] for l in ls])}
        for k in ("w_in", "w_glu", "w_out", "w_up", "w_dn"):
            maps[k] = np.stack([wts[l][k] for l in ls])
        return maps

    if FUSED:
        launches = [(list(range(DEPTH)), True)]
    else:
        launches = [([l], l == DEPTH - 1) for l in range(DEPTH)]
    for ls, final in launches:
        nc = _get_prog(len(ls), final)
        shared = stack(ls)
        in_maps = []
        for c in range(8):
            m = {"x_in": xs[c], "pos": poss[c], "cst": csts[c], "cbf": cbf}
            m.update(shared)
            in_maps.append(m)
        res = run_bass_kernel_spmd(nc, in_maps, core_ids=list(range(8)))
        xs = [np.asarray(r["out"], dtype=np.float32) for r in res.results]
    out = np.zeros((2, 8192, D), np.float32)
    for c in range(8):
        b, qi = c // 4, c % 4
        out[b, qi * T:(qi + 1) * T, :] = xs[c].transpose(1, 0, 2).reshape(D, T).T
    return out
```
